# Optimizing a Trainium2 kernel written in Bass

```python
import jax, jax.numpy as jnp
from jax import lax
import numpy as np

D_MODEL = 1024
BATCH = 4
SEQ = 4096
DEPTH = 1
DEC_BATCH = 128
DEC_SEQ = 4
PAST_LEN = 16384
PAGE_SIZE = 128

N_HEADS_A = 8
N_KV_A = 2
HD_A = 64
GROUP_A = N_HEADS_A // N_KV_A
WINDOW = 128
BLOCK = WINDOW
ROPE_THETA = 500000.0
ROPE_DIM = HD_A // 4
N_HEADS_R = 4
DK_R = 128
DV_R = 256
RET_CHUNK = 128
RET_THETA = 10000.0
D_FF = 3 * D_MODEL
CONV_W = 3
EPS = 1e-6
NEG_INF = -1e30

Q_A = N_HEADS_A * HD_A
KV_A = N_KV_A * HD_A
QK_R = N_HEADS_R * DK_R
V_R = N_HEADS_R * DV_R
D_IN = Q_A + 2 * KV_A + 2 * QK_R + 2 * V_R + 2 * D_MODEL

kernel_name = 'hybrid_swa_sink_retention_convffn_step'


def split_points():
    sizes = [Q_A, KV_A, KV_A, QK_R, QK_R, V_R, V_R, D_MODEL, D_MODEL]
    return np.cumsum(sizes)[:-1].tolist()


def rms_norm(x, g):
    xf = x.astype(jnp.float32)
    xf = xf * lax.rsqrt(jnp.mean(xf * xf, axis=-1, keepdims=True) + EPS)
    return (xf * g.astype(jnp.float32)).astype(x.dtype)


def head_rms(x):
    return x * lax.rsqrt(jnp.mean(x * x, axis=-1, keepdims=True) + EPS)


def rope_partial(x, pos):
    half = ROPE_DIM // 2
    inv = 1.0 / (ROPE_THETA ** (jnp.arange(half, dtype=jnp.float32) / half))
    ang = pos.astype(jnp.float32)[:, None] * inv[None, :]
    cos = jnp.cos(ang)[:, None, :]
    sin = jnp.sin(ang)[:, None, :]
    xr = x[..., :ROPE_DIM].astype(jnp.float32)
    x1, x2 = xr[..., :half], xr[..., half:]
    rot = jnp.concatenate([x1 * cos - x2 * sin, x2 * cos + x1 * sin], axis=-1)
    return jnp.concatenate([rot.astype(x.dtype), x[..., ROPE_DIM:]], axis=-1)


def rope_ret(x, pos):
    ang = 1.0 / (RET_THETA ** jnp.linspace(0.0, 1.0, DK_R // 2, dtype=jnp.float32))
    ang = jnp.repeat(ang, 2)
    th = pos.astype(jnp.float32)[:, None] * ang[None, :]
    cos = jnp.cos(th)[:, None, :]
    sin = jnp.sin(th)[:, None, :]
    x1 = x[..., ::2]
    x2 = x[..., 1::2]
    rot = jnp.stack([-x2, x1], axis=-1).reshape(x.shape)
    return x * cos + rot * sin


def window_attn(q, k, v, qpos, kpos, sinks):
    B, Tq = q.shape[0], q.shape[1]
    qg = q.reshape(B, Tq, N_KV_A, GROUP_A, HD_A)
    s = jnp.einsum('bqkgd,bskd->bkgqs', qg, k).astype(jnp.float32) * (HD_A ** -0.5)
    valid = ((kpos[None, :] <= qpos[:, None])
             & (kpos[None, :] > qpos[:, None] - WINDOW)
             & (kpos[None, :] >= 0))
    s = jnp.where(valid, s, NEG_INF)
    sink = jnp.broadcast_to(sinks.astype(jnp.float32).reshape(N_KV_A, GROUP_A, 1, 1),
                            s.shape[:-1] + (1,))
    p = jax.nn.softmax(jnp.concatenate([s, sink], axis=-1), axis=-1)[..., :-1]
    o = jnp.einsum('bkgqs,bskd->bqkgd', p.astype(v.dtype), v)
    return o.reshape(B, Tq, N_HEADS_A * HD_A)


def banded_window_attn(q, k, v, pos, sinks):
    B, T = q.shape[0], q.shape[1]
    nb = T // BLOCK
    qb = q.reshape(B, nb, BLOCK, N_HEADS_A, HD_A)
    kb = k.reshape(B, nb, BLOCK, N_KV_A, HD_A)
    vb = v.reshape(B, nb, BLOCK, N_KV_A, HD_A)
    pad = ((0, 0), (1, 0), (0, 0), (0, 0), (0, 0))
    kband = jnp.concatenate([jnp.pad(kb, pad)[:, :-1], kb], axis=2)
    vband = jnp.concatenate([jnp.pad(vb, pad)[:, :-1], vb], axis=2)
    qpos = pos.reshape(nb, BLOCK)
    kpos = jnp.concatenate([qpos - BLOCK, qpos], axis=-1)
    o = jax.vmap(window_attn, in_axes=(1, 1, 1, 0, 0, None), out_axes=1)(
        qb, kband, vband, qpos, kpos, sinks)
    return o.reshape(B, T, N_HEADS_A * HD_A)


def ret_log_decay():
    return jnp.log1p(-jnp.exp2(-5.0 - jnp.arange(N_HEADS_R, dtype=jnp.float32)))


def retention_chunk(q, k, v, s0, log_g):
    L = q.shape[1]
    idx = jnp.arange(L, dtype=jnp.float32)
    diff = idx[:, None] - idx[None, :]
    dmask = jnp.where(diff >= 0, jnp.exp(log_g[:, None, None] * jnp.maximum(diff, 0.0)), 0.0)
    scores = jnp.einsum('bihd,bjhd->bhij', q, k) * dmask[None]
    inner = jnp.einsum('bhij,bjhe->bihe', scores, v)
    q_decay = jnp.exp(log_g[None, :] * (idx[:, None] + 1.0))
    cross = jnp.einsum('bihd,bhde->bihe', q, s0) * q_decay[None, :, :, None]
    k_decay = jnp.exp(log_g[None, :] * (L - 1.0 - idx[:, None]))
    s_new = (jnp.exp(log_g * L)[None, :, None, None] * s0
             + jnp.einsum('bjhd,bjhe->bhde', k * k_decay[None, :, :, None], v))
    return inner + cross, s_new


def retention(q, k, v, s0):
    B, T = q.shape[0], q.shape[1]
    c = RET_CHUNK if T % RET_CHUNK == 0 else T
    nc = T // c
    log_g = ret_log_decay()

    def to_chunks(a):
        return a.reshape((B, nc, c) + a.shape[2:]).swapaxes(0, 1)

    def step(s, qkv):
        o, s = retention_chunk(qkv[0], qkv[1], qkv[2], s, log_g)
        return s, o

    s_final, o = lax.scan(step, s0, (to_chunks(q), to_chunks(k), to_chunks(v)))
    o = o.swapaxes(0, 1).reshape(B, T, N_HEADS_R, DV_R)
    return o, s_final


def conv_ffn(xn, conv_ctx, w_up, conv_w, conv_b, w_down):
    T = xn.shape[1]
    u = xn @ w_up
    u_ext = jnp.concatenate([conv_ctx.astype(u.dtype), u], axis=1)
    c = conv_b
    for tap in range(CONV_W):
        c = c + u_ext[:, tap:tap + T] * conv_w[tap]
    a, b = jnp.split(c, 2, axis=-1)
    h = jax.nn.gelu(a, approximate=True) * b
    return h @ w_down, u_ext[:, -(CONV_W - 1):]


def trunk_layer(x, pos, ctx_k, ctx_v, ctx_pos, ret_s0, conv_ctx,
                w_in, sinks, w_a_proj, w_r_proj, w_o,
                g_pre_mix, g_post_mix, g_pre_ffn, g_post_ffn,
                w_up, conv_w, conv_b, w_down):
    B, T = x.shape[0], x.shape[1]
    xn = rms_norm(x, g_pre_mix)
    h = xn @ w_in
    q_a, k_a, v_a, q_r, k_r, v_r, gate_r, gm_a, gm_r = jnp.split(h, split_points(), axis=-1)
    q_a = rope_partial(q_a.reshape(B, T, N_HEADS_A, HD_A), pos)
    k_a = rope_partial(k_a.reshape(B, T, N_KV_A, HD_A), pos)
    v_a = v_a.reshape(B, T, N_KV_A, HD_A)
    if ctx_k is None:
        o_a = banded_window_attn(q_a, k_a, v_a, pos, sinks)
        new_k, new_v = k_a[:, -WINDOW:], v_a[:, -WINDOW:]
    else:
        k_all = jnp.concatenate([ctx_k.astype(k_a.dtype), k_a], axis=1)
        v_all = jnp.concatenate([ctx_v.astype(v_a.dtype), v_a], axis=1)
        kpos = jnp.concatenate([ctx_pos, pos])
        o_a = window_attn(q_a, k_all, v_all, pos, kpos, sinks)
        new_k, new_v = k_all[:, -WINDOW:], v_all[:, -WINDOW:]
    qr = rope_ret(q_r.reshape(B, T, N_HEADS_R, DK_R).astype(jnp.float32), pos)
    kr = rope_ret(k_r.reshape(B, T, N_HEADS_R, DK_R).astype(jnp.float32), pos) * (DK_R ** -0.5)
    vr = v_r.reshape(B, T, N_HEADS_R, DV_R).astype(jnp.float32)
    o_r, new_s = retention(qr, kr, vr, ret_s0.astype(jnp.float32))
    o_r = head_rms(o_r).reshape(B, T, V_R).astype(x.dtype)
    y_a = o_a @ w_a_proj
    y_r = (jax.nn.silu(gate_r) * o_r) @ w_r_proj
    mixed = (jax.nn.sigmoid(gm_a) * y_a + jax.nn.sigmoid(gm_r) * y_r) @ w_o
    x = x + rms_norm(mixed, g_post_mix)
    f, new_conv = conv_ffn(rms_norm(x, g_pre_ffn), conv_ctx, w_up, conv_w, conv_b, w_down)
    x = x + rms_norm(f, g_post_ffn)
    return x, new_k, new_v, new_s, new_conv


def setup_inputs(seed: int = 0) -> dict:
    key = jax.random.key(seed)
    ks = jax.random.split(key, 20)
    F2 = 2 * D_FF

    def nrm(k, shape, scale):
        return jax.random.normal(k, shape, jnp.float32) * scale

    return {
        'x_prompt': nrm(ks[0], (BATCH, SEQ, D_MODEL), 1.0),
        'x_sample': nrm(ks[1], (DEC_BATCH, DEC_SEQ, D_MODEL), 1.0),
        'cache_k': nrm(ks[2], (DEPTH, DEC_BATCH, WINDOW, N_KV_A, HD_A), 1.0),
        'cache_v': nrm(ks[3], (DEPTH, DEC_BATCH, WINDOW, N_KV_A, HD_A), 1.0),
        'state_ret': nrm(ks[4], (DEPTH, DEC_BATCH, N_HEADS_R, DK_R, DV_R), 0.3),
        'state_conv': nrm(ks[5], (DEPTH, DEC_BATCH, CONV_W - 1, F2), 1.0),
        'w_in': nrm(ks[6], (DEPTH, D_MODEL, D_IN), D_MODEL ** -0.5),
        'attn_sinks': nrm(ks[7], (DEPTH, N_HEADS_A), 0.5),
        'w_a_proj': nrm(ks[8], (DEPTH, Q_A, D_MODEL), Q_A ** -0.5),
        'w_r_proj': nrm(ks[9], (DEPTH, V_R, D_MODEL), V_R ** -0.5),
        'w_o': nrm(ks[10], (DEPTH, D_MODEL, D_MODEL), D_MODEL ** -0.5),
        'g_pre_mix': 1.0 + nrm(ks[11], (DEPTH, D_MODEL), 0.05),
        'g_post_mix': 1.0 + nrm(ks[12], (DEPTH, D_MODEL), 0.05),
        'g_pre_ffn': 1.0 + nrm(ks[13], (DEPTH, D_MODEL), 0.05),
        'g_post_ffn': 1.0 + nrm(ks[14], (DEPTH, D_MODEL), 0.05),
        'w_up': nrm(ks[15], (DEPTH, D_MODEL, F2), D_MODEL ** -0.5),
        'conv_w': nrm(ks[16], (DEPTH, CONV_W, F2), CONV_W ** -0.5),
        'conv_b': nrm(ks[17], (DEPTH, F2), 0.01),
        'w_down': nrm(ks[18], (DEPTH, D_FF, D_MODEL), D_FF ** -0.5),
    }


def reference(x_prompt, x_sample, cache_k, cache_v, state_ret, state_conv,
              w_in, attn_sinks, w_a_proj, w_r_proj, w_o,
              g_pre_mix, g_post_mix, g_pre_ffn, g_post_ffn,
              w_up, conv_w, conv_b, w_down):
    bp, tp = x_prompt.shape[0], x_prompt.shape[1]
    ts = x_sample.shape[1]
    pos_p = jnp.arange(tp, dtype=jnp.int32)
    pos_s = PAST_LEN + jnp.arange(ts, dtype=jnp.int32)
    ctx_pos = PAST_LEN - WINDOW + jnp.arange(WINDOW, dtype=jnp.int32)
    hp, hs = x_prompt, x_sample
    kp_l, vp_l, sp_l, cp_l = [], [], [], []
    ks_l, vs_l, ss_l, cs_l = [], [], [], []
    for l in range(DEPTH):
        w = (w_in[l], attn_sinks[l], w_a_proj[l], w_r_proj[l], w_o[l],
             g_pre_mix[l], g_post_mix[l], g_pre_ffn[l], g_post_ffn[l],
             w_up[l], conv_w[l], conv_b[l], w_down[l])
        ret0 = jnp.zeros((bp, N_HEADS_R, DK_R, DV_R), jnp.float32)
        conv0 = jnp.zeros((bp, CONV_W - 1, 2 * D_FF), x_prompt.dtype)
        hp, kp, vp, sp, cp = trunk_layer(hp, pos_p, None, None, None, ret0, conv0, *w)
        hs, kss, vss, sss, css = trunk_layer(hs, pos_s, cache_k[l], cache_v[l], ctx_pos,
                                             state_ret[l], state_conv[l], *w)
        kp_l.append(kp); vp_l.append(vp); sp_l.append(sp); cp_l.append(cp)
        ks_l.append(kss); vs_l.append(vss); ss_l.append(sss); cs_l.append(css)
    k_prompt = jnp.stack(kp_l)
    v_prompt = jnp.stack(vp_l)
    ret_prompt = jnp.stack(sp_l)
    conv_prompt = jnp.stack(cp_l)
    k_sample = jnp.stack(ks_l)
    v_sample = jnp.stack(vs_l)
    ret_sample = jnp.stack(ss_l)
    conv_sample = jnp.stack(cs_l)
    return (hp, hs, k_prompt, v_prompt, ret_prompt, conv_prompt,
            k_sample, v_sample, ret_sample, conv_sample)
```

```python
import numpy as np
from contextlib import ExitStack
import ml_dtypes
import concourse.bass as bass
import concourse.mybir as mybir
from concourse.bass_utils import run_bass_kernel_spmd

F32 = mybir.dt.float32
BF16 = mybir.dt.bfloat16
U8 = mybir.dt.uint8
AF = mybir.ActivationFunctionType
ALU = mybir.AluOpType

D = 1024
DIN = 5888
F2 = 6144
DFF = 3072
EPS = 1e-6
NPRE = 15
NMAIN = 17
NTAB = 544
DEBUG_X1 = False
STAGE = 99
CUT = 0
SCUT = 0
GAM = [1.0 - 2.0 ** (-5 - h) for h in range(4)]


class Sched:
    def __init__(self, nc, stack):
        self.nc = nc
        self.stack = stack
        self.eng = {"pe": nc.tensor, "act": nc.scalar, "dve": nc.vector,
                    "pool": nc.gpsimd, "sp": nc.sync}
        self.queues = {e: [] for e in self.eng}
        self.sems = {}
        self.cnt = {}
        self.waited = {e: {} for e in self.eng}
        self.last_w = {}
        self.readers = {}

    def sem(self, name):
        if name not in self.sems:
            self.sems[name] = self.stack.enter_context(
                self.nc.semaphore("s_" + name.replace(":", "_")))
            self.cnt[name] = 0
        return self.sems[name]

    def _deps(self, reads, writes):
        deps = set()
        for k in reads:
            t = self.last_w.get(k)
            if t is not None:
                deps.add(t)
        for k in writes:
            t = self.last_w.get(k)
            if t is not None:
                deps.add(t)
            for t in self.readers.get(k, ()):
                deps.add(t)
        return deps

    def _commit(self, token, reads, writes):
        for k in reads:
            self.readers.setdefault(k, set()).add(token)
        for k in writes:
            self.last_w[k] = token
            self.readers[k] = set()

    def _waits(self, e, deps):
        best = {}
        for (s, v) in deps:
            if e == "pe" and s == "pe":
                continue
            if best.get(s, 0) < v:
                best[s] = v
        w = []
        for s, v in best.items():
            if self.waited[e].get(s, 0) < v:
                self.waited[e][s] = v
                w.append((s, v))
        return w

    def op(self, e, fn, reads=(), writes=(), inc=True):
        self.sem(e)
        deps = self._deps(reads, writes)
        waits = self._waits(e, deps)
        if inc:
            self.cnt[e] += 1
            token = (e, self.cnt[e])
        else:
            token = (e, self.cnt[e] + 1)
        self.queues[e].append((waits, fn, (e, 1) if inc else None))
        self._commit(token, reads, writes)
        return token

    def dma(self, q, out, in_, semkey, reads=(), writes=(), **kw):
        s = "d:" + semkey
        self.sem(s)
        deps = self._deps(reads, writes)
        waits = self._waits(q, deps)
        self.cnt[s] += 16
        token = (s, self.cnt[s])
        self.queues[q].append((waits, lambda eng: eng.dma_start(out=out, in_=in_, **kw), (s, 16)))
        self._commit(token, reads, writes)
        return token

    def dma_multi(self, q, items, semkey, **kw):
        s = "d:" + semkey
        self.sem(s)
        keys = []
        for (out, in_, key) in items:
            deps = self._deps((), [key])
            waits = self._waits(q, deps)
            self.cnt[s] += 16
            self.queues[q].append((waits, (lambda o, i: (lambda eng: eng.dma_start(out=o, in_=i, **kw)))(out, in_), (s, 16)))
            keys.append(key)
        token = (s, self.cnt[s])
        for key in keys:
            self.last_w[key] = token
            self.readers[key] = set()
        return token

    def barrier(self):
        for e in ("pe", "act", "dve", "pool"):
            self.sem(e)
        tot = dict(self.cnt)
        for e in self.eng:
            waits = []
            for s, v in tot.items():
                if v > 0 and self.waited[e].get(s, 0) < v and s != e:
                    self.waited[e][s] = v
                    waits.append((s, v))
            self.queues[e].append((waits, None, None))
        self.last_w = {}
        self.readers = {}

    def finish(self):
        nc = self.nc
        final = []
        for s, v in self.cnt.items():
            if v > 0 and self.waited["sp"].get(s, 0) < v:
                final.append((s, v))
        with nc.Block() as block:
            def emit(e):
                def body(eng):
                    for waits, fn, inc in self.queues[e]:
                        for (s, v) in waits:
                            eng.wait_ge(self.sems[s], v)
                        if fn is None:
                            continue
                        ins = fn(eng)
                        if inc is not None:
                            ins.then_inc(self.sems[inc[0]], inc[1])
                    if e == "sp":
                        for (s, v) in final:
                            eng.wait_ge(self.sems[s], v)
                return body
            block.tensor(emit("pe"))
            block.scalar(emit("act"))
            block.vector(emit("dve"))
            block.gpsimd(emit("pool"))
            block.sync(emit("sp"))


class Arena:
    def __init__(self, ap_u8, size):
        self.a = ap_u8
        self.size = size
        self.off = 0

    def alloc(self, parts, free, dt):
        esz = 4 if dt == F32 else 2
        n = 1
        for f in free:
            n *= f
        nb = n * esz
        nb_al = (nb + 63) // 64 * 64
        assert self.off + nb_al <= self.size, f"SBUF arena overflow {self.off}+{nb_al}>{self.size}"
        v = self.a[0:parts, self.off:self.off + nb].bitcast(dt)
        self.off += nb_al
        if len(free) == 2:
            v = v.rearrange("p (a b) -> p a b", a=free[0])
        elif len(free) == 3:
            v = v.rearrange("p (a b c) -> p a b c", a=free[0], b=free[1])
        return v


def build_program():
    nc = bass.Bass("TRN2", target_bir_lowering=False)

    def din(name, shape, dt=F32):
        return nc.dram_tensor(name, list(shape), dt, kind="ExternalInput").ap()

    def dout(name, shape):
        return nc.dram_tensor(name, list(shape), F32, kind="ExternalOutput").ap()

    xe = din("xe", [32 * 128, D])
    xsm = din("xsm", [64, D])
    ck = din("ck", [16, 128, 128])
    cv = din("cv", [16, 128, 128])
    sr = din("sr", [16, 4, 128, 256])
    scv = din("scv", [32, F2])
    w_in = din("w_in", [D, DIN])
    w_a = din("w_a", [128, 4, D])
    w_r = din("w_r", [D, D])
    w_o = din("w_o", [D, D])
    w_up = din("w_up", [D, F2])
    w_dn = din("w_dn", [DFF, D])
    gvT = din("gvT", [128, 16])
    gpost = din("gpost", [2, D])
    cwT = din("cwT", [128, 48 * 4])
    sinks = din("sinks", [128, 4])
    tabs = din("tabs", [33 * 128, NTAB])
    cst = din("cst", [128, 1280])
    cbf = din("cbf", [128, 1024], BF16)
    cs2 = din("cs2", [128, 544])

    y_o = dout("y_o", [16 * 128, D])
    ys_o = dout("ys_o", [64, D])
    kp_o = dout("kp_o", [128, 128])
    vp_o = dout("vp_o", [128, 128])
    rp_o = dout("rp_o", [4, 128, 256])
    cp_o = dout("cp_o", [2, F2])
    ks_o = dout("ks_o", [16, 128, 128])
    vs_o = dout("vs_o", [16, 128, 128])
    rs_o = dout("rs_o", [16, 4, 128, 256])
    cs_o = dout("cs_o", [32, F2])

    x1s = nc.dram_tensor("x1s", [NMAIN * 128 + 64, D], F32, kind="Internal").ap()
    wupbf = nc.dram_tensor("wupbf", [D, F2], BF16, kind="Internal").ap()
    wdnbf = nc.dram_tensor("wdnbf", [DFF, D], BF16, kind="Internal").ap()

    with ExitStack() as st:
        S = Sched(nc, st)
        ARENA = 212800
        arena_t = st.enter_context(nc.sbuf_tensor("arena", [128, ARENA], U8))
        A = Arena(arena_t, ARENA)
        psb = [st.enter_context(nc.psum_tensor(f"ps{i}", [128, 512], F32)) for i in range(8)]
        psctr = [0]

        pinned = set()

        def ps(pin=False):
            while True:
                i = psctr[0] % 8
                psctr[0] += 1
                if i not in pinned:
                    break
            if pin:
                pinned.add(i)
            return psb[i], f"ps{i}"

        def unpin(bk):
            pinned.discard(int(bk[2:]))

        def act(out, in_, func, reads, writes, scale=1.0, accum=None):
            if accum is None:
                S.op("act", lambda e: e.activation(out=out, in_=in_, func=func, scale=scale), reads, writes)
            else:
                S.op("act", lambda e: e.activation(out=out, in_=in_, func=func, scale=scale, accum_out=accum), reads, writes)

        def tt(eng, out, in0, in1, op, reads, writes):
            S.op(eng, lambda e: e.tensor_tensor(out=out, in0=in0, in1=in1, op=op), reads, writes)

        def ts(eng, out, in0, s1, s2, op0, op1, reads, writes):
            S.op(eng, lambda e: e.tensor_scalar(out=out, in0=in0, scalar1=s1, scalar2=s2, op0=op0, op1=op1), reads, writes)

        def stt(out, in0, scalar, in1, op0, op1, reads, writes):
            S.op("dve", lambda e: e.scalar_tensor_tensor(out=out, in0=in0, scalar=scalar, in1=in1, op0=op0, op1=op1), reads, writes)

        def cp(eng, out, in_, reads, writes):
            if eng == "act":
                S.op("act", lambda e: e.copy(out=out, in_=in_), reads, writes)
            else:
                S.op(eng, lambda e: e.tensor_copy(out=out, in_=in_), reads, writes)

        def mm(out, lhsT, rhs, start, stop, reads, writes, inc=None):
            S.op("pe", lambda e: e.matmul(out, lhsT, rhs, start=start, stop=stop), reads, writes,
                 inc=(stop if inc is None else inc))

        def tp(out, in_, ident, reads, writes, inc):
            S.op("pe", lambda e: e.transpose(out, in_, ident), reads, writes, inc=inc)

        def bc(ap, shape, axis):
            return ap.unsqueeze(axis).to_broadcast(shape)

        CST = A.alloc(128, [1280], F32)
        CBF = A.alloc(128, [1024], BF16)
        GVT = A.alloc(128, [16], F32)
        GPM = A.alloc(128, [D], F32)
        SNK = A.alloc(128, [4], F32)
        ESK = A.alloc(128, [4], F32)
        CN05 = A.alloc(128, [4], F32)
        S.dma("sp", CST, cst, "cst", writes=["cst"])
        S.dma("sp", CBF, cbf, "cbf", writes=["cbf"])
        S.dma("sp", GVT, gvT, "gvt", writes=["gvt"])
        S.dma("sp", GPM, gpost[0:1, :].to_broadcast([128, D]) if False else gpost[0].partition_broadcast(128), "gpm", writes=["gpm"])
        S.dma("sp", SNK, sinks, "snk", writes=["snk"])
        act(ESK, SNK, AF.Exp, ["snk"], ["esk"])
        S.op("pool", lambda e: e.memset(CN05, -0.5), (), ["cn05"])
        dqT = CST[:, 0:512]
        dkT = CST[:, 512:1024]
        ktok = CST[:, 1024:1028]
        idf = CST[:, 1152:1280]
        ident = CBF[:, 0:128]
        ones64 = CBF[:, 128:192]
        mown = CBF[:, 256:384]
        mprev = CBF[:, 384:512]
        mprev1 = CBF[:, 512:640]

        mark_w = A.off
        Win = A.alloc(128, [8, DIN], BF16)
        X2buf = A.alloc(128, [D], F32)
        Wa = A.alloc(128, [4, D], BF16)
        Wr = A.alloc(128, [8, D], BF16)
        Wo = A.alloc(128, [8, D], BF16)
        itemsA0, itemsA, itemsB = [], [], []
        for k in range(8):
            itemsA0.append((Win[:, k, 1280:1792], w_in[k * 128:(k + 1) * 128, 1280:1792], "winA0"))
        for k in range(8):
            for (c0, c1) in ((1792, 2816), (512, 768)):
                itemsA.append((Win[:, k, c0:c1], w_in[k * 128:(k + 1) * 128, c0:c1], "winA"))
        S.dma_multi("pool", itemsA0, "winA0")
        for k in range(8):
            for (c0, c1) in ((0, 512), (768, 1280), (2816, 3840), (3840, 4864), (4864, 5888)):
                itemsB.append((Win[:, k, c0:c1], w_in[k * 128:(k + 1) * 128, c0:c1], "winB"))
        S.dma_multi("pool", itemsA, "winA")
        pend_w = [("winB", it) for it in itemsB]
        pend_w += [("wa", (Wa[:, j, :], w_a[:, j, :], "wa")) for j in range(4)]
        pend_w += [("wr", (Wr[:, k, :], w_r[k * 128:(k + 1) * 128, :], "wr")) for k in range(8)]
        pend_w += [("wo", (Wo[:, k, :], w_o[k * 128:(k + 1) * 128, :], "wo")) for k in range(8)]

        def issue_weights(n):
            for _ in range(n):
                if not pend_w:
                    return
                sk, it = pend_w.pop(0)
                S.dma_multi("pool", [it], sk)
        mark_mix = A.off
        win_b_pending = [True]

        X = [A.alloc(128, [D], F32) for _ in range(2)] + [X2buf]
        TAB = [A.alloc(128, [NTAB], F32) for _ in range(2)]
        xs = A.alloc(128, [D], BF16)
        xnT = A.alloc(128, [8, 128], BF16)
        qa = A.alloc(128, [512], BF16)
        qaT = A.alloc(128, [512], BF16)
        ka32 = A.alloc(128, [128], F32)
        kabf = A.alloc(128, [128], BF16)
        kT = [A.alloc(128, [128], BF16) for _ in range(2)]
        va32 = A.alloc(128, [128], F32)
        vbf = [A.alloc(128, [128], BF16) for _ in range(2)]
        t1 = A.alloc(128, [512], F32)
        t2 = A.alloc(128, [512], F32)
        qr = A.alloc(128, [512], BF16)
        kr = A.alloc(128, [512], BF16)
        ktl = A.alloc(128, [512], BF16)
        qrT = A.alloc(128, [512], BF16)
        krT = A.alloc(128, [512], BF16)
        vr = A.alloc(128, [1024], BF16)
        sg = A.alloc(128, [1024], BF16)
        PT = [A.alloc(128, [512], BF16) for _ in range(4)]
        oaT = A.alloc(128, [512], BF16)
        AT = A.alloc(128, [512], BF16)
        S32 = A.alloc(128, [1024], F32)
        Sbf = A.alloc(128, [1024], BF16)
        og = A.alloc(128, [1024], BF16)
        ogT = A.alloc(128, [8, 128], BF16)
        tha = A.alloc(128, [1024], BF16)
        thr = A.alloc(128, [1024], BF16)
        st8 = A.alloc(128, [32], F32)
        tA1 = A.alloc(128, [8, 16], F32)
        tA2 = A.alloc(128, [8, 16], F32)

        def load_x(slot, src_rows, tab_rows, np_=128):
            S.dma("sp", X[slot][:np_], src_rows, f"x{slot}", writes=[f"x{slot}"])
            S.dma("sp", TAB[slot][:np_], tab_rows, f"tab{slot}", writes=[f"tab{slot}"])

        def norm_part(slot, np_, a=True, b=True):
            xb = X[slot]
            xk = f"x{slot}"
            if a:
                act(xs[:np_], xb[:np_], AF.Square, [xk], ["xs", "ssq"], accum=st8[:np_, 0:1])
                ts("dve", st8[:np_, 1:2], st8[:np_, 0:1], 1.0 / D, EPS, ALU.mult, ALU.add, ["ssq"], ["ms"])
                tt("pool", st8[:np_, 2:3], st8[:np_, 1:2], CN05[:np_, 0:1], ALU.pow, ["ms", "cn05"], ["rstd"])
            if b:
                act(xs[:np_], xb[:np_], AF.Copy, [xk, "rstd"], ["xs"], scale=st8[:np_, 2:3])

        def norm_T(slot, np_, gcol0, wkeys=("gvt",), do_part=True):
            if do_part:
                norm_part(slot, np_)
            bank, bk = ps()
            pb = bank[:].bitcast(BF16)
            for k in range(8):
                tp(pb[:, k * 128:k * 128 + np_], xs[:np_, k * 128:(k + 1) * 128], ident[:np_, :np_],
                   ["xs", "cbf"], [bk], inc=(k == 7))
            pv = pb.rearrange("p (k t) -> p k t", k=8)[:, :, :np_]
            tt("dve", xnT[:, :, :np_], pv, bc(GVT[:, gcol0:gcol0 + 8], [128, 8, np_], 2), ALU.mult,
               [bk, "gvt"], ["xnT"])

        def inproj(c0, n, np_):
            bank, bk = ps()
            for k in range(8):
                mm(bank[:np_, :n], xnT[:, k, :np_], Win[:, k, c0:c0 + n], k == 0, k == 7,
                   ["xnT", "winA0" if 1280 <= c0 < 1792 else ("winA" if (512 <= c0 < 768 or 1792 <= c0 < 2816) else "winB")], [bk])
            return bank, bk

        def rope_small(psv, nh, np_, tab, outv, bk, okey):
            cosA = tab[:np_, 512:528]
            ssinA = tab[:np_, 528:544]
            a1 = tA1[:np_, :nh, :]
            a2 = tA2[:np_, :nh, :]
            tk = [bk, "tabcur"]
            tt("dve", a1, psv[:, :, 0:16], bc(cosA, [np_, nh, 16], 1), ALU.mult, tk, ["tA1"])
            tt("dve", a2[:, :, 0:8], psv[:, :, 8:16], bc(ssinA[:, 0:8], [np_, nh, 8], 1), ALU.mult, tk, ["tA2"])
            tt("dve", a2[:, :, 8:16], psv[:, :, 0:8], bc(ssinA[:, 8:16], [np_, nh, 8], 1), ALU.mult, tk, ["tA2b"])
            tt("dve", outv, a1, a2, ALU.add, ["tA1", "tA2", "tA2b"], [okey])

        def rope_ret(bank, bk, np_, tab, tcol, outbf, okey):
            cos = tab[:np_, tcol:tcol + 128]
            ssin = tab[:np_, tcol + 128:tcol + 256].rearrange("p (i two) -> p i two", two=2)
            pv = bank[:np_, 0:512].rearrange("p (h d) -> p h d", h=4)
            pp = bank[:np_, 0:512].rearrange("p (h i two) -> p h i two", h=4, two=2)
            t1v = t1[:np_].rearrange("p (h d) -> p h d", h=4)
            t2p = t2[:np_].rearrange("p (h i two) -> p h i two", h=4, two=2)
            tk = [bk, "tabcur"]
            tt("dve", t1v, pv, bc(cos, [np_, 4, 128], 1), ALU.mult, tk, ["t1"])
            tt("dve", t2p[:, :, :, 0], pp[:, :, :, 1], bc(ssin[:, :, 0], [np_, 4, 64], 1), ALU.mult, tk, ["t2a"])
            tt("dve", t2p[:, :, :, 1], pp[:, :, :, 0], bc(ssin[:, :, 1], [np_, 4, 64], 1), ALU.mult, tk, ["t2b"])
            tt("dve", outbf[:np_], t1[:np_], t2[:np_], ALU.add, ["t1", "t2a", "t2b"], [okey])

        def kv_tiles(slot, tabslot, np_, ka_out=True):
            tab = TAB[tabslot]
            S.readers.setdefault("tabcur", set())
            bank, bk = inproj(512, 256, np_)
            cp("act", ka32[:np_], bank[:np_, 0:128], [bk], ["ka32"])
            rope_small(bank[:np_, 0:128].rearrange("p (h d) -> p h d", h=2), 2, np_, tab,
                       ka32[:np_].rearrange("p (h d) -> p h d", h=2)[:, :, 0:16], bk, "ka32")
            cp("act", va32[:np_], bank[:np_, 128:256], [bk], ["va32"])
            cp("dve", kabf[:np_], ka32[:np_], ["ka32"], ["kabf"])
            cp("act", vbf[slot][:np_], va32[:np_], ["va32"], [f"vbf{slot}"])

        def state_update(np_=128):
            sbanks = [ps(), ps()]
            for h in range(4):
                sb, sk = sbanks[h // 2]
                c = (h % 2) * 256
                mm(sb[:, c:c + 256], ktl[:np_, h * 128:(h + 1) * 128], vr[:np_, h * 256:(h + 1) * 256], True, True,
                   ["ktl", "vr"], [sk])
            for h in range(4):
                sb, sk = sbanks[h // 2]
                c = (h % 2) * 256
                stt(S32[:, h * 256:(h + 1) * 256], S32[:, h * 256:(h + 1) * 256], float(GAM[h] ** 128),
                    sb[:, c:c + 256], ALU.mult, ALU.add, [sk, "S32"], ["S32"])

        def kr_vr_tiles(tabslot, np_=128):
            tab = TAB[tabslot]
            bank, bk = inproj(1280, 512, np_)
            rope_ret(bank, bk, np_, tab, 256, kr, "kr")
            tt("pool", ktl[:np_].rearrange("p (h d) -> p h d", h=4), kr[:np_].rearrange("p (h d) -> p h d", h=4),
               bc(ktok[:np_], [np_, 4, 128], 2), ALU.mult, ["kr", "cst"], ["ktl"])
            for n in range(2):
                bank, bk = inproj(1792 + n * 512, 512, np_)
                cp("act", vr[:np_, n * 512:(n + 1) * 512], bank[:np_, 0:512], [bk], ["vr"])


        S.op("dve", lambda e: e.memset(S32, 0.0), (), ["S32"])

        def load_blk(slot, gb):
            S.dma("sp", X[slot], xe[gb * 128:(gb + 1) * 128, :], f"x{slot}", writes=[f"x{slot}"])
            S.dma("sp", TAB[slot], tabs[gb * 128:(gb + 1) * 128, :], f"tab{slot}", writes=[f"tab{slot}"])

        cur = {"tab": None}

        def pxi(p_):
            return (p_ + 1) % 3

        def pload_x(p_):
            S.dma("sp", X[pxi(p_)], xe[p_ * 128:(p_ + 1) * 128, :], f"x{pxi(p_)}", writes=[f"x{pxi(p_)}"])

        def pload_tab(p_):
            S.dma("sp", TAB[p_ % 2], tabs[p_ * 128:(p_ + 1) * 128, :], f"tab{p_ % 2}", writes=[f"tab{p_ % 2}"])

        pload_x(0)
        pload_tab(0)
        pload_x(1)
        norm_part(pxi(0), 128)
        norm_T(pxi(0), 128, 0, do_part=False)
        for p in range(NPRE):
            slot = p % 2
            pload_tab(p + 1)
            if p + 2 <= NPRE:
                pload_x(p + 2)
            TK = f"tab{slot}"
            if p + 1 < NPRE:
                norm_part(pxi(p + 1), 128)
            tab = TAB[slot]
            bank, bk = inproj(1280, 512, 128)
            cos = tab[:, 256:384]
            rope_ret_keys = [bk, TK]
            ssin = tab[:, 384:512].rearrange("p (i two) -> p i two", two=2)
            pv = bank[:, 0:512].rearrange("p (h d) -> p h d", h=4)
            pp = bank[:, 0:512].rearrange("p (h i two) -> p h i two", h=4, two=2)
            t1v = t1.rearrange("p (h d) -> p h d", h=4)
            t2p = t2.rearrange("p (h i two) -> p h i two", h=4, two=2)
            tt("dve", t1v, pv, bc(cos, [128, 4, 128], 1), ALU.mult, rope_ret_keys, ["t1"])
            tt("dve", t2p[:, :, :, 0], pp[:, :, :, 1], bc(ssin[:, :, 0], [128, 4, 64], 1), ALU.mult, rope_ret_keys, ["t2a"])
            tt("dve", t2p[:, :, :, 1], pp[:, :, :, 0], bc(ssin[:, :, 1], [128, 4, 64], 1), ALU.mult, rope_ret_keys, ["t2b"])
            tt("dve", kr, t1, t2, ALU.add, ["t1", "t2a", "t2b"], ["kr"])
            tt("pool", ktl.rearrange("p (h d) -> p h d", h=4), kr.rearrange("p (h d) -> p h d", h=4),
               bc(ktok, [128, 4, 128], 2), ALU.mult, ["kr", "cst"], ["ktl"])
            for n in range(2):
                bank, bk = inproj(1792 + n * 512, 512, 128)
                cp("act", vr[:, n * 512:(n + 1) * 512], bank[:, 0:512], [bk], ["vr"])
            if p + 1 < NPRE:
                norm_T(pxi(p + 1), 128, 0, do_part=False)
            state_update()
            issue_weights(8)
            if p == NPRE - 1:
                bank, bk = inproj(512, 256, 128)
                cp("act", ka32, bank[:, 0:128], [bk], ["ka32"])
                psv = bank[:, 0:128].rearrange("p (h d) -> p h d", h=2)
                cosA = tab[:, 512:528]
                ssinA = tab[:, 528:544]
                tk = [bk, TK]
                tt("dve", tA1[:, :2, :], psv[:, :, 0:16], bc(cosA, [128, 2, 16], 1), ALU.mult, tk, ["tA1"])
                tt("dve", tA2[:, :2, 0:8], psv[:, :, 8:16], bc(ssinA[:, 0:8], [128, 2, 8], 1), ALU.mult, tk, ["tA2"])
                tt("dve", tA2[:, :2, 8:16], psv[:, :, 0:8], bc(ssinA[:, 8:16], [128, 2, 8], 1), ALU.mult, tk, ["tA2b"])
                tt("dve", ka32.rearrange("p (h d) -> p h d", h=2)[:, :, 0:16], tA1[:, :2, :], tA2[:, :2, :], ALU.add,
                   ["tA1", "tA2", "tA2b", "ka32"], ["ka32"])
                cp("dve", kabf, ka32, ["ka32"], ["kabf"])
                cp("act", vbf[1], bank[:, 128:256], [bk], ["vbf1"])
                bank2, bk2 = ps()
                pb2 = bank2[:].bitcast(BF16)
                tp(pb2[:, 0:128], kabf, ident, ["kabf", "cbf"], [bk2], inc=True)
                cp("act", kT[1], pb2[:, 0:128], [bk2], ["kT1"])
        cp("act", Sbf, S32, ["S32"], ["Sbf"])
        issue_weights(1000)

        def blk_slot(m):
            return (NPRE + m) % 2

        def tile_qa(m):
            slot = blk_slot(m)
            tab = TAB[slot]
            TK = f"tab{slot}"
            bank, bk = inproj(0, 512, 128)
            cp("act", t1, bank[:, 0:512], [bk], ["t1"])
            for g in range(2):
                cp("pool", qa.rearrange("p (j g d) -> p j g d", j=4, g=2)[:, :, g, :],
                   t1[:, g * 256:(g + 1) * 256].rearrange("p (j d) -> p j d", j=4), ["t1"], ["qa"])
            psv = t1.rearrange("p (h d) -> p h d", h=8)
            cosA = tab[:, 512:528]
            ssinA = tab[:, 528:544]
            tk = ["t1", TK]
            tt("dve", tA1, psv[:, :, 0:16], bc(cosA, [128, 8, 16], 1), ALU.mult, tk, ["tA1"])
            tt("dve", tA2[:, :, 0:8], psv[:, :, 8:16], bc(ssinA[:, 0:8], [128, 8, 8], 1), ALU.mult, tk, ["tA2"])
            tt("dve", tA2[:, :, 8:16], psv[:, :, 0:8], bc(ssinA[:, 8:16], [128, 8, 8], 1), ALU.mult, tk, ["tA2b"])
            for g in range(2):
                tt("dve", qa.rearrange("p (j g d) -> p j g d", j=4, g=2)[:, :, g, 0:16],
                   tA1[:, g * 4:(g + 1) * 4, :], tA2[:, g * 4:(g + 1) * 4, :], ALU.add,
                   ["tA1", "tA2", "tA2b", "qa"], ["qa"])

        def tile_kv(m):
            slot = blk_slot(m)
            tab = TAB[slot]
            TK = f"tab{slot}"
            cslot = m % 2
            cosA = tab[:, 512:528]
            ssinA = tab[:, 528:544]
            bank, bk = inproj(512, 256, 128)
            cp("act", ka32, bank[:, 0:128], [bk], ["ka32"])
            psv = bank[:, 0:128].rearrange("p (h d) -> p h d", h=2)
            tk = [bk, TK]
            tt("dve", tA1[:, :2, :], psv[:, :, 0:16], bc(cosA, [128, 2, 16], 1), ALU.mult, tk, ["tA1"])
            tt("dve", tA2[:, :2, 0:8], psv[:, :, 8:16], bc(ssinA[:, 0:8], [128, 2, 8], 1), ALU.mult, tk, ["tA2"])
            tt("dve", tA2[:, :2, 8:16], psv[:, :, 0:8], bc(ssinA[:, 8:16], [128, 2, 8], 1), ALU.mult, tk, ["tA2b"])
            tt("dve", ka32.rearrange("p (h d) -> p h d", h=2)[:, :, 0:16], tA1[:, :2, :], tA2[:, :2, :], ALU.add,
               ["tA1", "tA2", "tA2b", "ka32"], ["ka32"])
            cp("dve", kabf, ka32, ["ka32"], ["kabf"])
            cp("act", va32, bank[:, 128:256], [bk], ["va32"])
            cp("act", vbf[cslot], bank[:, 128:256], [bk], [f"vbf{cslot}"])

        def tile_rope(m, c0, tcol, dst, dkey):
            slot = blk_slot(m)
            tab = TAB[slot]
            TK = f"tab{slot}"
            bank, bk = inproj(c0, 512, 128)
            rk = [bk, TK]
            cos = tab[:, tcol:tcol + 128]
            ssin = tab[:, tcol + 128:tcol + 256].rearrange("p (i two) -> p i two", two=2)
            pv = bank[:, 0:512].rearrange("p (h d) -> p h d", h=4)
            pp = bank[:, 0:512].rearrange("p (h i two) -> p h i two", h=4, two=2)
            t1v = t1.rearrange("p (h d) -> p h d", h=4)
            t2p = t2.rearrange("p (h i two) -> p h i two", h=4, two=2)
            tt("dve", t1v, pv, bc(cos, [128, 4, 128], 1), ALU.mult, rk, ["t1"])
            tt("dve", t2p[:, :, :, 0], pp[:, :, :, 1], bc(ssin[:, :, 0], [128, 4, 64], 1), ALU.mult, rk, ["t2a"])
            tt("dve", t2p[:, :, :, 1], pp[:, :, :, 0], bc(ssin[:, :, 1], [128, 4, 64], 1), ALU.mult, rk, ["t2b"])
            tt("dve", dst, t1, t2, ALU.add, ["t1", "t2a", "t2b"], [dkey])

        def tile_qr(m):
            tile_rope(m, 768, 0, qr, "qr")

        def tile_kr(m):
            tile_rope(m, 1280, 256, kr, "kr")
            tt("pool", ktl.rearrange("p (h d) -> p h d", h=4), kr.rearrange("p (h d) -> p h d", h=4),
               bc(ktok, [128, 4, 128], 2), ALU.mult, ["kr", "cst"], ["ktl"])

        def tile_vr(m):
            for n in range(2):
                bank, bk = inproj(1792 + n * 512, 512, 128)
                cp("act", vr[:, n * 512:(n + 1) * 512], bank[:, 0:512], [bk], ["vr"])

        def tile_gate(m):
            for n in range(2):
                bank, bk = inproj(2816 + n * 512, 512, 128)
                act(og[:, n * 512:(n + 1) * 512], bank[:, 0:512], AF.Tanh, [bk], ["og"], scale=0.5)
                stt(sg[:, n * 512:(n + 1) * 512], og[:, n * 512:(n + 1) * 512], 1.0, bank[:, 0:512],
                    ALU.add, ALU.mult, [bk, "og"], ["sg"])

        def tile_gm(m):
            for n in range(2):
                bank, bk = inproj(3840 + n * 512, 512, 128)
                act(tha[:, n * 512:(n + 1) * 512], bank[:, 0:512], AF.Tanh, [bk], ["tha"], scale=0.5)
            for n in range(2):
                bank, bk = inproj(4864 + n * 512, 512, 128)
                act(thr[:, n * 512:(n + 1) * 512], bank[:, 0:512], AF.Tanh, [bk], ["thr"], scale=0.5)

        def head_norm(m):
            norm_T(blk_slot(m), 128, 0)

        def xi(m):
            return (m + 1) % 3

        def load_x(m):
            gb = NPRE + m
            S.dma("sp", X[xi(m)], xe[gb * 128:(gb + 1) * 128, :], f"x{xi(m)}", writes=[f"x{xi(m)}"])

        def load_tab(m):
            gb = NPRE + m
            sl = gb % 2
            S.dma("sp", TAB[sl], tabs[gb * 128:(gb + 1) * 128, :], f"tab{sl}", writes=[f"tab{sl}"])

        def head_norm_x(m, part=True, trans=True, a=True, b=True):
            if part:
                norm_part(xi(m), 128, a=a, b=b)
            if trans:
                norm_T(xi(m), 128, 0, do_part=False)

        tail_state = {}

        def p1(m):
            for n in range(2):
                bank, bk = ps()
                for j in range(4):
                    mm(bank[:, 0:512], oaT[:, j * 128:(j + 1) * 128], Wa[:, j, n * 512:(n + 1) * 512], j == 0, j == 3,
                       ["oaT", "wa"], [bk])
                stt(tha[:, n * 512:(n + 1) * 512], tha[:, n * 512:(n + 1) * 512], 1.0, bank[:, 0:512],
                    ALU.add, ALU.mult, [bk, "tha"], ["tha"])
                bank, bk = ps()
                for k in range(8):
                    mm(bank[:, 0:512], ogT[:, k, :], Wr[:, k, n * 512:(n + 1) * 512], k == 0, k == 7,
                       ["ogT", "wr"], [bk])
                stt(thr[:, n * 512:(n + 1) * 512], thr[:, n * 512:(n + 1) * 512], 1.0, bank[:, 0:512],
                    ALU.add, ALU.mult, [bk, "thr"], ["thr"])

        def p2(m):
            tt("dve", tha, tha, thr, ALU.add, ["tha", "thr"], ["tha"])
            bank, bk = ps()
            pb = bank[:].bitcast(BF16)
            for k in range(8):
                tp(pb[:, k * 128:(k + 1) * 128], tha[:, k * 128:(k + 1) * 128], ident, ["tha", "cbf"], [bk], inc=(k == 7))
            cp("act", ogT.rearrange("p k t -> p (k t)"), pb[:, 0:1024], [bk], ["ogT"])

        def p3(m):
            mb = []
            for n in range(2):
                bank, bk = ps(pin=True)
                for k in range(8):
                    mm(bank[:, 0:512], ogT[:, k, :], Wo[:, k, n * 512:(n + 1) * 512], k == 0, k == 7,
                       ["ogT", "wo"], [bk])
                mb.append((bank, bk))
                act(xs[:, n * 512:(n + 1) * 512], bank[:, 0:512], AF.Square, [bk], ["xs", f"ssm{n}"],
                    accum=st8[:, 16 + n:17 + n])
            tail_state["mb"] = mb

        def p4a(m):
            tt("dve", st8[:, 18:19], st8[:, 16:17], st8[:, 17:18], ALU.add, ["ssm0", "ssm1"], ["ssm"])
            ts("dve", st8[:, 19:20], st8[:, 18:19], 1.0 / D, 4.0 * EPS, ALU.mult, ALU.add, ["ssm"], ["msm"])
            tt("pool", st8[:, 20:21], st8[:, 19:20], CN05[:, 0:1], ALU.pow, ["msm", "cn05"], ["rstm"])

        def p4(m, last):
            mb = tail_state["mb"]
            xb = X[xi(m)]
            XK = f"x{xi(m)}"
            for n in range(2):
                bank, bk = mb[n]
                stt(bank[:, 0:512], bank[:, 0:512], st8[:, 20:21], GPM[:, n * 512:(n + 1) * 512], ALU.mult, ALU.mult,
                    [bk, "rstm", "gpm"], [bk])
                tt("dve", xb[:, n * 512:(n + 1) * 512], xb[:, n * 512:(n + 1) * 512], bank[:, 0:512], ALU.add, [XK, bk], [XK])
                unpin(bk)
            S.dma("sp", x1s[m * 128:(m + 1) * 128, :], xb, XK, reads=[XK], writes=[f"x1s{m}"])
            if last:
                S.dma("sp", kp_o, ka32, "ka32o", reads=["ka32"])
                S.dma("sp", vp_o, va32, "va32o", reads=["va32"])
            if m + 3 < NMAIN:
                load_x(m + 3)

        def mixer_block(m, last):
            pslot = (m - 1) % 2
            cslot = m % 2
            nxt = (not last)
            prev = m >= 1
            if m + 2 < NMAIN:
                load_tab(m + 2)
            bank, bk = ps()
            pb = bank[:].bitcast(BF16)
            for j in range(4):
                tp(pb[:, j * 128:(j + 1) * 128], qa[:, j * 128:(j + 1) * 128], ident, ["qa", "cbf"], [bk], inc=False)
            tp(pb[:, 512:640], kabf, ident, ["kabf", "cbf"], [bk], inc=True)
            cp("act", qaT, pb[:, 0:512], [bk], ["qaT"])
            cp("act", kT[cslot], pb[:, 512:640], [bk], [f"kT{cslot}"])
            bank, bk = ps()
            pb = bank[:].bitcast(BF16)
            for h in range(4):
                tp(pb[:, h * 128:(h + 1) * 128], qr[:, h * 128:(h + 1) * 128], ident, ["qr", "cbf"], [bk], inc=False)
            for h in range(4):
                tp(pb[:, 512 + h * 128:512 + (h + 1) * 128], kr[:, h * 128:(h + 1) * 128], ident, ["kr", "cbf"], [bk], inc=(h == 3))
            tt("dve", qrT, pb[:, 0:512], dqT, ALU.mult, [bk, "cst"], ["qrT"])
            tt("dve", krT, pb[:, 512:1024], dkT, ALU.mult, [bk, "cst"], ["krT"])
            if prev:
                p1(m - 1)
            if nxt:
                head_norm_x(m + 1, part=True, trans=False, a=True, b=False)
            sbanks = [ps(pin=True), ps(pin=True)]
            for h in range(4):
                sb, sk = sbanks[h // 2]
                c = (h % 2) * 256
                mm(sb[:, c:c + 256], ktl[:, h * 128:(h + 1) * 128], vr[:, h * 256:(h + 1) * 256], True, True,
                   ["ktl", "vr"], [sk])
            pm = mprev1 if m == 1 else mprev
            srcs = [(pslot, pm), (cslot, mown)]
            idx = 0
            for kv in range(2):
                for (sl, msk) in srcs:
                    bank, bk = ps()
                    mm(bank[:, 0:512], kT[sl][kv * 64:(kv + 1) * 64, :], qaT[kv * 64:(kv + 1) * 64, :], True, True,
                       [f"kT{sl}", "qaT"], [bk])
                    act(PT[idx], bank[:, 0:512], AF.Exp, [bk], [f"PT{idx}"], scale=0.125)
                    tt("pool", PT[idx].rearrange("p (j q) -> p j q", j=4), PT[idx].rearrange("p (j q) -> p j q", j=4),
                       bc(msk, [128, 4, 128], 1), ALU.mult, [f"PT{idx}", "cbf"], [f"PT{idx}"])
                    idx += 1
            if prev:
                p2(m - 1)
            tile_gm(m)
            if nxt:
                head_norm_x(m + 1, part=True, trans=True, a=False, b=True)
            bank, bk = ps()
            for h in range(4):
                mm(bank[:, h * 128:(h + 1) * 128], krT[:, h * 128:(h + 1) * 128], qrT[:, h * 128:(h + 1) * 128], True, True,
                   ["krT", "qrT"], [bk])
            tt("dve", AT.rearrange("p (h i) -> p h i", h=4), bank[:, 0:512].rearrange("p (h i) -> p h i", h=4),
               bc(mown, [128, 4, 128], 1), ALU.mult, [bk, "cbf"], ["AT"])
            bo, bok = ps()
            bd, bdk = ps()
            idx = 0
            for kv in range(2):
                for i, (sl, msk) in enumerate(srcs):
                    mm(bo[kv * 64:(kv + 1) * 64, 0:512], vbf[sl][:, kv * 64:(kv + 1) * 64], PT[idx], i == 0, i == 1,
                       [f"vbf{sl}", f"PT{idx}"], [bok])
                    idx += 1
            idx = 0
            for kv in range(2):
                for i, (sl, msk) in enumerate(srcs):
                    mm(bd[kv * 64:(kv + 1) * 64, 0:512], ones64, PT[idx], i == 0, i == 1,
                       ["cbf", f"PT{idx}"], [bdk])
                    idx += 1
            tt("dve", t1.rearrange("p (j q) -> p j q", j=4), bd[:, 0:512].rearrange("p (j q) -> p j q", j=4),
               bc(ESK, [128, 4, 128], 2), ALU.add, [bdk, "esk"], ["t1"])
            S.op("dve", lambda e: e.reciprocal(out=t1, in_=t1), ["t1"], ["t1"])
            tt("dve", oaT, bo[:, 0:512], t1, ALU.mult, [bok, "t1"], ["oaT"])
            if nxt:
                tile_qa(m + 1)
            if prev:
                p3(m - 1)
                p4a(m - 1)
            obanks = [ps(), ps()]
            for h in range(4):
                ob, ok = obanks[h // 2]
                c = (h % 2) * 256
                mm(ob[:, c:c + 256], AT[:, h * 128:(h + 1) * 128], vr[:, h * 256:(h + 1) * 256], True, False,
                   ["AT", "vr"], [ok])
                mm(ob[:, c:c + 256], qrT[:, h * 128:(h + 1) * 128], Sbf[:, h * 256:(h + 1) * 256], False, True,
                   ["qrT", "Sbf"], [ok])
            for h in range(4):
                ob, ok = obanks[h // 2]
                c = (h % 2) * 256
                act(xs[:, h * 256:(h + 1) * 256], ob[:, c:c + 256], AF.Square, [ok], ["xs", f"ssr{h}"], accum=st8[:, 4 + h:5 + h])
            ts("dve", st8[:, 8:12], st8[:, 4:8], 4.0 / 256.0, 4.0 * EPS, ALU.mult, ALU.add,
               [f"ssr{h}" for h in range(4)], ["msr"])
            tt("pool", st8[:, 12:16], st8[:, 8:12], CN05, ALU.pow, ["msr", "cn05"], ["rstr"])
            if nxt:
                tile_kr(m + 1)
            for h in range(4):
                ob, ok = obanks[h // 2]
                c = (h % 2) * 256
                stt(og[:, h * 256:(h + 1) * 256], ob[:, c:c + 256], st8[:, 12 + h:13 + h], sg[:, h * 256:(h + 1) * 256],
                    ALU.mult, ALU.mult, [ok, "rstr", "sg"], ["og"])
            if prev:
                p4(m - 1, False)
            for h in range(4):
                sb, sk = sbanks[h // 2]
                c = (h % 2) * 256
                stt(S32[:, h * 256:(h + 1) * 256], S32[:, h * 256:(h + 1) * 256], float(GAM[h] ** 128),
                    sb[:, c:c + 256], ALU.mult, ALU.add, [sk, "S32"], ["S32"])
            unpin(sbanks[0][1])
            unpin(sbanks[1][1])
            cp("act", Sbf, S32, ["S32"], ["Sbf"])
            if nxt:
                tile_qr(m + 1)
                tile_kv(m + 1)
                tile_vr(m + 1)
            bank, bk = ps()
            pb = bank[:].bitcast(BF16)
            for k in range(8):
                tp(pb[:, k * 128:(k + 1) * 128], og[:, k * 128:(k + 1) * 128], ident, ["og", "cbf"], [bk], inc=(k == 7))
            cp("act", ogT.rearrange("p k t -> p (k t)"), pb[:, 0:1024], [bk], ["ogT"])
            if nxt:
                tile_gate(m + 1)
            if m < 16:
                r0 = m * 64
                its = [(wupbf[r0:r0 + 64, c * 1024:(c + 1) * 1024], w_up[r0:r0 + 64, c * 1024:(c + 1) * 1024], f"cvu{m}_{c}")
                       for c in range(6)]
                r1 = m * 192
                its.append((wdnbf[r1:r1 + 192, :], w_dn[r1:r1 + 192, :], f"cvd{m}"))
                S.dma_multi("pool", its, "cvt")

        load_x(1)
        load_tab(1)
        load_x(2)
        head_norm_x(0)
        tile_qa(0)
        tile_kv(0)
        tile_qr(0)
        tile_kr(0)
        tile_vr(0)
        tile_gate(0)
        for m in range(NMAIN):
            mixer_block(m, m == NMAIN - 1)
        p1(NMAIN - 1)
        p2(NMAIN - 1)
        p3(NMAIN - 1)
        p4a(NMAIN - 1)
        p4(NMAIN - 1, True)
        S.dma("sp", rp_o.rearrange("h k v -> k h v"), S32.rearrange("p (h v) -> p h v", h=4), "s32o", reads=["S32"])


        pre_state = {}

        def sample_mixer():
            S.barrier()
            NP = 64
            CS2 = A.alloc(128, [544], F32)
            CKB = [kT[1], A.alloc(128, [128], BF16)]
            CVB = [vbf[1], A.alloc(128, [128], BF16)]
            CKT = [A.alloc(128, [128], BF16) for _ in range(2)]
            ones128 = CBF[:, 640:768]
            maskc = CBF[:, 768:772]
            mnew = CBF[0:64, 832:896]
            dqs = CS2[:, 0:256]
            dms = CS2[0:64, 256:512]
            ktoks = CS2[0:64, 512:516]
            rowm = CS2[0:64, 516:532]
            S0B = [X[1], S32]
            Qb = [qaT[:, 256:512], krT[:, 256:512]]
            Kb = [PT[2], PT[3]]
            S.dma("sp", CS2, cs2, "cs2", writes=["cs2"])
            S.dma("sp", X[0][:NP], xsm, "x0", writes=["x0"])
            S.dma("sp", TAB[0][:NP], tabs[4096:4160, :], "tab0", writes=["tab0"])
            S.dma("sp", ks_o[:, 0:124, :], ck[:, 4:128, :], "kcpy")
            S.dma("sp", vs_o[:, 0:124, :], cv[:, 4:128, :], "vcpy")
            tab = TAB[0]
            TK = "tab0"
            xb = X[0]
            XK = "x0"
            norm_T(0, NP, 0)
            if SCUT == 1:
                pinned.clear()
                return
            bank, bk = inproj(0, 512, NP)
            cp("act", t1[:NP], bank[:NP, 0:512], [bk], ["t1"])
            for g in range(2):
                cp("pool", qa[:NP].rearrange("p (j g d) -> p j g d", j=4, g=2)[:, :, g, :],
                   t1[:NP, g * 256:(g + 1) * 256].rearrange("p (j d) -> p j d", j=4), ["t1"], ["qa"])
            psv = t1[:NP].rearrange("p (h d) -> p h d", h=8)
            cosA = tab[:NP, 512:528]
            ssinA = tab[:NP, 528:544]
            tk = ["t1", TK]
            tt("dve", tA1[:NP], psv[:, :, 0:16], bc(cosA, [NP, 8, 16], 1), ALU.mult, tk, ["tA1"])
            tt("dve", tA2[:NP, :, 0:8], psv[:, :, 8:16], bc(ssinA[:, 0:8], [NP, 8, 8], 1), ALU.mult, tk, ["tA2"])
            tt("dve", tA2[:NP, :, 8:16], psv[:, :, 0:8], bc(ssinA[:, 8:16], [NP, 8, 8], 1), ALU.mult, tk, ["tA2b"])
            for g in range(2):
                tt("dve", qa[:NP].rearrange("p (j g d) -> p j g d", j=4, g=2)[:, :, g, 0:16],
                   tA1[:NP, g * 4:(g + 1) * 4, :], tA2[:NP, g * 4:(g + 1) * 4, :], ALU.add,
                   ["tA1", "tA2", "tA2b", "qa"], ["qa"])
            bank, bk = inproj(512, 256, NP)
            cp("act", ka32[:NP], bank[:NP, 0:128], [bk], ["ka32"])
            cp("act", t2[:NP, 0:128], bank[:NP, 0:128], [bk], ["t2"])
            psv = t2[:NP, 0:128].rearrange("p (h d) -> p h d", h=2)
            tk = ["t2", TK]
            tt("dve", tA1[:NP, :2, :], psv[:, :, 0:16], bc(cosA, [NP, 2, 16], 1), ALU.mult, tk, ["tA1"])
            tt("dve", tA2[:NP, :2, 0:8], psv[:, :, 8:16], bc(ssinA[:, 0:8], [NP, 2, 8], 1), ALU.mult, tk, ["tA2"])
            tt("dve", tA2[:NP, :2, 8:16], psv[:, :, 0:8], bc(ssinA[:, 8:16], [NP, 2, 8], 1), ALU.mult, tk, ["tA2b"])
            tt("dve", ka32[:NP].rearrange("p (h d) -> p h d", h=2)[:, :, 0:16], tA1[:NP, :2, :], tA2[:NP, :2, :], ALU.add,
               ["tA1", "tA2", "tA2b", "ka32"], ["ka32"])
            cp("dve", kabf[:NP], ka32[:NP], ["ka32"], ["kabf"])
            cp("act", va32[:NP], bank[:NP, 128:256], [bk], ["va32"])
            cp("act", vbf[0][:NP], bank[:NP, 128:256], [bk], ["vbf0"])
            for b in range(16):
                S.dma("sp", ks_o[b, 124:128, :], ka32[b * 4:(b + 1) * 4, :], "ka32o", reads=["ka32"])
                S.dma("sp", vs_o[b, 124:128, :], va32[b * 4:(b + 1) * 4, :], "va32o", reads=["va32"])
            if SCUT == 2:
                pinned.clear()
                return
            for (c0, tcol, dst, dk_) in ((768, 0, qr, "qr"), (1280, 256, kr, "kr")):
                bank, bk = inproj(c0, 512, NP)
                rk = [bk, TK]
                cos = tab[:NP, tcol:tcol + 128]
                ssin = tab[:NP, tcol + 128:tcol + 256].rearrange("p (i two) -> p i two", two=2)
                pv = bank[:NP, 0:512].rearrange("p (h d) -> p h d", h=4)
                pp = bank[:NP, 0:512].rearrange("p (h i two) -> p h i two", h=4, two=2)
                t1v = t1[:NP].rearrange("p (h d) -> p h d", h=4)
                t2p = t2[:NP].rearrange("p (h i two) -> p h i two", h=4, two=2)
                tt("dve", t1v, pv, bc(cos, [NP, 4, 128], 1), ALU.mult, rk, ["t1"])
                tt("dve", t2p[:, :, :, 0], pp[:, :, :, 1], bc(ssin[:, :, 0], [NP, 4, 64], 1), ALU.mult, rk, ["t2"])
                tt("dve", t2p[:, :, :, 1], pp[:, :, :, 0], bc(ssin[:, :, 1], [NP, 4, 64], 1), ALU.mult, rk, ["t2b"])
                tt("dve", dst[:NP], t1[:NP], t2[:NP], ALU.add, ["t1", "t2", "t2b"], [dk_])
            tt("pool", ktl[:NP].rearrange("p (h d) -> p h d", h=4), kr[:NP].rearrange("p (h d) -> p h d", h=4),
               bc(ktoks, [NP, 4, 128], 2), ALU.mult, ["kr", "cs2"], ["ktl"])
            for n in range(2):
                bank, bk = inproj(1792 + n * 512, 512, NP)
                cp("act", vr[:NP, n * 512:(n + 1) * 512], bank[:NP, 0:512], [bk], ["vr"])
            for n in range(2):
                bank, bk = inproj(2816 + n * 512, 512, NP)
                act(tha[:NP, n * 512:(n + 1) * 512], bank[:NP, 0:512], AF.Tanh, [bk], ["tha"], scale=0.5)
                stt(sg[:NP, n * 512:(n + 1) * 512], tha[:NP, n * 512:(n + 1) * 512], 1.0, bank[:NP, 0:512],
                    ALU.add, ALU.mult, [bk, "tha"], ["sg"])
            if SCUT == 3:
                pinned.clear()
                return
            for n in range(2):
                bank, bk = inproj(3840 + n * 512, 512, NP)
                act(tha[:NP, n * 512:(n + 1) * 512], bank[:NP, 0:512], AF.Tanh, [bk], ["tha"], scale=0.5)
            for n in range(2):
                bank, bk = inproj(4864 + n * 512, 512, NP)
                act(thr[:NP, n * 512:(n + 1) * 512], bank[:NP, 0:512], AF.Tanh, [bk], ["thr"], scale=0.5)
            off_save = A.off
            A.off = mark_w
            WupE = A.alloc(128, [8, F2], BF16)
            A.off = off_save
            for k in range(8):
                S.dma("sp", WupE[:, k, :], wupbf[k * 128:(k + 1) * 128, :], "wup",
                      writes=(["winA0", "winA", "winB", "x2", "wupk0"] if k == 0 else [f"wupk{k}"]))
            pre_state["wup"] = True
            bank, bk = ps()
            pb = bank[:].bitcast(BF16)
            for j in range(4):
                tp(pb[:, j * 128:j * 128 + 64], qa[:NP, j * 128:(j + 1) * 128], ident[:NP, :NP], ["qa", "cbf"], [bk], inc=False)
            tp(pb[:, 512:576], kabf[:NP], ident[:NP, :NP], ["kabf", "cbf"], [bk], inc=True)
            v4 = lambda ap: ap.rearrange("p (j q) -> p j q", j=4)
            cp("act", v4(qaT[:, 0:256]), v4(pb[:, 0:512])[:, :, 0:64], [bk], ["qaT"])
            cp("act", kT[0][:, 0:64], pb[:, 512:576], [bk], ["kT0"])
            if SCUT == 31:
                return
            bank, bk = ps()
            pb = bank[:].bitcast(BF16)
            for h in range(4):
                tp(pb[:, h * 128:h * 128 + 64], qr[:NP, h * 128:(h + 1) * 128], ident[:NP, :NP], ["qr", "cbf"], [bk], inc=False)
            for h in range(4):
                tp(pb[:, 512 + h * 128:512 + h * 128 + 64], kr[:NP, h * 128:(h + 1) * 128], ident[:NP, :NP], ["kr", "cbf"], [bk], inc=(h == 3))
            if SCUT == 32:
                S.op("pe", lambda e: e.transpose(pb[:, 0:64], qr[:NP, 0:128], ident[:NP, :NP]), ["qr"], [bk])
                return
            cp("act", v4(qrT[:, 0:256]), v4(pb[:, 0:512])[:, :, 0:64], [bk], ["qrT"])
            if SCUT == 33:
                return
            tt("dve", qrT[:, 256:512], qrT[:, 0:256], dqs, ALU.mult, ["qrT", "cs2"], ["qtT"])
            cp("act", v4(krT[:, 0:256]), v4(pb[:, 512:1024])[:, :, 0:64], [bk], ["krT"])
            if SCUT == 4:
                pinned.clear()
                return
            bS2 = [ps(pin=True), ps(pin=True)]
            CB = [PT[2][:, i * 128:(i + 1) * 128] for i in range(4)] + [PT[3][:, i * 128:(i + 1) * 128] for i in range(4)]
            for b in range(16):
                sl = b % 2
                c8 = b % 8
                S.dma("pool", CB[c8], ck[b], f"cb{c8}", writes=[f"cb{c8}"])
                bank, bk = ps()
                pb = bank[:].bitcast(BF16)
                tp(pb[:, 0:128], CB[c8], ident, [f"cb{c8}", "cbf"], [bk], inc=True)
                cp("act", CKT[sl], pb[:, 0:128], [bk], [f"ckt{sl}"])
                for kv in range(2):
                    off = b * 16
                    mm(bS2[kv][0][:, off:off + 16].rearrange("p (j t) -> p j t", j=4),
                       CKT[sl][kv * 64:(kv + 1) * 64, :],
                       qaT[kv * 64:(kv + 1) * 64, 0:256].rearrange("p (j q) -> p j q", j=4)[:, :, b * 4:(b + 1) * 4],
                       True, True, [f"ckt{sl}", "qaT"], [bS2[kv][1]])
            if SCUT == 5:
                pinned.clear()
                return
            PTc = PT[0]
            PTn = PT[1]
            for kv in range(2):
                act(PTc[:, kv * 256:(kv + 1) * 256], bS2[kv][0][:, 0:256], AF.Exp, [bS2[kv][1]], ["PTc"], scale=0.125)
                unpin(bS2[kv][1])
            tt("pool", PTc.rearrange("p (g t) -> p g t", t=4), PTc.rearrange("p (g t) -> p g t", t=4),
               bc(maskc, [128, 128, 4], 1), ALU.mult, ["PTc", "cbf"], ["PTc"])
            if SCUT == 51:
                return
            S.op("pool", lambda e: e.memset(PTn, 0.0), (), ["PTn"])
            S.op("pool", lambda e: e.memset(AT, 0.0), (), ["AT"])
            for kv in range(2):
                bN, bNk = ps()
                mm(bN[:, 0:256], kT[0][kv * 64:(kv + 1) * 64, 0:128], qaT[kv * 64:(kv + 1) * 64, 0:256],
                   True, True, ["kT0", "qaT"], [bNk])
                act(PTn[:NP, kv * 256:(kv + 1) * 256], bN[:NP, 0:256], AF.Exp, [bNk], ["PTn"], scale=0.125)
            if SCUT == 52:
                return
            tt("pool", PTn[:NP].rearrange("p (g q) -> p g q", g=8), PTn[:NP].rearrange("p (g q) -> p g q", g=8),
               bc(mnew, [NP, 8, 64], 1), ALU.mult, ["PTn", "cbf"], ["PTn"])
            if SCUT == 6:
                pinned.clear()
                return
            bo, bok = ps()
            bdn = [ps(), ps()]
            bdc, bdck = ps()
            mm(bdc[:, 0:512], ones128, PTc, True, True, ["cbf", "PTc"], [bdck])
            for kv in range(2):
                mm(bdn[kv][0][:, 0:256], ones128, PTn[:, kv * 256:(kv + 1) * 256], True, True,
                   ["cbf", "PTn"], [bdn[kv][1]])
            for kv in range(2):
                mm(bo[kv * 64:(kv + 1) * 64, 0:256], vbf[0][:, kv * 64:(kv + 1) * 64], PTn[:, kv * 256:(kv + 1) * 256],
                   True, False, ["vbf0", "PTn"], [bok])
            for b in range(16):
                c8 = b % 8
                S.dma("pool", CB[c8], cv[b], f"cb{c8}", writes=[f"cb{c8}"])
                for kv in range(2):
                    off = kv * 256 + b * 16
                    mm(bo[kv * 64:(kv + 1) * 64, 0:256].rearrange("p (j q) -> p j q", j=4)[:, :, b * 4:(b + 1) * 4],
                       CB[c8][:, kv * 64:(kv + 1) * 64],
                       PTc[:, off:off + 16].rearrange("p (j t) -> p j t", j=4),
                       False, b == 15, [f"cb{c8}", "PTc"], [bok], inc=True)
            if SCUT == 7:
                pinned.clear()
                return
            cp("act", t2, bdc[:, 0:512], [bdck], ["t2"])
            for kv in range(2):
                hs = slice(kv * 64, (kv + 1) * 64)
                tt("dve", t1[hs, 0:256].rearrange("p (j b t) -> p j b t", j=4, b=16),
                   bdn[kv][0][hs, 0:256].rearrange("p (j b t) -> p j b t", j=4, b=16),
                   t2[hs, :].rearrange("p (kv b j t) -> p kv j b t", b=16, kv=2, j=4)[:, kv],
                   ALU.add, [bdn[kv][1], "t2"], [f"t1{kv}"])
            tt("dve", t1[:, 0:256].rearrange("p (j q) -> p j q", j=4), t1[:, 0:256].rearrange("p (j q) -> p j q", j=4),
               bc(ESK, [128, 4, 64], 2), ALU.add, ["t10", "t11", "esk"], ["t1"])
            S.op("dve", lambda e: e.reciprocal(out=t1[:, 0:256], in_=t1[:, 0:256]), ["t1"], ["t1"])
            tt("dve", oaT[:, 0:256], bo[:, 0:256], t1[:, 0:256], ALU.mult, [bok, "t1"], ["oaT"])
            if SCUT == 8:
                pinned.clear()
                return
            bR, bRk = ps()
            for h in range(4):
                mm(bR[:NP, h * 64:(h + 1) * 64], krT[:, h * 64:(h + 1) * 64], qrT[:, h * 64:(h + 1) * 64], True, True,
                   ["krT", "qrT"], [bRk])
            tt("dve", AT[:NP, 0:256], bR[:NP, 0:256], dms, ALU.mult, [bRk, "cs2"], ["AT"])
            ob = [ps(pin=True) for _ in range(4)]
            for h in range(4):
                mm(ob[h][0][:NP, 0:256], AT[:, h * 64:(h + 1) * 64], vr[:, h * 256:(h + 1) * 256], True, False,
                   ["AT", "vr"], [ob[h][1]])
            if SCUT == 9:
                pinned.clear()
                return
            S.op("dve", lambda e: e.memset(Qb[0], 0.0), (), ["qb0"])
            S.op("dve", lambda e: e.memset(Qb[1], 0.0), (), ["qb1"])
            def load_s0(b_):
                sl_ = b_ % 2
                S.dma("sp", S0B[sl_].rearrange("p (h v) -> p h v", h=4), sr[b_].rearrange("h k v -> k h v"), f"s0{sl_}",
                      writes=[f"s0{sl_}"])

            load_s0(0)
            load_s0(1)
            for b in range(16):
                sl = b % 2
                s0 = S0B[sl]
                sk_ = f"s0{sl}"
                cp("act", Sbf, s0, [sk_], ["Sbf"])
                qv = Qb[sl].rearrange("p (h q) -> p h q", h=4)
                if b >= 2:
                    S.op("dve", (lambda v: lambda e: e.memset(v, 0.0))(qv[:, :, (b - 2) * 4:(b - 1) * 4]), (), [f"qb{sl}"])
                cp("dve", qv[:, :, b * 4:(b + 1) * 4], qrT[:, 256:512].rearrange("p (h q) -> p h q", h=4)[:, :, b * 4:(b + 1) * 4],
                   ["qtT"], [f"qb{sl}"])
                act(Kb[sl][:NP], ktl[:NP], AF.Copy, ["ktl", "cs2"], [f"kb{sl}"] + [f"cb{sl * 4 + i}" for i in range(4)],
                    scale=rowm[:, b:b + 1])
                for h in range(4):
                    mm(ob[h][0][:NP, 0:256], Qb[sl][:, h * 64:(h + 1) * 64], Sbf[:, h * 256:(h + 1) * 256], False, b == 15,
                       [f"qb{sl}", "Sbf"], [ob[h][1]])
                sbanks = [ps(), ps()]
                for h in range(4):
                    sb, sk2 = sbanks[h // 2]
                    c = (h % 2) * 256
                    mm(sb[:, c:c + 256], Kb[sl][:NP, h * 128:(h + 1) * 128], vr[:NP, h * 256:(h + 1) * 256], True, True,
                       [f"kb{sl}", "vr"], [sk2])
                for h in range(4):
                    sb, sk2 = sbanks[h // 2]
                    c = (h % 2) * 256
                    stt(s0[:, h * 256:(h + 1) * 256], s0[:, h * 256:(h + 1) * 256], float(GAM[h] ** 4),
                        sb[:, c:c + 256], ALU.mult, ALU.add, [sk2, sk_, "Sbf"], [sk_])
                S.dma("sp", rs_o[b].rearrange("h k v -> k h v"), s0.rearrange("p (h v) -> p h v", h=4), sk_, reads=[sk_])
                if b + 2 < 16:
                    load_s0(b + 2)
            if SCUT == 10:
                pinned.clear()
                return
            for h in range(4):
                unpin(ob[h][1])
            for h in range(4):
                act(xs[:NP, h * 256:(h + 1) * 256], ob[h][0][:NP, 0:256], AF.Square, [ob[h][1]], ["xs", f"ssr{h}"], accum=st8[:NP, 4 + h:5 + h])
            ts("dve", st8[:NP, 8:12], st8[:NP, 4:8], 4.0 / 256.0, 4.0 * EPS, ALU.mult, ALU.add,
               [f"ssr{h}" for h in range(4)], ["msr"])
            tt("pool", st8[:NP, 12:16], st8[:NP, 8:12], CN05[:NP], ALU.pow, ["msr", "cn05"], ["rstr"])
            for h in range(4):
                stt(og[:NP, h * 256:(h + 1) * 256], ob[h][0][:NP, 0:256], st8[:NP, 12 + h:13 + h], sg[:NP, h * 256:(h + 1) * 256],
                    ALU.mult, ALU.mult, [ob[h][1], "rstr", "sg"], ["og"])
            bank, bk = ps()
            pb = bank[:].bitcast(BF16)
            for k in range(8):
                tp(pb[:, k * 128:k * 128 + 64], og[:NP, k * 128:(k + 1) * 128], ident[:NP, :NP], ["og", "cbf"], [bk], inc=(k == 7))
            cp("act", ogT[:, :, 0:64], pb[:, 0:1024].rearrange("p (k t) -> p k t", k=8)[:, :, 0:64], [bk], ["ogT"])
            for n in range(2):
                bank, bk = ps()
                for j in range(4):
                    mm(bank[:NP, 0:512], oaT[:, j * 64:(j + 1) * 64], Wa[:, j, n * 512:(n + 1) * 512], j == 0, j == 3,
                       ["oaT", "wa"], [bk])
                stt(tha[:NP, n * 512:(n + 1) * 512], tha[:NP, n * 512:(n + 1) * 512], 1.0, bank[:NP, 0:512],
                    ALU.add, ALU.mult, [bk, "tha"], ["tha"])
                bank, bk = ps()
                for k in range(8):
                    mm(bank[:NP, 0:512], ogT[:, k, 0:64], Wr[:, k, n * 512:(n + 1) * 512], k == 0, k == 7,
                       ["ogT", "wr"], [bk])
                stt(thr[:NP, n * 512:(n + 1) * 512], thr[:NP, n * 512:(n + 1) * 512], 1.0, bank[:NP, 0:512],
                    ALU.add, ALU.mult, [bk, "thr"], ["thr"])
            tt("dve", tha[:NP], tha[:NP], thr[:NP], ALU.add, ["tha", "thr"], ["tha"])
            bank, bk = ps()
            pb = bank[:].bitcast(BF16)
            for k in range(8):
                tp(pb[:, k * 128:k * 128 + 64], tha[:NP, k * 128:(k + 1) * 128], ident[:NP, :NP], ["tha", "cbf"], [bk], inc=(k == 7))
            cp("act", ogT[:, :, 0:64], pb[:, 0:1024].rearrange("p (k t) -> p k t", k=8)[:, :, 0:64], [bk], ["ogT"])
            mb = []
            for n in range(2):
                bank, bk = ps()
                for k in range(8):
                    mm(bank[:NP, 0:512], ogT[:, k, 0:64], Wo[:, k, n * 512:(n + 1) * 512], k == 0, k == 7,
                       ["ogT", "wo"], [bk])
                mb.append((bank, bk))
                act(xs[:NP, n * 512:(n + 1) * 512], bank[:NP, 0:512], AF.Square, [bk], ["xs", f"ssm{n}"],
                    accum=st8[:NP, 16 + n:17 + n])
            tt("dve", st8[:NP, 18:19], st8[:NP, 16:17], st8[:NP, 17:18], ALU.add, ["ssm0", "ssm1"], ["ssm"])
            ts("dve", st8[:NP, 19:20], st8[:NP, 18:19], 1.0 / D, 4.0 * EPS, ALU.mult, ALU.add, ["ssm"], ["msm"])
            tt("pool", st8[:NP, 20:21], st8[:NP, 19:20], CN05[:NP, 0:1], ALU.pow, ["msm", "cn05"], ["rstm"])
            for n in range(2):
                bank, bk = mb[n]
                stt(bank[:NP, 0:512], bank[:NP, 0:512], st8[:NP, 20:21], GPM[:NP, n * 512:(n + 1) * 512], ALU.mult, ALU.mult,
                    [bk, "rstm", "gpm"], [bk])
                tt("dve", xb[:NP, n * 512:(n + 1) * 512], xb[:NP, n * 512:(n + 1) * 512], bank[:NP, 0:512], ALU.add, [XK, bk], [XK])
            S.dma("sp", x1s[NMAIN * 128:NMAIN * 128 + 64, :], xb[:NP], XK, reads=[XK], writes=["x1ss"])

        sample_mixer()

        def ffn_phase():
            S.barrier()
            A.off = mark_w
            Wup = A.alloc(128, [8, F2], BF16)
            Wdn = A.alloc(128, [24, D], BF16)
            CW = A.alloc(128, [48, 4], F32)
            Uh = A.alloc(128, [48, 2], F32)
            X1R = A.alloc(128, [4 * D], F32)
            X1 = [X1R[:, i * D:(i + 1) * D] for i in range(4)]
            xs2 = A.alloc(128, [D], BF16)
            XN2 = [A.alloc(128, [8, 256], BF16) for _ in range(2)]
            xn2T = XN2[0]
            Ua = [A.alloc(128, [258], F32) for _ in range(2)]
            Ub = [A.alloc(128, [258], F32) for _ in range(2)]
            CA = [A.alloc(128, [256], F32) for _ in range(2)]
            CB = [A.alloc(128, [256], F32) for _ in range(2)]
            G2s = A.alloc(128, [256], F32)
            G2 = [G2s, G2s]
            G1 = [CA[1], CB[1]]
            ca, cb_, g1, g2 = CA[0], CB[0], CA[1], G2s
            hT = A.alloc(128, [24, 256], BF16)
            tmpf = A.alloc(128, [D], F32)
            stf = A.alloc(128, [16], F32)
            CPO = A.alloc(128, [F2], F32) if False else None
            items = []
            for k in range(8):
                for c in range(6):
                    items.append((Wup[:, k, c * 1024:(c + 1) * 1024], w_up[k * 128:(k + 1) * 128, c * 1024:(c + 1) * 1024], "wup"))
            if not pre_state.get("wup"):
                S.dma_multi("pool", items, "wup")
            S.dma_multi("sp", [(Wdn[:, k, :], wdnbf[k * 128:(k + 1) * 128, :], "wdn") for k in range(24)], "wdn")
            S.dma("sp", CW.rearrange("p a b -> p (a b)"), cwT, "cw", writes=["cw"])
            S.dma("sp", GPM, gpost[1].partition_broadcast(128), "gpm", writes=["gpm"])

            def norm_T2(xt, xk, np_, dest, dkey, part=True, trans=True):
                if part:
                    act(tmpf[:np_], xt[:np_], AF.Square, [xk], ["tmpf", "fssq"], accum=stf[:np_, 0:1])
                    ts("dve", stf[:np_, 1:2], stf[:np_, 0:1], 1.0 / D, EPS, ALU.mult, ALU.add, ["fssq"], ["fms"])
                    tt("pool", stf[:np_, 2:3], stf[:np_, 1:2], CN05[:np_, 0:1], ALU.pow, ["fms", "cn05"], ["frstd"])
                    act(xs2[:np_], xt[:np_], AF.Copy, [xk, "frstd"], ["xs2"], scale=stf[:np_, 2:3])
                if not trans:
                    return
                bank, bk = ps()
                pb = bank[:].bitcast(BF16)
                for k in range(8):
                    tp(pb[:, k * 128:k * 128 + np_], xs2[:np_, k * 128:(k + 1) * 128], ident[:np_, :np_],
                       ["xs2", "cbf"], [bk], inc=(k == 7))
                pv = pb.rearrange("p (k t) -> p k t", k=8)[:, :, :np_]
                tt("dve", dest, pv, bc(GVT[:, 8:16], [128, 8, np_], 2), ALU.mult, [bk, "gvt"], [dkey])

            S.dma("sp", X1[0][0:2], x1s[126:128, :], "x1_0", writes=["x1_0"])
            norm_T2(X1[0], "x1_0", 2, xn2T[:, :, 0:2], "xn2T0")
            bank, bk = ps()
            for t in range(48):
                for k in range(8):
                    mm(bank[:, t * 2:t * 2 + 2], Wup[:, k, t * 128:(t + 1) * 128], xn2T[:, k, 0:2], k == 0, k == 7,
                       ["wup", "xn2T0"], [bk])
            cp("act", Uh.rearrange("p a b -> p (a b)"), bank[:, 0:96], [bk], ["uh"])

            def act_id(out, in_, sc, bi, reads, writes):
                S.op("act", lambda e: e.activation(out=out, in_=in_, func=AF.Identity, scale=sc, bias=bi), reads, writes)

            def conv_gelu(j, ntok, ub_a, ub_b, bka, bkb, hdst, uview, slot):
                tiles = ((Ua[slot], ub_a, bka, j, CA[slot], f"ca{slot}"), (Ub[slot], ub_b, bkb, 24 + j, CB[slot], f"cb{slot}"))
                for (U, bank_, bk_, tix, cdst, ckey) in tiles:
                    uk = f"U{ckey}"
                    cp("pool", U[:, 0:2], Uh[:, tix, :], ["uh"], [uk + "h"])
                    cp("act", U[:, 2:2 + ntok], bank_[:, 0:ntok], [bk_], [uk])
                    act_id(cdst[:, :ntok], bank_[:, 0:ntok], CW[:, tix, 2:3], CW[:, tix, 3:4], [bk_, "cw"], [ckey])
                    cp("pool", Uh[:, tix, :], U[:, ntok:ntok + 2], [uk], ["uh"])
                for tap in (1, 0):
                    for (U, bank_, bk_, tix, cdst, ckey) in tiles:
                        uk = f"U{ckey}"
                        stt(cdst[:, :ntok], U[:, tap:tap + ntok], CW[:, tix, tap:tap + 1], cdst[:, :ntok], ALU.mult, ALU.add,
                            [uk, uk + "h", "cw", ckey], [ckey])
                if uview != "defer":
                    gelu_mul(ntok, hdst, slot, uview if uview else "hT")

            def gelu_mul(ntok, hdst, slot=0, hkey="hT"):
                a = CA[slot][:, :ntok]
                b_ = CB[slot][:, :ntok]
                g2_ = G2s[:, :ntok]
                ak, bk2, g2k = f"ca{slot}", f"cb{slot}", "g2s"
                act(g2_, a, AF.Gelu_apprx_tanh, [ak], [g2k])
                tt("dve", hdst, g2_, b_, ALU.mult, [g2k, bk2], [hkey])

            def down_mm(fb, k, np_, tcol0):
                for n in range(2):
                    bank, bk = fb[n]
                    mm(bank[:np_, 0:512], hT[:, k, tcol0:tcol0 + np_], Wdn[:, k, n * 512:(n + 1) * 512], k == 0, k == 23,
                       [f"hT{k}", "wdn"], [bk])

            def down_tail(fb, xt, xk, np_, out_rows):
                for n in range(2):
                    bank, bk = fb[n]
                    act(tmpf[:np_, n * 512:(n + 1) * 512], bank[:np_, 0:512], AF.Square, [bk], ["tmpf", f"fs{n}"],
                        accum=stf[:np_, 4 + n:5 + n])
                tt("dve", stf[:np_, 6:7], stf[:np_, 4:5], stf[:np_, 5:6], ALU.add, ["fs0", "fs1"], ["fs"])
                ts("dve", stf[:np_, 7:8], stf[:np_, 6:7], 1.0 / D, EPS, ALU.mult, ALU.add, ["fs"], ["fm"])
                tt("pool", stf[:np_, 8:9], stf[:np_, 7:8], CN05[:np_, 0:1], ALU.pow, ["fm", "cn05"], ["fr"])
                for n in range(2):
                    bank, bk = fb[n]
                    act(tmpf[:np_, n * 512:(n + 1) * 512], bank[:np_, 0:512], AF.Copy, [bk, "fr"], ["tmpf"], scale=stf[:np_, 8:9])
                    unpin(bk)
                tt("pool", tmpf[:np_], tmpf[:np_], GPM[:np_], ALU.mult, ["tmpf", "gpm"], ["tmpf"])
                tt("dve", xt[:np_], xt[:np_], tmpf[:np_], ALU.add, [xk, "tmpf"], [xk])
                S.dma("sp", out_rows, xt[:np_], xk, reads=[xk])

            def tail_A(fb, blk):
                for n in range(2):
                    bank, bk = fb[n]
                    act(tmpf[:, n * 512:(n + 1) * 512], bank[:, 0:512], AF.Square, [bk], ["tmpf", f"fsq{blk}{n}"],
                        accum=stf[:, 4 + 2 * blk + n:5 + 2 * blk + n])

            def tail_B(blk):
                tt("dve", stf[:, 8 + blk:9 + blk], stf[:, 4 + 2 * blk:5 + 2 * blk], stf[:, 5 + 2 * blk:6 + 2 * blk], ALU.add,
                   [f"fsq{blk}0", f"fsq{blk}1"], [f"fsum{blk}"])
                ts("dve", stf[:, 10 + blk:11 + blk], stf[:, 8 + blk:9 + blk], 1.0 / D, EPS, ALU.mult, ALU.add,
                   [f"fsum{blk}"], [f"fms{blk}"])
                tt("pool", stf[:, 12 + blk:13 + blk], stf[:, 10 + blk:11 + blk], CN05[:, 0:1], ALU.pow,
                   [f"fms{blk}", "cn05"], [f"frs{blk}"])

            def tail_C(fb, blk, xt, xk, out_rows):
                for n in range(2):
                    bank, bk = fb[n]
                    stt(bank[:, 0:512], bank[:, 0:512], stf[:, 12 + blk:13 + blk], GPM[:, n * 512:(n + 1) * 512],
                        ALU.mult, ALU.mult, [bk, f"frs{blk}", "gpm"], [bk])
                    tt("dve", xt[:, n * 512:(n + 1) * 512], xt[:, n * 512:(n + 1) * 512], bank[:, 0:512], ALU.add, [xk, bk], [xk])
                    unpin(bk)
                S.dma("sp", out_rows, xt, xk, reads=[xk])

            def down_post(xt, xk, np_, tcol0, out_rows, okey_sem):
                fb = [ps(pin=True), ps(pin=True)]
                for k in range(24):
                    down_mm(fb, k, np_, tcol0)
                down_tail(fb, xt, xk, np_, out_rows)

            NG = 8

            def load_group(g):
                for i in range(2):
                    xi = (g % 2) * 2 + i
                    m = 1 + g * 2 + i
                    S.dma("sp", X1[xi], x1s[m * 128:(m + 1) * 128, :], f"x1_{xi}", writes=[f"x1_{xi}"])

            load_group(0)

            def norm_blk(g, i, part, trans):
                xi = (g % 2) * 2 + i
                norm_T2(X1[xi], f"x1_{xi}", 128, XN2[g % 2][:, :, i * 128:(i + 1) * 128], f"xn2T{g % 2}", part=part, trans=trans)

            norm_blk(0, 0, True, True)
            norm_blk(0, 1, True, True)
            LAG = 3
            pending = None
            for g in range(NG):
                if g == 0:
                    load_group(1)
                xn = XN2[g % 2]
                xnk = f"xn2T{g % 2}"
                fbs = None
                for j in range(24):
                    banks = []
                    for tix in (j, 24 + j):
                        bank, bk = ps()
                        for k in range(8):
                            mm(bank[:, 0:256], Wup[:, k, tix * 128:(tix + 1) * 128], xn[:, k, 0:256], k == 0, k == 7,
                               ["wup", xnk], [bk])
                        banks.append((bank, bk))
                    conv_gelu(j, 256, banks[0][0], banks[1][0], banks[0][1], banks[1][1], hT[:, j, 0:256], "defer", j % 2)
                    if j >= 1:
                        gelu_mul(256, hT[:, j - 1, 0:256], (j - 1) % 2, f"hT{j - 1}")
                    if pending is not None and j == 0:
                        tail_B(0)
                        tail_B(1)
                    if pending is not None and j == 1:
                        pf, pg = pending
                        for i in range(2):
                            xi = (pg % 2) * 2 + i
                            row0 = (pg * 2 + i) * 128
                            tail_C(pf[i], i, X1[xi], f"x1_{xi}", y_o[row0:row0 + 128, :])
                        pending = None
                        if g + 1 < NG:
                            load_group(g + 1)
                    if g + 1 < NG:
                        if j == 5:
                            norm_blk(g + 1, 0, True, False)
                        if j == 8:
                            norm_blk(g + 1, 0, False, True)
                        if j == 11:
                            norm_blk(g + 1, 1, True, False)
                        if j == 14:
                            norm_blk(g + 1, 1, False, True)
                    if j - LAG == 0:
                        fbs = [[ps(pin=True), ps(pin=True)] for _ in range(2)]
                    if j - LAG >= 0:
                        for i in range(2):
                            down_mm(fbs[i], j - LAG, 128, i * 128)
                gelu_mul(256, hT[:, 23, 0:256], 23 % 2, "hT23")
                for k in range(24 - LAG, 24):
                    for i in range(2):
                        down_mm(fbs[i], k, 128, i * 128)
                tail_A(fbs[0], 0)
                tail_A(fbs[1], 1)
                pending = (fbs, g)
            tail_B(0)
            tail_B(1)
            pf, pg = pending
            for i in range(2):
                xi = (pg % 2) * 2 + i
                row0 = (pg * 2 + i) * 128
                tail_C(pf[i], i, X1[xi], f"x1_{xi}", y_o[row0:row0 + 128, :])
            CP2 = tmpf
            for q4 in range(12):
                bank, bk = ps()
                for i in range(4):
                    t = q4 * 4 + i
                    tp(bank[0:2, i * 128:(i + 1) * 128], Uh[:, t, :], idf, ["uh", "cst"], [bk], inc=(i == 3))
                cp("act", g1[0:2, 0:256], bank[0:2, 0:256], [bk], ["ca1"])
                cp("act", g2[0:2, 0:256], bank[0:2, 256:512], [bk], ["g2s"])
                S.dma("sp", cp_o[:, q4 * 512:q4 * 512 + 256], g1[0:2, 0:256], "ca1", reads=["ca1"])
                S.dma("sp", cp_o[:, q4 * 512 + 256:q4 * 512 + 512], g2[0:2, 0:256], "g2s", reads=["g2s"])

            if SCUT != 0:
                return
            S.barrier()
            NP = 64
            XS = X1R[:, 0:1024]
            CTX = X1R[:, 1024:2560].rearrange("p (a b) -> p a b", a=48)
            US = X1R[:, 2560:4096].rearrange("p (a b) -> p a b", a=48)
            S.dma("sp", XS[:NP], x1s[NMAIN * 128:NMAIN * 128 + 64, :], "x1_0", writes=["x1_0"])
            for q4 in range(12):
                sc = tmpf[:32, (q4 % 2) * 512:(q4 % 2) * 512 + 512]
                sck = f"sc{q4 % 2}"
                S.dma("sp", sc, scv[:, q4 * 512:(q4 + 1) * 512], sck, writes=[sck])
                bank, bk = ps()
                for i in range(4):
                    tp(bank[:, i * 32:(i + 1) * 32], sc[:, i * 128:(i + 1) * 128], idf[:32, :32], [sck, "cst"], [bk], inc=(i == 3))
                cp("act", CTX[:, q4 * 4:(q4 + 1) * 4, :], bank[:, 0:128].rearrange("p (a b) -> p a b", a=4), [bk], ["ctx"])
            norm_T2(XS, "x1_0", NP, xn2T[:, :, 0:NP], "xn2T")
            for j in range(24):
                slot = j % 2
                banks = []
                for tix in (j, 24 + j):
                    bank, bk = ps()
                    for k in range(8):
                        mm(bank[:, 0:NP], Wup[:, k, tix * 128:(tix + 1) * 128], xn2T[:, k, 0:NP], k == 0, k == 7,
                           ["wup", "xn2T"], [bk])
                    banks.append((bank, bk))
                for (U, (bank_, bk_), tix, cdst, ckey) in ((Ua[slot], banks[0], j, CA[slot], f"ca{slot}"), (Ub[slot], banks[1], 24 + j, CB[slot], f"cb{slot}")):
                    uk = f"U{ckey}"
                    Ue = U[:, 0:96].rearrange("p (b s) -> p b s", b=16)
                    cv_ = cdst[:, 0:64].rearrange("p (b t) -> p b t", b=16)
                    cp("pool", Ue[:, :, 0:2], CTX[:, tix, :].rearrange("p (b c) -> p b c", b=16), ["ctx"], [uk])
                    cp("act", Ue[:, :, 2:6], bank_[:, 0:64].rearrange("p (b t) -> p b t", b=16), [bk_], [uk])
                    cp("pool", US[:, tix, :].rearrange("p (b c) -> p b c", b=16), Ue[:, :, 4:6], [uk], ["us"])
                    ts("dve", cv_, Ue[:, :, 2:6], CW[:, tix, 2:3], CW[:, tix, 3:4], ALU.mult, ALU.add, [uk, "cw"], [ckey])
                    stt(cv_, Ue[:, :, 1:5], CW[:, tix, 1:2], cv_, ALU.mult, ALU.add, [uk, "cw", ckey], [ckey])
                    stt(cv_, Ue[:, :, 0:4], CW[:, tix, 0:1], cv_, ALU.mult, ALU.add, [uk, "cw", ckey], [ckey])
                if j >= 1:
                    gelu_mul(NP, hT[:, j - 1, 0:NP], (j - 1) % 2, f"hT{j - 1}")
            gelu_mul(NP, hT[:, 23, 0:NP], 23 % 2, "hT23")
            down_post(XS, "x1_0", NP, 0, ys_o, None)
            for q4 in range(12):
                bank, bk = ps()
                for i in range(4):
                    tp(bank[0:32, i * 128:(i + 1) * 128], US[:, q4 * 4 + i, :], idf, ["us", "cst"], [bk], inc=(i == 3))
                stg = CA[1] if q4 % 2 == 0 else G2s
                sgk = "ca1" if q4 % 2 == 0 else "g2s"
                cp("act", stg[0:32, 0:256], bank[0:32, 0:256], [bk], [sgk])
                S.dma("sp", cs_o[:, q4 * 512:q4 * 512 + 256], stg[0:32, 0:256], sgk, reads=[sgk])
                stg2 = CA[0] if q4 % 2 == 0 else CB[0]
                sgk2 = "ca0" if q4 % 2 == 0 else "cb0"
                cp("act", stg2[0:32, 0:256], bank[0:32, 256:512], [bk], [sgk2])
                S.dma("sp", cs_o[:, q4 * 512 + 256:q4 * 512 + 512], stg2[0:32, 0:256], sgk2, reads=[sgk2])

        ffn_phase()

        S.finish()
    return nc


def _tables(pos):
    pos = np.asarray(pos)
    T = pos.shape[0]
    pf = pos.astype(np.float32)
    out = np.zeros((T, NTAB), np.float32)
    ang = (1.0 / (np.float32(10000.0) ** np.linspace(0.0, 1.0, 64, dtype=np.float32))).astype(np.float32)
    ang = np.repeat(ang, 2)
    th = (pf[:, None] * ang[None, :]).astype(np.float32)
    c = np.cos(th.astype(np.float64))
    s = np.sin(th.astype(np.float64))
    sgn = np.tile(np.array([-1.0, 1.0]), 64)[None, :]
    out[:, 0:128] = c
    out[:, 128:256] = s * sgn
    sc = 128.0 ** -0.5
    out[:, 256:384] = c * sc
    out[:, 384:512] = s * sgn * sc
    half = 8
    inv = (1.0 / (np.float32(500000.0) ** (np.arange(half, dtype=np.float32) / np.float32(half)))).astype(np.float32)
    a = (pf[:, None] * inv[None, :]).astype(np.float32)
    ca = np.cos(a.astype(np.float64))
    sa = np.sin(a.astype(np.float64))
    out[:, 512:520] = ca
    out[:, 520:528] = ca
    out[:, 528:536] = -sa
    out[:, 536:544] = sa
    return out


def _consts(is_b):
    cst = np.zeros((128, 1280), np.float64)
    i = np.arange(128)
    for h in range(4):
        g = GAM[h]
        cst[:, h * 128:(h + 1) * 128] = (g ** (i + 1.0))[None, :]
        cst[:, 512 + h * 128:512 + (h + 1) * 128] = (g ** (-(i + 1.0)))[None, :]
        cst[:, 1024 + h] = g ** (127.0 - i)
    cst[:, 1152:1280] = np.eye(128)
    cbf = np.zeros((128, 1024), np.float32)
    cbf[:, 0:128] = np.eye(128)
    cbf[:, 128:192] = 1.0
    k = np.arange(128)[:, None]
    q = np.arange(128)[None, :]
    cbf[:, 256:384] = (k <= q)
    cbf[:, 384:512] = (k > q)
    cbf[:, 512:640] = (k > q) if is_b else 0.0
    cbf[:, 640:768] = 1.0
    cbf[:, 768:772] = (np.arange(128)[:, None] > np.arange(4)[None, :])
    kk = np.arange(64)[:, None]
    qq = np.arange(64)[None, :]
    cbf[0:64, 832:896] = ((kk // 4) == (qq // 4)) & ((kk % 4) <= (qq % 4))
    cs2 = np.zeros((128, 544), np.float64)
    tok = np.arange(64)
    for h in range(4):
        g = GAM[h]
        cs2[:, h * 64:(h + 1) * 64] = (g ** ((tok % 4) + 1.0))[None, :]
        dm = np.where(((kk // 4) == (qq // 4)) & ((qq % 4) >= (kk % 4)), g ** ((qq % 4) - (kk % 4)).astype(np.float64), 0.0)
        cs2[0:64, 256 + h * 64:256 + (h + 1) * 64] = dm
        cs2[0:64, 512 + h] = g ** (3.0 - (tok % 4))
    cs2[0:64, 516:532] = ((tok[:, None] // 4) == np.arange(16)[None, :])
    return cst.astype(np.float32), cbf.astype(ml_dtypes.bfloat16), cs2.astype(np.float32)


_NC_CACHE = {}


def kernel(x_prompt, x_sample, cache_k, cache_v, state_ret, state_conv,
           w_in, attn_sinks, w_a_proj, w_r_proj, w_o,
           g_pre_mix, g_post_mix, g_pre_ffn, g_post_ffn,
           w_up, conv_w, conv_b, w_down):
    f = lambda a: np.ascontiguousarray(np.asarray(a, dtype=np.float32))
    x_prompt = f(x_prompt); x_sample = f(x_sample)
    cache_k = f(cache_k); cache_v = f(cache_v); state_ret = f(state_ret); state_conv = f(state_conv)
    if "nc" not in _NC_CACHE:
        _NC_CACHE["nc"] = build_program()
    nc = _NC_CACHE["nc"]

    wa = f(w_a_proj)[0].reshape(8, 64, D)
    wa_l = np.zeros((128, 4, D), np.float32)
    for j in range(4):
        wa_l[0:64, j] = wa[j]
        wa_l[64:128, j] = wa[4 + j]
    gvT = np.concatenate([f(g_pre_mix)[0].reshape(8, 128).T, f(g_pre_ffn)[0].reshape(8, 128).T], axis=1)
    gpost = np.stack([f(g_post_mix)[0], f(g_post_ffn)[0]])
    cw = f(conv_w)[0]
    cb = f(conv_b)[0]
    cwT = np.zeros((128, 48, 4), np.float32)
    for t in range(48):
        cwT[:, t, 0:3] = cw[:, t * 128:(t + 1) * 128].T
        cwT[:, t, 3] = cb[t * 128:(t + 1) * 128]
    cwT = cwT.reshape(128, 192)
    sk = f(attn_sinks)[0]
    sinks = np.zeros((128, 4), np.float32)
    sinks[0:64, :] = sk[0:4][None, :]
    sinks[64:128, :] = sk[4:8][None, :]

    in_maps = []
    for c in range(8):
        b, half = c // 2, c % 2
        xe = np.zeros((32 * 128, D), np.float32)
        if half == 0:
            xe[16 * 128:] = x_prompt[b, 0:2048]
            pos = np.concatenate([np.arange(-2048, 2048), 16384 + np.tile(np.arange(4), 16), np.zeros(64, np.int64)])
        else:
            xe[:] = x_prompt[b]
            pos = np.concatenate([np.arange(0, 4096), 16384 + np.tile(np.arange(4), 16), np.zeros(64, np.int64)])
        cst, cbf, cs2 = _consts(half == 1)
        in_maps.append({
            "xe": xe,
            "xsm": x_sample[16 * c:16 * (c + 1)].reshape(64, D),
            "ck": cache_k[0, 16 * c:16 * (c + 1)].reshape(16, 128, 128),
            "cv": cache_v[0, 16 * c:16 * (c + 1)].reshape(16, 128, 128),
            "sr": state_ret[0, 16 * c:16 * (c + 1)],
            "scv": state_conv[0, 16 * c:16 * (c + 1)].reshape(32, F2),
            "w_in": f(w_in)[0], "w_a": wa_l, "w_r": f(w_r_proj)[0], "w_o": f(w_o)[0],
            "w_up": f(w_up)[0], "w_dn": f(w_down)[0],
            "gvT": np.ascontiguousarray(gvT), "gpost": gpost, "cwT": cwT, "sinks": sinks,
            "tabs": _tables(pos), "cst": cst, "cbf": cbf, "cs2": cs2,
        })
    res = run_bass_kernel_spmd(nc, in_maps, core_ids=list(range(8)))
    R = res.results
    y = np.zeros((4, 4096, D), np.float32)
    ys = np.zeros((128, 4, D), np.float32)
    kp = np.zeros((1, 4, 128, 2, 64), np.float32)
    vp = np.zeros((1, 4, 128, 2, 64), np.float32)
    rp = np.zeros((1, 4, 4, 128, 256), np.float32)
    cpo = np.zeros((1, 4, 2, F2), np.float32)
    kso = np.zeros((1, 128, 128, 2, 64), np.float32)
    vso = np.zeros((1, 128, 128, 2, 64), np.float32)
    rso = np.zeros((1, 128, 4, 128, 256), np.float32)
    cso = np.zeros((1, 128, 2, F2), np.float32)
    for c in range(8):
        b, half = c // 2, c % 2
        r = R[c]
        y[b, half * 2048:(half + 1) * 2048] = r["y_o"]
        ys[16 * c:16 * (c + 1)] = r["ys_o"].reshape(16, 4, D)
        if half == 1:
            kp[0, b] = r["kp_o"].reshape(128, 2, 64)
            vp[0, b] = r["vp_o"].reshape(128, 2, 64)
            rp[0, b] = r["rp_o"]
            cpo[0, b] = r["cp_o"]
        kso[0, 16 * c:16 * (c + 1)] = r["ks_o"].reshape(16, 128, 2, 64)
        vso[0, 16 * c:16 * (c + 1)] = r["vs_o"].reshape(16, 128, 2, 64)
        rso[0, 16 * c:16 * (c + 1)] = r["rs_o"]
        cso[0, 16 * c:16 * (c + 1)] = r["cs_o"].reshape(16, 2, F2)
    return (y, ys, kp, vp, rp, cpo, kso, vso, rso, cso)
```

```python
import numpy as np
from contextlib import ExitStack
import ml_dtypes
import concourse.bass as bass
import concourse.mybir as mybir
from concourse.bass_utils import run_bass_kernel_spmd

F32 = mybir.dt.float32
BF16 = mybir.dt.bfloat16
U8 = mybir.dt.uint8
AF = mybir.ActivationFunctionType
ALU = mybir.AluOpType

D = 1024
DIN = 5888
F2 = 6144
DFF = 3072
EPS = 1e-6
NPRE = 15
NMAIN = 17
NTAB = 544
DEBUG_X1 = False
STAGE = 99
CUT = 0
SCUT = 0
GAM = [1.0 - 2.0 ** (-5 - h) for h in range(4)]


class Sched:
    def __init__(self, nc, stack):
        self.nc = nc
        self.stack = stack
        self.eng = {"pe": nc.tensor, "act": nc.scalar, "dve": nc.vector,
                    "pool": nc.gpsimd, "sp": nc.sync}
        self.queues = {e: [] for e in self.eng}
        self.sems = {}
        self.cnt = {}
        self.waited = {e: {} for e in self.eng}
        self.last_w = {}
        self.readers = {}

    def sem(self, name):
        if name not in self.sems:
            self.sems[name] = self.stack.enter_context(
                self.nc.semaphore("s_" + name.replace(":", "_")))
            self.cnt[name] = 0
        return self.sems[name]

    def _deps(self, reads, writes):
        deps = set()
        for k in reads:
            t = self.last_w.get(k)
            if t is not None:
                deps.add(t)
        for k in writes:
            t = self.last_w.get(k)
            if t is not None:
                deps.add(t)
            for t in self.readers.get(k, ()):
                deps.add(t)
        return deps

    def _commit(self, token, reads, writes):
        for k in reads:
            self.readers.setdefault(k, set()).add(token)
        for k in writes:
            self.last_w[k] = token
            self.readers[k] = set()

    def _waits(self, e, deps):
        best = {}
        for (s, v) in deps:
            if e == "pe" and s == "pe":
                continue
            if best.get(s, 0) < v:
                best[s] = v
        w = []
        for s, v in best.items():
            if self.waited[e].get(s, 0) < v:
                self.waited[e][s] = v
                w.append((s, v))
        return w

    def op(self, e, fn, reads=(), writes=(), inc=True):
        self.sem(e)
        deps = self._deps(reads, writes)
        waits = self._waits(e, deps)
        if inc:
            self.cnt[e] += 1
            token = (e, self.cnt[e])
        else:
            token = (e, self.cnt[e] + 1)
        self.queues[e].append((waits, fn, (e, 1) if inc else None))
        self._commit(token, reads, writes)
        return token

    def dma(self, q, out, in_, semkey, reads=(), writes=(), **kw):
        s = "d:" + semkey
        self.sem(s)
        deps = self._deps(reads, writes)
        waits = self._waits(q, deps)
        self.cnt[s] += 16
        token = (s, self.cnt[s])
        self.queues[q].append((waits, lambda eng: eng.dma_start(out=out, in_=in_, **kw), (s, 16)))
        self._commit(token, reads, writes)
        return token

    def dma_multi(self, q, items, semkey, **kw):
        s = "d:" + semkey
        self.sem(s)
        keys = []
        for (out, in_, key) in items:
            deps = self._deps((), [key])
            waits = self._waits(q, deps)
            self.cnt[s] += 16
            self.queues[q].append((waits, (lambda o, i: (lambda eng: eng.dma_start(out=o, in_=i, **kw)))(out, in_), (s, 16)))
            keys.append(key)
        token = (s, self.cnt[s])
        for key in keys:
            self.last_w[key] = token
            self.readers[key] = set()
        return token

    def barrier(self):
        for e in ("pe", "act", "dve", "pool"):
            self.sem(e)
        tot = dict(self.cnt)
        for e in self.eng:
            waits = []
            for s, v in tot.items():
                if v > 0 and self.waited[e].get(s, 0) < v and s != e:
                    self.waited[e][s] = v
                    waits.append((s, v))
            self.queues[e].append((waits, None, None))
        self.last_w = {}
        self.readers = {}

    def finish(self):
        nc = self.nc
        final = []
        for s, v in self.cnt.items():
            if v > 0 and self.waited["sp"].get(s, 0) < v:
                final.append((s, v))
        with nc.Block() as block:
            def emit(e):
                def body(eng):
                    for waits, fn, inc in self.queues[e]:
                        for (s, v) in waits:
                            eng.wait_ge(self.sems[s], v)
                        if fn is None:
                            continue
                        ins = fn(eng)
                        if inc is not None:
                            ins.then_inc(self.sems[inc[0]], inc[1])
                    if e == "sp":
                        for (s, v) in final:
                            eng.wait_ge(self.sems[s], v)
                return body
            block.tensor(emit("pe"))
            block.scalar(emit("act"))
            block.vector(emit("dve"))
            block.gpsimd(emit("pool"))
            block.sync(emit("sp"))


class Arena:
    def __init__(self, ap_u8, size):
        self.a = ap_u8
        self.size = size
        self.off = 0

    def alloc(self, parts, free, dt):
        esz = 4 if dt == F32 else 2
        n = 1
        for f in free:
            n *= f
        nb = n * esz
        nb_al = (nb + 63) // 64 * 64
        assert self.off + nb_al <= self.size, f"SBUF arena overflow {self.off}+{nb_al}>{self.size}"
        v = self.a[0:parts, self.off:self.off + nb].bitcast(dt)
        self.off += nb_al
        if len(free) == 2:
            v = v.rearrange("p (a b) -> p a b", a=free[0])
        elif len(free) == 3:
            v = v.rearrange("p (a b c) -> p a b c", a=free[0], b=free[1])
        return v


def build_program():
    nc = bass.Bass("TRN2", target_bir_lowering=False)

    def din(name, shape, dt=F32):
        return nc.dram_tensor(name, list(shape), dt, kind="ExternalInput").ap()

    def dout(name, shape):
        return nc.dram_tensor(name, list(shape), F32, kind="ExternalOutput").ap()

    xe = din("xe", [32 * 128, D])
    xsm = din("xsm", [64, D])
    ck = din("ck", [16, 128, 128])
    cv = din("cv", [16, 128, 128])
    sr = din("sr", [16, 4, 128, 256])
    scv = din("scv", [32, F2])
    w_in = din("w_in", [D, DIN])
    w_a = din("w_a", [128, 4, D])
    w_r = din("w_r", [D, D])
    w_o = din("w_o", [D, D])
    w_up = din("w_up", [D, F2])
    w_dn = din("w_dn", [DFF, D])
    gvT = din("gvT", [128, 16])
    gpost = din("gpost", [2, D])
    cwT = din("cwT", [128, 48 * 4])
    sinks = din("sinks", [128, 4])
    tabs = din("tabs", [33 * 128, NTAB])
    cst = din("cst", [128, 1280])
    cbf = din("cbf", [128, 1024], BF16)
    cs2 = din("cs2", [128, 544])

    y_o = dout("y_o", [16 * 128, D])
    ys_o = dout("ys_o", [64, D])
    kp_o = dout("kp_o", [128, 128])
    vp_o = dout("vp_o", [128, 128])
    rp_o = dout("rp_o", [4, 128, 256])
    cp_o = dout("cp_o", [2, F2])
    ks_o = dout("ks_o", [16, 128, 128])
    vs_o = dout("vs_o", [16, 128, 128])
    rs_o = dout("rs_o", [16, 4, 128, 256])
    cs_o = dout("cs_o", [32, F2])

    x1s = nc.dram_tensor("x1s", [NMAIN * 128 + 64, D], F32, kind="Internal").ap()
    wupbf = nc.dram_tensor("wupbf", [D, F2], BF16, kind="Internal").ap()
    wdnbf = nc.dram_tensor("wdnbf", [DFF, D], BF16, kind="Internal").ap()

    with ExitStack() as st:
        S = Sched(nc, st)
        ARENA = 212800
        arena_t = st.enter_context(nc.sbuf_tensor("arena", [128, ARENA], U8))
        A = Arena(arena_t, ARENA)
        psb = [st.enter_context(nc.psum_tensor(f"ps{i}", [128, 512], F32)) for i in range(8)]
        psctr = [0]

        pinned = set()

        def ps(pin=False):
            while True:
                i = psctr[0] % 8
                psctr[0] += 1
                if i not in pinned:
                    break
            if pin:
                pinned.add(i)
            return psb[i], f"ps{i}"

        def unpin(bk):
            pinned.discard(int(bk[2:]))

        def act(out, in_, func, reads, writes, scale=1.0, accum=None):
            if accum is None:
                S.op("act", lambda e: e.activation(out=out, in_=in_, func=func, scale=scale), reads, writes)
            else:
                S.op("act", lambda e: e.activation(out=out, in_=in_, func=func, scale=scale, accum_out=accum), reads, writes)

        def tt(eng, out, in0, in1, op, reads, writes):
            S.op(eng, lambda e: e.tensor_tensor(out=out, in0=in0, in1=in1, op=op), reads, writes)

        def ts(eng, out, in0, s1, s2, op0, op1, reads, writes):
            S.op(eng, lambda e: e.tensor_scalar(out=out, in0=in0, scalar1=s1, scalar2=s2, op0=op0, op1=op1), reads, writes)

        def stt(out, in0, scalar, in1, op0, op1, reads, writes):
            S.op("dve", lambda e: e.scalar_tensor_tensor(out=out, in0=in0, scalar=scalar, in1=in1, op0=op0, op1=op1), reads, writes)

        def cp(eng, out, in_, reads, writes):
            if eng == "act":
                S.op("act", lambda e: e.copy(out=out, in_=in_), reads, writes)
            else:
                S.op(eng, lambda e: e.tensor_copy(out=out, in_=in_), reads, writes)

        def mm(out, lhsT, rhs, start, stop, reads, writes, inc=None):
            S.op("pe", lambda e: e.matmul(out, lhsT, rhs, start=start, stop=stop), reads, writes,
                 inc=(stop if inc is None else inc))

        def tp(out, in_, ident, reads, writes, inc):
            S.op("pe", lambda e: e.transpose(out, in_, ident), reads, writes, inc=inc)

        def bc(ap, shape, axis):
            return ap.unsqueeze(axis).to_broadcast(shape)

        CST = A.alloc(128, [1280], F32)
        CBF = A.alloc(128, [1024], BF16)
        GVT = A.alloc(128, [16], F32)
        GPM = A.alloc(128, [D], F32)
        SNK = A.alloc(128, [4], F32)
        ESK = A.alloc(128, [4], F32)
        CN05 = A.alloc(128, [4], F32)
        S.dma("sp", CST, cst, "cst", writes=["cst"])
        S.dma("sp", CBF, cbf, "cbf", writes=["cbf"])
        S.dma("sp", GVT, gvT, "gvt", writes=["gvt"])
        S.dma("sp", GPM, gpost[0:1, :].to_broadcast([128, D]) if False else gpost[0].partition_broadcast(128), "gpm", writes=["gpm"])
        S.dma("sp", SNK, sinks, "snk", writes=["snk"])
        act(ESK, SNK, AF.Exp, ["snk"], ["esk"])
        S.op("pool", lambda e: e.memset(CN05, -0.5), (), ["cn05"])
        EPSC = A.alloc(128, [4], F32)
        S.op("dve", lambda e: e.memset(EPSC, EPS), (), ["epsc"])
        dqT = CST[:, 0:512]
        dkT = CST[:, 512:1024]
        ktok = CST[:, 1024:1028]
        idf = CST[:, 1152:1280]
        ident = CBF[:, 0:128]
        ones64 = CBF[:, 128:192]
        mown = CBF[:, 256:384]
        mprev = CBF[:, 384:512]
        mprev1 = CBF[:, 512:640]

        mark_w = A.off
        Win = A.alloc(128, [8, DIN], BF16)
        X2buf = A.alloc(128, [D], F32)
        Wa = A.alloc(128, [4, D], BF16)
        Wr = A.alloc(128, [8, D], BF16)
        Wo = A.alloc(128, [8, D], BF16)
        itemsA, itemsB = [], []
        for k in range(8):
            for (c0, c1) in ((512, 768), (1280, 2048), (2048, 2816)):
                itemsA.append((Win[:, k, c0:c1], w_in[k * 128:(k + 1) * 128, c0:c1], "winA"))
        for k in range(8):
            for (c0, c1) in ((0, 512), (768, 1280), (2816, 3840), (3840, 4864), (4864, 5888)):
                itemsB.append((Win[:, k, c0:c1], w_in[k * 128:(k + 1) * 128, c0:c1], "winB"))
        S.dma_multi("pool", itemsA, "winA")
        S.dma_multi("pool", itemsB, "winB")
        S.dma_multi("pool", [(Wa[:, j, :], w_a[:, j, :], "wa") for j in range(4)], "wa")
        S.dma_multi("pool", [(Wr[:, k, :], w_r[k * 128:(k + 1) * 128, :], "wr") for k in range(8)], "wr")
        S.dma_multi("pool", [(Wo[:, k, :], w_o[k * 128:(k + 1) * 128, :], "wo") for k in range(8)], "wo")
        mark_mix = A.off
        win_b_pending = [True]

        X = [A.alloc(128, [D], F32) for _ in range(2)] + [X2buf]
        TAB = [A.alloc(128, [NTAB], F32) for _ in range(2)]
        xs = A.alloc(128, [D], BF16)
        xnT = A.alloc(128, [8, 128], BF16)
        qa = A.alloc(128, [512], BF16)
        qaT = A.alloc(128, [512], BF16)
        ka32 = A.alloc(128, [128], F32)
        kabf = A.alloc(128, [128], BF16)
        kT = [A.alloc(128, [128], BF16) for _ in range(2)]
        va32 = A.alloc(128, [128], F32)
        vbf = [A.alloc(128, [128], BF16) for _ in range(2)]
        t1 = A.alloc(128, [512], F32)
        t2 = A.alloc(128, [512], F32)
        qr = A.alloc(128, [512], BF16)
        kr = A.alloc(128, [512], BF16)
        ktl = A.alloc(128, [512], BF16)
        qrT = A.alloc(128, [512], BF16)
        krT = A.alloc(128, [512], BF16)
        vr = A.alloc(128, [1024], BF16)
        sg = A.alloc(128, [1024], BF16)
        PT = [A.alloc(128, [512], BF16) for _ in range(4)]
        oaT = A.alloc(128, [512], BF16)
        AT = A.alloc(128, [512], BF16)
        S32 = A.alloc(128, [1024], F32)
        Sbf = A.alloc(128, [1024], BF16)
        og = A.alloc(128, [1024], BF16)
        ogT = A.alloc(128, [8, 128], BF16)
        tha = A.alloc(128, [1024], BF16)
        thr = A.alloc(128, [1024], BF16)
        st8 = A.alloc(128, [32], F32)
        tA1 = A.alloc(128, [8, 16], F32)
        tA2 = A.alloc(128, [8, 16], F32)

        def load_x(slot, src_rows, tab_rows, np_=128):
            S.dma("sp", X[slot][:np_], src_rows, f"x{slot}", writes=[f"x{slot}"])
            S.dma("sp", TAB[slot][:np_], tab_rows, f"tab{slot}", writes=[f"tab{slot}"])

        def norm_part(slot, np_, a=True, b=True, nopool=False):
            xb = X[slot]
            xk = f"x{slot}"
            if a and nopool:
                act(xs[:np_], xb[:np_], AF.Square, [xk], ["xs", "ssq"], accum=st8[:np_, 0:1])
                S.op("act", lambda e: e.activation(out=st8[:np_, 1:2], in_=st8[:np_, 0:1], func=AF.Sqrt, scale=1.0 / D, bias=EPSC[:np_, 0:1]),
                     ["ssq", "epsc"], ["ms"])
                S.op("dve", lambda e: e.reciprocal(out=st8[:np_, 2:3], in_=st8[:np_, 1:2]), ["ms"], ["rstd"])
            elif a:
                act(xs[:np_], xb[:np_], AF.Square, [xk], ["xs", "ssq"], accum=st8[:np_, 0:1])
                ts("dve", st8[:np_, 1:2], st8[:np_, 0:1], 1.0 / D, EPS, ALU.mult, ALU.add, ["ssq"], ["ms"])
                tt("pool", st8[:np_, 2:3], st8[:np_, 1:2], CN05[:np_, 0:1], ALU.pow, ["ms", "cn05"], ["rstd"])
            if b:
                act(xs[:np_], xb[:np_], AF.Copy, [xk, "rstd"], ["xs"], scale=st8[:np_, 2:3])

        def norm_T(slot, np_, gcol0, wkeys=("gvt",), do_part=True):
            if do_part:
                norm_part(slot, np_)
            bank, bk = ps()
            pb = bank[:].bitcast(BF16)
            for k in range(8):
                tp(pb[:, k * 128:k * 128 + np_], xs[:np_, k * 128:(k + 1) * 128], ident[:np_, :np_],
                   ["xs", "cbf"], [bk], inc=(k == 7))
            pv = pb.rearrange("p (k t) -> p k t", k=8)[:, :, :np_]
            tt("dve", xnT[:, :, :np_], pv, bc(GVT[:, gcol0:gcol0 + 8], [128, 8, np_], 2), ALU.mult,
               [bk, "gvt"], ["xnT"])

        def inproj(c0, n, np_):
            bank, bk = ps()
            for k in range(8):
                mm(bank[:np_, :n], xnT[:, k, :np_], Win[:, k, c0:c0 + n], k == 0, k == 7,
                   ["xnT", "winA" if (512 <= c0 < 768 or 1280 <= c0 < 2816) else "winB"], [bk])
            return bank, bk

        def rope_small(psv, nh, np_, tab, outv, bk, okey):
            cosA = tab[:np_, 512:528]
            ssinA = tab[:np_, 528:544]
            a1 = tA1[:np_, :nh, :]
            a2 = tA2[:np_, :nh, :]
            tk = [bk, "tabcur"]
            tt("dve", a1, psv[:, :, 0:16], bc(cosA, [np_, nh, 16], 1), ALU.mult, tk, ["tA1"])
            tt("dve", a2[:, :, 0:8], psv[:, :, 8:16], bc(ssinA[:, 0:8], [np_, nh, 8], 1), ALU.mult, tk, ["tA2"])
            tt("dve", a2[:, :, 8:16], psv[:, :, 0:8], bc(ssinA[:, 8:16], [np_, nh, 8], 1), ALU.mult, tk, ["tA2b"])
            tt("dve", outv, a1, a2, ALU.add, ["tA1", "tA2", "tA2b"], [okey])

        def rope_ret(bank, bk, np_, tab, tcol, outbf, okey):
            cos = tab[:np_, tcol:tcol + 128]
            ssin = tab[:np_, tcol + 128:tcol + 256].rearrange("p (i two) -> p i two", two=2)
            pv = bank[:np_, 0:512].rearrange("p (h d) -> p h d", h=4)
            pp = bank[:np_, 0:512].rearrange("p (h i two) -> p h i two", h=4, two=2)
            t1v = t1[:np_].rearrange("p (h d) -> p h d", h=4)
            t2p = t2[:np_].rearrange("p (h i two) -> p h i two", h=4, two=2)
            tk = [bk, "tabcur"]
            tt("dve", t1v, pv, bc(cos, [np_, 4, 128], 1), ALU.mult, tk, ["t1"])
            tt("dve", t2p[:, :, :, 0], pp[:, :, :, 1], bc(ssin[:, :, 0], [np_, 4, 64], 1), ALU.mult, tk, ["t2a"])
            tt("dve", t2p[:, :, :, 1], pp[:, :, :, 0], bc(ssin[:, :, 1], [np_, 4, 64], 1), ALU.mult, tk, ["t2b"])
            tt("dve", outbf[:np_], t1[:np_], t2[:np_], ALU.add, ["t1", "t2a", "t2b"], [okey])

        def kv_tiles(slot, tabslot, np_, ka_out=True):
            tab = TAB[tabslot]
            S.readers.setdefault("tabcur", set())
            bank, bk = inproj(512, 256, np_)
            cp("act", ka32[:np_], bank[:np_, 0:128], [bk], ["ka32"])
            rope_small(bank[:np_, 0:128].rearrange("p (h d) -> p h d", h=2), 2, np_, tab,
                       ka32[:np_].rearrange("p (h d) -> p h d", h=2)[:, :, 0:16], bk, "ka32")
            cp("act", va32[:np_], bank[:np_, 128:256], [bk], ["va32"])
            cp("dve", kabf[:np_], ka32[:np_], ["ka32"], ["kabf"])
            cp("act", vbf[slot][:np_], va32[:np_], ["va32"], [f"vbf{slot}"])

        def state_update(np_=128):
            sbanks = [ps(), ps()]
            for h in range(4):
                sb, sk = sbanks[h // 2]
                c = (h % 2) * 256
                mm(sb[:, c:c + 256], ktl[:np_, h * 128:(h + 1) * 128], vr[:np_, h * 256:(h + 1) * 256], True, True,
                   ["ktl", "vr"], [sk])
            for h in range(4):
                sb, sk = sbanks[h // 2]
                c = (h % 2) * 256
                stt(S32[:, h * 256:(h + 1) * 256], S32[:, h * 256:(h + 1) * 256], float(GAM[h] ** 128),
                    sb[:, c:c + 256], ALU.mult, ALU.add, [sk, "S32"], ["S32"])

        def kr_vr_tiles(tabslot, np_=128):
            tab = TAB[tabslot]
            bank, bk = inproj(1280, 512, np_)
            rope_ret(bank, bk, np_, tab, 256, kr, "kr")
            tt("pool", ktl[:np_].rearrange("p (h d) -> p h d", h=4), kr[:np_].rearrange("p (h d) -> p h d", h=4),
               bc(ktok[:np_], [np_, 4, 128], 2), ALU.mult, ["kr", "cst"], ["ktl"])
            for n in range(2):
                bank, bk = inproj(1792 + n * 512, 512, np_)
                cp("act", vr[:np_, n * 512:(n + 1) * 512], bank[:np_, 0:512], [bk], ["vr"])


        S.op("dve", lambda e: e.memset(S32, 0.0), (), ["S32"])

        def load_blk(slot, gb):
            S.dma("sp", X[slot], xe[gb * 128:(gb + 1) * 128, :], f"x{slot}", writes=[f"x{slot}"])
            S.dma("sp", TAB[slot], tabs[gb * 128:(gb + 1) * 128, :], f"tab{slot}", writes=[f"tab{slot}"])

        cur = {"tab": None}

        def pxi(p_):
            return (p_ + 1) % 3

        def pload_x(p_):
            S.dma("sp", X[pxi(p_)], xe[p_ * 128:(p_ + 1) * 128, :], f"x{pxi(p_)}", writes=[f"x{pxi(p_)}"])

        def pload_tab(p_):
            S.dma("sp", TAB[p_ % 2], tabs[p_ * 128:(p_ + 1) * 128, :], f"tab{p_ % 2}", writes=[f"tab{p_ % 2}"])

        pload_x(0)
        pload_tab(0)
        pload_x(1)
        norm_part(pxi(0), 128, nopool=True)
        norm_T(pxi(0), 128, 0, do_part=False)
        for p in range(NPRE):
            slot = p % 2
            pload_tab(p + 1)
            if p + 2 <= NPRE:
                pload_x(p + 2)
            TK = f"tab{slot}"
            if p + 1 < NPRE:
                norm_part(pxi(p + 1), 128, nopool=True)
            tab = TAB[slot]
            bank, bk = inproj(1280, 512, 128)
            cos = tab[:, 256:384]
            rope_ret_keys = [bk, TK]
            ssin = tab[:, 384:512].rearrange("p (i two) -> p i two", two=2)
            pv = bank[:, 0:512].rearrange("p (h d) -> p h d", h=4)
            pp = bank[:, 0:512].rearrange("p (h i two) -> p h i two", h=4, two=2)
            t1v = t1.rearrange("p (h d) -> p h d", h=4)
            t2p = t2.rearrange("p (h i two) -> p h i two", h=4, two=2)
            tt("dve", t1v, pv, bc(cos, [128, 4, 128], 1), ALU.mult, rope_ret_keys, ["t1"])
            tt("dve", t2p[:, :, :, 0], pp[:, :, :, 1], bc(ssin[:, :, 0], [128, 4, 64], 1), ALU.mult, rope_ret_keys, ["t2a"])
            tt("dve", t2p[:, :, :, 1], pp[:, :, :, 0], bc(ssin[:, :, 1], [128, 4, 64], 1), ALU.mult, rope_ret_keys, ["t2b"])
            tt("dve", kr, t1, t2, ALU.add, ["t1", "t2a", "t2b"], ["kr"])
            tt("dve", ktl.rearrange("p (h d) -> p h d", h=4), kr.rearrange("p (h d) -> p h d", h=4),
               bc(ktok, [128, 4, 128], 2), ALU.mult, ["kr", "cst"], ["ktl"])
            for n in range(2):
                bank, bk = inproj(1792 + n * 512, 512, 128)
                cp("act", vr[:, n * 512:(n + 1) * 512], bank[:, 0:512], [bk], ["vr"])
            if p + 1 < NPRE:
                norm_T(pxi(p + 1), 128, 0, do_part=False)
            state_update()
            if p == NPRE - 1:
                bank, bk = inproj(512, 256, 128)
                cp("act", ka32, bank[:, 0:128], [bk], ["ka32"])
                psv = bank[:, 0:128].rearrange("p (h d) -> p h d", h=2)
                cosA = tab[:, 512:528]
                ssinA = tab[:, 528:544]
                tk = [bk, TK]
                tt("dve", tA1[:, :2, :], psv[:, :, 0:16], bc(cosA, [128, 2, 16], 1), ALU.mult, tk, ["tA1"])
                tt("dve", tA2[:, :2, 0:8], psv[:, :, 8:16], bc(ssinA[:, 0:8], [128, 2, 8], 1), ALU.mult, tk, ["tA2"])
                tt("dve", tA2[:, :2, 8:16], psv[:, :, 0:8], bc(ssinA[:, 8:16], [128, 2, 8], 1), ALU.mult, tk, ["tA2b"])
                tt("dve", ka32.rearrange("p (h d) -> p h d", h=2)[:, :, 0:16], tA1[:, :2, :], tA2[:, :2, :], ALU.add,
                   ["tA1", "tA2", "tA2b", "ka32"], ["ka32"])
                cp("dve", kabf, ka32, ["ka32"], ["kabf"])
                cp("act", vbf[1], bank[:, 128:256], [bk], ["vbf1"])
                bank2, bk2 = ps()
                pb2 = bank2[:].bitcast(BF16)
                tp(pb2[:, 0:128], kabf, ident, ["kabf", "cbf"], [bk2], inc=True)
                cp("act", kT[1], pb2[:, 0:128], [bk2], ["kT1"])
        cp("act", Sbf, S32, ["S32"], ["Sbf"])

        def blk_slot(m):
            return (NPRE + m) % 2

        def tile_qa(m):
            slot = blk_slot(m)
            tab = TAB[slot]
            TK = f"tab{slot}"
            bank, bk = inproj(0, 512, 128)
            cp("act", t1, bank[:, 0:512], [bk], ["t1"])
            for g in range(2):
                cp("pool", qa.rearrange("p (j g d) -> p j g d", j=4, g=2)[:, :, g, :],
                   t1[:, g * 256:(g + 1) * 256].rearrange("p (j d) -> p j d", j=4), ["t1"], ["qa"])
            psv = t1.rearrange("p (h d) -> p h d", h=8)
            cosA = tab[:, 512:528]
            ssinA = tab[:, 528:544]
            tk = ["t1", TK]
            tt("dve", tA1, psv[:, :, 0:16], bc(cosA, [128, 8, 16], 1), ALU.mult, tk, ["tA1"])
            tt("dve", tA2[:, :, 0:8], psv[:, :, 8:16], bc(ssinA[:, 0:8], [128, 8, 8], 1), ALU.mult, tk, ["tA2"])
            tt("dve", tA2[:, :, 8:16], psv[:, :, 0:8], bc(ssinA[:, 8:16], [128, 8, 8], 1), ALU.mult, tk, ["tA2b"])
            for g in range(2):
                tt("dve", qa.rearrange("p (j g d) -> p j g d", j=4, g=2)[:, :, g, 0:16],
                   tA1[:, g * 4:(g + 1) * 4, :], tA2[:, g * 4:(g + 1) * 4, :], ALU.add,
                   ["tA1", "tA2", "tA2b", "qa"], ["qa"])

        def tile_kv(m):
            slot = blk_slot(m)
            tab = TAB[slot]
            TK = f"tab{slot}"
            cslot = m % 2
            cosA = tab[:, 512:528]
            ssinA = tab[:, 528:544]
            bank, bk = inproj(512, 256, 128)
            cp("act", ka32, bank[:, 0:128], [bk], ["ka32"])
            psv = bank[:, 0:128].rearrange("p (h d) -> p h d", h=2)
            tk = [bk, TK]
            tt("dve", tA1[:, :2, :], psv[:, :, 0:16], bc(cosA, [128, 2, 16], 1), ALU.mult, tk, ["tA1"])
            tt("dve", tA2[:, :2, 0:8], psv[:, :, 8:16], bc(ssinA[:, 0:8], [128, 2, 8], 1), ALU.mult, tk, ["tA2"])
            tt("dve", tA2[:, :2, 8:16], psv[:, :, 0:8], bc(ssinA[:, 8:16], [128, 2, 8], 1), ALU.mult, tk, ["tA2b"])
            tt("dve", ka32.rearrange("p (h d) -> p h d", h=2)[:, :, 0:16], tA1[:, :2, :], tA2[:, :2, :], ALU.add,
               ["tA1", "tA2", "tA2b", "ka32"], ["ka32"])
            cp("dve", kabf, ka32, ["ka32"], ["kabf"])
            cp("act", va32, bank[:, 128:256], [bk], ["va32"])
            cp("act", vbf[cslot], bank[:, 128:256], [bk], [f"vbf{cslot}"])

        def tile_rope(m, c0, tcol, dst, dkey):
            slot = blk_slot(m)
            tab = TAB[slot]
            TK = f"tab{slot}"
            bank, bk = inproj(c0, 512, 128)
            rk = [bk, TK]
            cos = tab[:, tcol:tcol + 128]
            ssin = tab[:, tcol + 128:tcol + 256].rearrange("p (i two) -> p i two", two=2)
            pv = bank[:, 0:512].rearrange("p (h d) -> p h d", h=4)
            pp = bank[:, 0:512].rearrange("p (h i two) -> p h i two", h=4, two=2)
            t1v = t1.rearrange("p (h d) -> p h d", h=4)
            t2p = t2.rearrange("p (h i two) -> p h i two", h=4, two=2)
            tt("dve", t1v, pv, bc(cos, [128, 4, 128], 1), ALU.mult, rk, ["t1"])
            tt("dve", t2p[:, :, :, 0], pp[:, :, :, 1], bc(ssin[:, :, 0], [128, 4, 64], 1), ALU.mult, rk, ["t2a"])
            tt("dve", t2p[:, :, :, 1], pp[:, :, :, 0], bc(ssin[:, :, 1], [128, 4, 64], 1), ALU.mult, rk, ["t2b"])
            tt("dve", dst, t1, t2, ALU.add, ["t1", "t2a", "t2b"], [dkey])

        def tile_qr(m):
            tile_rope(m, 768, 0, qr, "qr")

        def tile_kr(m):
            tile_rope(m, 1280, 256, kr, "kr")
            tt("pool", ktl.rearrange("p (h d) -> p h d", h=4), kr.rearrange("p (h d) -> p h d", h=4),
               bc(ktok, [128, 4, 128], 2), ALU.mult, ["kr", "cst"], ["ktl"])

        def tile_vr(m):
            for n in range(2):
                bank, bk = inproj(1792 + n * 512, 512, 128)
                cp("act", vr[:, n * 512:(n + 1) * 512], bank[:, 0:512], [bk], ["vr"])

        def tile_gate(m):
            for n in range(2):
                bank, bk = inproj(2816 + n * 512, 512, 128)
                act(og[:, n * 512:(n + 1) * 512], bank[:, 0:512], AF.Tanh, [bk], ["og"], scale=0.5)
                stt(sg[:, n * 512:(n + 1) * 512], og[:, n * 512:(n + 1) * 512], 1.0, bank[:, 0:512],
                    ALU.add, ALU.mult, [bk, "og"], ["sg"])

        def tile_gm(m):
            for n in range(2):
                bank, bk = inproj(3840 + n * 512, 512, 128)
                act(tha[:, n * 512:(n + 1) * 512], bank[:, 0:512], AF.Tanh, [bk], ["tha"], scale=0.5)
            for n in range(2):
                bank, bk = inproj(4864 + n * 512, 512, 128)
                act(thr[:, n * 512:(n + 1) * 512], bank[:, 0:512], AF.Tanh, [bk], ["thr"], scale=0.5)

        def head_norm(m):
            norm_T(blk_slot(m), 128, 0)

        def xi(m):
            return (m + 1) % 3

        def load_x(m):
            gb = NPRE + m
            S.dma("sp", X[xi(m)], xe[gb * 128:(gb + 1) * 128, :], f"x{xi(m)}", writes=[f"x{xi(m)}"])

        def load_tab(m):
            gb = NPRE + m
            sl = gb % 2
            S.dma("sp", TAB[sl], tabs[gb * 128:(gb + 1) * 128, :], f"tab{sl}", writes=[f"tab{sl}"])

        def head_norm_x(m, part=True, trans=True, a=True, b=True):
            if part:
                norm_part(xi(m), 128, a=a, b=b)
            if trans:
                norm_T(xi(m), 128, 0, do_part=False)

        tail_state = {}

        def p1(m):
            for n in range(2):
                bank, bk = ps()
                for j in range(4):
                    mm(bank[:, 0:512], oaT[:, j * 128:(j + 1) * 128], Wa[:, j, n * 512:(n + 1) * 512], j == 0, j == 3,
                       ["oaT", "wa"], [bk])
                stt(tha[:, n * 512:(n + 1) * 512], tha[:, n * 512:(n + 1) * 512], 1.0, bank[:, 0:512],
                    ALU.add, ALU.mult, [bk, "tha"], ["tha"])
                bank, bk = ps()
                for k in range(8):
                    mm(bank[:, 0:512], ogT[:, k, :], Wr[:, k, n * 512:(n + 1) * 512], k == 0, k == 7,
                       ["ogT", "wr"], [bk])
                stt(thr[:, n * 512:(n + 1) * 512], thr[:, n * 512:(n + 1) * 512], 1.0, bank[:, 0:512],
                    ALU.add, ALU.mult, [bk, "thr"], ["thr"])

        def p2(m):
            tt("dve", tha, tha, thr, ALU.add, ["tha", "thr"], ["tha"])
            bank, bk = ps()
            pb = bank[:].bitcast(BF16)
            for k in range(8):
                tp(pb[:, k * 128:(k + 1) * 128], tha[:, k * 128:(k + 1) * 128], ident, ["tha", "cbf"], [bk], inc=(k == 7))
            cp("act", ogT.rearrange("p k t -> p (k t)"), pb[:, 0:1024], [bk], ["ogT"])

        def p3(m):
            mb = []
            for n in range(2):
                bank, bk = ps(pin=True)
                for k in range(8):
                    mm(bank[:, 0:512], ogT[:, k, :], Wo[:, k, n * 512:(n + 1) * 512], k == 0, k == 7,
                       ["ogT", "wo"], [bk])
                mb.append((bank, bk))
                act(xs[:, n * 512:(n + 1) * 512], bank[:, 0:512], AF.Square, [bk], ["xs", f"ssm{n}"],
                    accum=st8[:, 16 + n:17 + n])
            tail_state["mb"] = mb

        def p4a(m):
            tt("dve", st8[:, 18:19], st8[:, 16:17], st8[:, 17:18], ALU.add, ["ssm0", "ssm1"], ["ssm"])
            ts("dve", st8[:, 19:20], st8[:, 18:19], 1.0 / D, 4.0 * EPS, ALU.mult, ALU.add, ["ssm"], ["msm"])
            tt("pool", st8[:, 20:21], st8[:, 19:20], CN05[:, 0:1], ALU.pow, ["msm", "cn05"], ["rstm"])

        def p4(m, last):
            mb = tail_state["mb"]
            xb = X[xi(m)]
            XK = f"x{xi(m)}"
            for n in range(2):
                bank, bk = mb[n]
                stt(bank[:, 0:512], bank[:, 0:512], st8[:, 20:21], GPM[:, n * 512:(n + 1) * 512], ALU.mult, ALU.mult,
                    [bk, "rstm", "gpm"], [bk])
                tt("dve", xb[:, n * 512:(n + 1) * 512], xb[:, n * 512:(n + 1) * 512], bank[:, 0:512], ALU.add, [XK, bk], [XK])
                unpin(bk)
            S.dma("sp", x1s[m * 128:(m + 1) * 128, :], xb, XK, reads=[XK], writes=[f"x1s{m}"])
            if last:
                S.dma("sp", kp_o, ka32, "ka32o", reads=["ka32"])
                S.dma("sp", vp_o, va32, "va32o", reads=["va32"])
            if m + 3 < NMAIN:
                load_x(m + 3)

        def mixer_block(m, last):
            pslot = (m - 1) % 2
            cslot = m % 2
            nxt = (not last)
            prev = m >= 1
            if m + 2 < NMAIN:
                load_tab(m + 2)
            bank, bk = ps()
            pb = bank[:].bitcast(BF16)
            for j in range(4):
                tp(pb[:, j * 128:(j + 1) * 128], qa[:, j * 128:(j + 1) * 128], ident, ["qa", "cbf"], [bk], inc=False)
            tp(pb[:, 512:640], kabf, ident, ["kabf", "cbf"], [bk], inc=True)
            cp("act", qaT, pb[:, 0:512], [bk], ["qaT"])
            cp("act", kT[cslot], pb[:, 512:640], [bk], [f"kT{cslot}"])
            bank, bk = ps()
            pb = bank[:].bitcast(BF16)
            for h in range(4):
                tp(pb[:, h * 128:(h + 1) * 128], qr[:, h * 128:(h + 1) * 128], ident, ["qr", "cbf"], [bk], inc=False)
            for h in range(4):
                tp(pb[:, 512 + h * 128:512 + (h + 1) * 128], kr[:, h * 128:(h + 1) * 128], ident, ["kr", "cbf"], [bk], inc=(h == 3))
            tt("dve", qrT, pb[:, 0:512], dqT, ALU.mult, [bk, "cst"], ["qrT"])
            tt("dve", krT, pb[:, 512:1024], dkT, ALU.mult, [bk, "cst"], ["krT"])
            if prev:
                p1(m - 1)
            if nxt:
                head_norm_x(m + 1, part=True, trans=False, a=True, b=False)
            sbanks = [ps(pin=True), ps(pin=True)]
            for h in range(4):
                sb, sk = sbanks[h // 2]
                c = (h % 2) * 256
                mm(sb[:, c:c + 256], ktl[:, h * 128:(h + 1) * 128], vr[:, h * 256:(h + 1) * 256], True, True,
                   ["ktl", "vr"], [sk])
            pm = mprev1 if m == 1 else mprev
            srcs = [(pslot, pm), (cslot, mown)]
            idx = 0
            for kv in range(2):
                for (sl, msk) in srcs:
                    bank, bk = ps()
                    mm(bank[:, 0:512], kT[sl][kv * 64:(kv + 1) * 64, :], qaT[kv * 64:(kv + 1) * 64, :], True, True,
                       [f"kT{sl}", "qaT"], [bk])
                    act(PT[idx], bank[:, 0:512], AF.Exp, [bk], [f"PT{idx}"], scale=0.125)
                    tt("pool", PT[idx].rearrange("p (j q) -> p j q", j=4), PT[idx].rearrange("p (j q) -> p j q", j=4),
                       bc(msk, [128, 4, 128], 1), ALU.mult, [f"PT{idx}", "cbf"], [f"PT{idx}"])
                    idx += 1
            if prev:
                p2(m - 1)
            tile_gm(m)
            if nxt:
                head_norm_x(m + 1, part=True, trans=True, a=False, b=True)
            bank, bk = ps()
            for h in range(4):
                mm(bank[:, h * 128:(h + 1) * 128], krT[:, h * 128:(h + 1) * 128], qrT[:, h * 128:(h + 1) * 128], True, True,
                   ["krT", "qrT"], [bk])
            tt("dve", AT.rearrange("p (h i) -> p h i", h=4), bank[:, 0:512].rearrange("p (h i) -> p h i", h=4),
               bc(mown, [128, 4, 128], 1), ALU.mult, [bk, "cbf"], ["AT"])
            bo, bok = ps()
            bd, bdk = ps()
            idx = 0
            for kv in range(2):
                for i, (sl, msk) in enumerate(srcs):
                    mm(bo[kv * 64:(kv + 1) * 64, 0:512], vbf[sl][:, kv * 64:(kv + 1) * 64], PT[idx], i == 0, i == 1,
                       [f"vbf{sl}", f"PT{idx}"], [bok])
                    idx += 1
            idx = 0
            for kv in range(2):
                for i, (sl, msk) in enumerate(srcs):
                    mm(bd[kv * 64:(kv + 1) * 64, 0:512], ones64, PT[idx], i == 0, i == 1,
                       ["cbf", f"PT{idx}"], [bdk])
                    idx += 1
            tt("dve", t1.rearrange("p (j q) -> p j q", j=4), bd[:, 0:512].rearrange("p (j q) -> p j q", j=4),
               bc(ESK, [128, 4, 128], 2), ALU.add, [bdk, "esk"], ["t1"])
            S.op("dve", lambda e: e.reciprocal(out=t1, in_=t1), ["t1"], ["t1"])
            tt("dve", oaT, bo[:, 0:512], t1, ALU.mult, [bok, "t1"], ["oaT"])
            if nxt:
                tile_qa(m + 1)
            if prev:
                p3(m - 1)
                p4a(m - 1)
            obanks = [ps(), ps()]
            for h in range(4):
                ob, ok = obanks[h // 2]
                c = (h % 2) * 256
                mm(ob[:, c:c + 256], AT[:, h * 128:(h + 1) * 128], vr[:, h * 256:(h + 1) * 256], True, False,
                   ["AT", "vr"], [ok])
                mm(ob[:, c:c + 256], qrT[:, h * 128:(h + 1) * 128], Sbf[:, h * 256:(h + 1) * 256], False, True,
                   ["qrT", "Sbf"], [ok])
            for h in range(4):
                ob, ok = obanks[h // 2]
                c = (h % 2) * 256
                act(xs[:, h * 256:(h + 1) * 256], ob[:, c:c + 256], AF.Square, [ok], ["xs", f"ssr{h}"], accum=st8[:, 4 + h:5 + h])
            ts("dve", st8[:, 8:12], st8[:, 4:8], 4.0 / 256.0, 4.0 * EPS, ALU.mult, ALU.add,
               [f"ssr{h}" for h in range(4)], ["msr"])
            tt("pool", st8[:, 12:16], st8[:, 8:12], CN05, ALU.pow, ["msr", "cn05"], ["rstr"])
            if nxt:
                tile_kr(m + 1)
            for h in range(4):
                ob, ok = obanks[h // 2]
                c = (h % 2) * 256
                stt(og[:, h * 256:(h + 1) * 256], ob[:, c:c + 256], st8[:, 12 + h:13 + h], sg[:, h * 256:(h + 1) * 256],
                    ALU.mult, ALU.mult, [ok, "rstr", "sg"], ["og"])
            if prev:
                p4(m - 1, False)
            for h in range(4):
                sb, sk = sbanks[h // 2]
                c = (h % 2) * 256
                stt(S32[:, h * 256:(h + 1) * 256], S32[:, h * 256:(h + 1) * 256], float(GAM[h] ** 128),
                    sb[:, c:c + 256], ALU.mult, ALU.add, [sk, "S32"], ["S32"])
            unpin(sbanks[0][1])
            unpin(sbanks[1][1])
            cp("act", Sbf, S32, ["S32"], ["Sbf"])
            if nxt:
                tile_qr(m + 1)
                tile_kv(m + 1)
                tile_vr(m + 1)
            bank, bk = ps()
            pb = bank[:].bitcast(BF16)
            for k in range(8):
                tp(pb[:, k * 128:(k + 1) * 128], og[:, k * 128:(k + 1) * 128], ident, ["og", "cbf"], [bk], inc=(k == 7))
            cp("act", ogT.rearrange("p k t -> p (k t)"), pb[:, 0:1024], [bk], ["ogT"])
            if nxt:
                tile_gate(m + 1)
            if m < 16:
                r0 = m * 64
                its = [(wupbf[r0:r0 + 64, c * 1024:(c + 1) * 1024], w_up[r0:r0 + 64, c * 1024:(c + 1) * 1024], f"cvu{m}_{c}")
                       for c in range(6)]
                r1 = m * 192
                its.append((wdnbf[r1:r1 + 192, :], w_dn[r1:r1 + 192, :], f"cvd{m}"))
                S.dma_multi("pool", its, "cvt")

        load_x(1)
        load_tab(1)
        load_x(2)
        head_norm_x(0)
        tile_qa(0)
        tile_kv(0)
        tile_qr(0)
        tile_kr(0)
        tile_vr(0)
        tile_gate(0)
        for m in range(NMAIN):
            mixer_block(m, m == NMAIN - 1)
        p1(NMAIN - 1)
        p2(NMAIN - 1)
        p3(NMAIN - 1)
        p4a(NMAIN - 1)
        p4(NMAIN - 1, True)
        S.dma("sp", rp_o.rearrange("h k v -> k h v"), S32.rearrange("p (h v) -> p h v", h=4), "s32o", reads=["S32"])


        pre_state = {}

        def sample_mixer():
            S.barrier()
            NP = 64
            CS2 = A.alloc(128, [544], F32)
            CKB = [kT[1], A.alloc(128, [128], BF16)]
            CVB = [vbf[1], A.alloc(128, [128], BF16)]
            CKT = [A.alloc(128, [128], BF16) for _ in range(2)]
            ones128 = CBF[:, 640:768]
            maskc = CBF[:, 768:772]
            mnew = CBF[0:64, 832:896]
            dqs = CS2[:, 0:256]
            dms = CS2[0:64, 256:512]
            ktoks = CS2[0:64, 512:516]
            rowm = CS2[0:64, 516:532]
            S0B = [X[1], S32]
            Qb = [qaT[:, 256:512], krT[:, 256:512]]
            Kb = [PT[2], PT[3]]
            S.dma("sp", CS2, cs2, "cs2", writes=["cs2"])
            S.dma("sp", X[0][:NP], xsm, "x0", writes=["x0"])
            S.dma("sp", TAB[0][:NP], tabs[4096:4160, :], "tab0", writes=["tab0"])
            S.dma("sp", ks_o[:, 0:124, :], ck[:, 4:128, :], "kcpy")
            S.dma("sp", vs_o[:, 0:124, :], cv[:, 4:128, :], "vcpy")
            tab = TAB[0]
            TK = "tab0"
            xb = X[0]
            XK = "x0"
            norm_T(0, NP, 0)
            if SCUT == 1:
                pinned.clear()
                return
            bank, bk = inproj(0, 512, NP)
            cp("act", t1[:NP], bank[:NP, 0:512], [bk], ["t1"])
            for g in range(2):
                cp("pool", qa[:NP].rearrange("p (j g d) -> p j g d", j=4, g=2)[:, :, g, :],
                   t1[:NP, g * 256:(g + 1) * 256].rearrange("p (j d) -> p j d", j=4), ["t1"], ["qa"])
            psv = t1[:NP].rearrange("p (h d) -> p h d", h=8)
            cosA = tab[:NP, 512:528]
            ssinA = tab[:NP, 528:544]
            tk = ["t1", TK]
            tt("dve", tA1[:NP], psv[:, :, 0:16], bc(cosA, [NP, 8, 16], 1), ALU.mult, tk, ["tA1"])
            tt("dve", tA2[:NP, :, 0:8], psv[:, :, 8:16], bc(ssinA[:, 0:8], [NP, 8, 8], 1), ALU.mult, tk, ["tA2"])
            tt("dve", tA2[:NP, :, 8:16], psv[:, :, 0:8], bc(ssinA[:, 8:16], [NP, 8, 8], 1), ALU.mult, tk, ["tA2b"])
            for g in range(2):
                tt("dve", qa[:NP].rearrange("p (j g d) -> p j g d", j=4, g=2)[:, :, g, 0:16],
                   tA1[:NP, g * 4:(g + 1) * 4, :], tA2[:NP, g * 4:(g + 1) * 4, :], ALU.add,
                   ["tA1", "tA2", "tA2b", "qa"], ["qa"])
            bank, bk = inproj(512, 256, NP)
            cp("act", ka32[:NP], bank[:NP, 0:128], [bk], ["ka32"])
            cp("act", t2[:NP, 0:128], bank[:NP, 0:128], [bk], ["t2"])
            psv = t2[:NP, 0:128].rearrange("p (h d) -> p h d", h=2)
            tk = ["t2", TK]
            tt("dve", tA1[:NP, :2, :], psv[:, :, 0:16], bc(cosA, [NP, 2, 16], 1), ALU.mult, tk, ["tA1"])
            tt("dve", tA2[:NP, :2, 0:8], psv[:, :, 8:16], bc(ssinA[:, 0:8], [NP, 2, 8], 1), ALU.mult, tk, ["tA2"])
            tt("dve", tA2[:NP, :2, 8:16], psv[:, :, 0:8], bc(ssinA[:, 8:16], [NP, 2, 8], 1), ALU.mult, tk, ["tA2b"])
            tt("dve", ka32[:NP].rearrange("p (h d) -> p h d", h=2)[:, :, 0:16], tA1[:NP, :2, :], tA2[:NP, :2, :], ALU.add,
               ["tA1", "tA2", "tA2b", "ka32"], ["ka32"])
            cp("dve", kabf[:NP], ka32[:NP], ["ka32"], ["kabf"])
            cp("act", va32[:NP], bank[:NP, 128:256], [bk], ["va32"])
            cp("act", vbf[0][:NP], bank[:NP, 128:256], [bk], ["vbf0"])
            for b in range(16):
                S.dma("sp", ks_o[b, 124:128, :], ka32[b * 4:(b + 1) * 4, :], "ka32o", reads=["ka32"])
                S.dma("sp", vs_o[b, 124:128, :], va32[b * 4:(b + 1) * 4, :], "va32o", reads=["va32"])
            if SCUT == 2:
                pinned.clear()
                return
            for (c0, tcol, dst, dk_) in ((768, 0, qr, "qr"), (1280, 256, kr, "kr")):
                bank, bk = inproj(c0, 512, NP)
                rk = [bk, TK]
                cos = tab[:NP, tcol:tcol + 128]
                ssin = tab[:NP, tcol + 128:tcol + 256].rearrange("p (i two) -> p i two", two=2)
                pv = bank[:NP, 0:512].rearrange("p (h d) -> p h d", h=4)
                pp = bank[:NP, 0:512].rearrange("p (h i two) -> p h i two", h=4, two=2)
                t1v = t1[:NP].rearrange("p (h d) -> p h d", h=4)
                t2p = t2[:NP].rearrange("p (h i two) -> p h i two", h=4, two=2)
                tt("dve", t1v, pv, bc(cos, [NP, 4, 128], 1), ALU.mult, rk, ["t1"])
                tt("dve", t2p[:, :, :, 0], pp[:, :, :, 1], bc(ssin[:, :, 0], [NP, 4, 64], 1), ALU.mult, rk, ["t2"])
                tt("dve", t2p[:, :, :, 1], pp[:, :, :, 0], bc(ssin[:, :, 1], [NP, 4, 64], 1), ALU.mult, rk, ["t2b"])
                tt("dve", dst[:NP], t1[:NP], t2[:NP], ALU.add, ["t1", "t2", "t2b"], [dk_])
            tt("pool", ktl[:NP].rearrange("p (h d) -> p h d", h=4), kr[:NP].rearrange("p (h d) -> p h d", h=4),
               bc(ktoks, [NP, 4, 128], 2), ALU.mult, ["kr", "cs2"], ["ktl"])
            for n in range(2):
                bank, bk = inproj(1792 + n * 512, 512, NP)
                cp("act", vr[:NP, n * 512:(n + 1) * 512], bank[:NP, 0:512], [bk], ["vr"])
            for n in range(2):
                bank, bk = inproj(2816 + n * 512, 512, NP)
                act(tha[:NP, n * 512:(n + 1) * 512], bank[:NP, 0:512], AF.Tanh, [bk], ["tha"], scale=0.5)
                stt(sg[:NP, n * 512:(n + 1) * 512], tha[:NP, n * 512:(n + 1) * 512], 1.0, bank[:NP, 0:512],
                    ALU.add, ALU.mult, [bk, "tha"], ["sg"])
            if SCUT == 3:
                pinned.clear()
                return
            for n in range(2):
                bank, bk = inproj(3840 + n * 512, 512, NP)
                act(tha[:NP, n * 512:(n + 1) * 512], bank[:NP, 0:512], AF.Tanh, [bk], ["tha"], scale=0.5)
            for n in range(2):
                bank, bk = inproj(4864 + n * 512, 512, NP)
                act(thr[:NP, n * 512:(n + 1) * 512], bank[:NP, 0:512], AF.Tanh, [bk], ["thr"], scale=0.5)
            off_save = A.off
            A.off = mark_w
            WupE = A.alloc(128, [8, F2], BF16)
            A.off = off_save
            for k in range(8):
                S.dma("sp", WupE[:, k, :], wupbf[k * 128:(k + 1) * 128, :], "wup",
                      writes=(["winA", "winB", "x2", "wupk0"] if k == 0 else [f"wupk{k}"]))
            pre_state["wup"] = True
            bank, bk = ps()
            pb = bank[:].bitcast(BF16)
            for j in range(4):
                tp(pb[:, j * 128:j * 128 + 64], qa[:NP, j * 128:(j + 1) * 128], ident[:NP, :NP], ["qa", "cbf"], [bk], inc=False)
            tp(pb[:, 512:576], kabf[:NP], ident[:NP, :NP], ["kabf", "cbf"], [bk], inc=True)
            v4 = lambda ap: ap.rearrange("p (j q) -> p j q", j=4)
            cp("act", v4(qaT[:, 0:256]), v4(pb[:, 0:512])[:, :, 0:64], [bk], ["qaT"])
            cp("act", kT[0][:, 0:64], pb[:, 512:576], [bk], ["kT0"])
            if SCUT == 31:
                return
            bank, bk = ps()
            pb = bank[:].bitcast(BF16)
            for h in range(4):
                tp(pb[:, h * 128:h * 128 + 64], qr[:NP, h * 128:(h + 1) * 128], ident[:NP, :NP], ["qr", "cbf"], [bk], inc=False)
            for h in range(4):
                tp(pb[:, 512 + h * 128:512 + h * 128 + 64], kr[:NP, h * 128:(h + 1) * 128], ident[:NP, :NP], ["kr", "cbf"], [bk], inc=(h == 3))
            if SCUT == 32:
                S.op("pe", lambda e: e.transpose(pb[:, 0:64], qr[:NP, 0:128], ident[:NP, :NP]), ["qr"], [bk])
                return
            cp("act", v4(qrT[:, 0:256]), v4(pb[:, 0:512])[:, :, 0:64], [bk], ["qrT"])
            if SCUT == 33:
                return
            tt("dve", qrT[:, 256:512], qrT[:, 0:256], dqs, ALU.mult, ["qrT", "cs2"], ["qtT"])
            cp("act", v4(krT[:, 0:256]), v4(pb[:, 512:1024])[:, :, 0:64], [bk], ["krT"])
            if SCUT == 4:
                pinned.clear()
                return
            bS2 = [ps(pin=True), ps(pin=True)]
            CB = [PT[2][:, i * 128:(i + 1) * 128] for i in range(4)] + [PT[3][:, i * 128:(i + 1) * 128] for i in range(4)]
            for b in range(16):
                sl = b % 2
                c8 = b % 8
                S.dma("pool", CB[c8], ck[b], f"cb{c8}", writes=[f"cb{c8}"])
                bank, bk = ps()
                pb = bank[:].bitcast(BF16)
                tp(pb[:, 0:128], CB[c8], ident, [f"cb{c8}", "cbf"], [bk], inc=True)
                cp("act", CKT[sl], pb[:, 0:128], [bk], [f"ckt{sl}"])
                for kv in range(2):
                    off = b * 16
                    mm(bS2[kv][0][:, off:off + 16].rearrange("p (j t) -> p j t", j=4),
                       CKT[sl][kv * 64:(kv + 1) * 64, :],
                       qaT[kv * 64:(kv + 1) * 64, 0:256].rearrange("p (j q) -> p j q", j=4)[:, :, b * 4:(b + 1) * 4],
                       True, True, [f"ckt{sl}", "qaT"], [bS2[kv][1]])
            if SCUT == 5:
                pinned.clear()
                return
            PTc = PT[0]
            PTn = PT[1]
            for kv in range(2):
                act(PTc[:, kv * 256:(kv + 1) * 256], bS2[kv][0][:, 0:256], AF.Exp, [bS2[kv][1]], ["PTc"], scale=0.125)
                unpin(bS2[kv][1])
            tt("pool", PTc.rearrange("p (g t) -> p g t", t=4), PTc.rearrange("p (g t) -> p g t", t=4),
               bc(maskc, [128, 128, 4], 1), ALU.mult, ["PTc", "cbf"], ["PTc"])
            if SCUT == 51:
                return
            S.op("pool", lambda e: e.memset(PTn, 0.0), (), ["PTn"])
            S.op("pool", lambda e: e.memset(AT, 0.0), (), ["AT"])
            for kv in range(2):
                bN, bNk = ps()
                mm(bN[:, 0:256], kT[0][kv * 64:(kv + 1) * 64, 0:128], qaT[kv * 64:(kv + 1) * 64, 0:256],
                   True, True, ["kT0", "qaT"], [bNk])
                act(PTn[:NP, kv * 256:(kv + 1) * 256], bN[:NP, 0:256], AF.Exp, [bNk], ["PTn"], scale=0.125)
            if SCUT == 52:
                return
            tt("pool", PTn[:NP].rearrange("p (g q) -> p g q", g=8), PTn[:NP].rearrange("p (g q) -> p g q", g=8),
               bc(mnew, [NP, 8, 64], 1), ALU.mult, ["PTn", "cbf"], ["PTn"])
            if SCUT == 6:
                pinned.clear()
                return
            bo, bok = ps()
            bdn = [ps(), ps()]
            bdc, bdck = ps()
            mm(bdc[:, 0:512], ones128, PTc, True, True, ["cbf", "PTc"], [bdck])
            for kv in range(2):
                mm(bdn[kv][0][:, 0:256], ones128, PTn[:, kv * 256:(kv + 1) * 256], True, True,
                   ["cbf", "PTn"], [bdn[kv][1]])
            for kv in range(2):
                mm(bo[kv * 64:(kv + 1) * 64, 0:256], vbf[0][:, kv * 64:(kv + 1) * 64], PTn[:, kv * 256:(kv + 1) * 256],
                   True, False, ["vbf0", "PTn"], [bok])
            for b in range(16):
                c8 = b % 8
                S.dma("pool", CB[c8], cv[b], f"cb{c8}", writes=[f"cb{c8}"])
                for kv in range(2):
                    off = kv * 256 + b * 16
                    mm(bo[kv * 64:(kv + 1) * 64, 0:256].rearrange("p (j q) -> p j q", j=4)[:, :, b * 4:(b + 1) * 4],
                       CB[c8][:, kv * 64:(kv + 1) * 64],
                       PTc[:, off:off + 16].rearrange("p (j t) -> p j t", j=4),
                       False, b == 15, [f"cb{c8}", "PTc"], [bok], inc=True)
            if SCUT == 7:
                pinned.clear()
                return
            cp("act", t2, bdc[:, 0:512], [bdck], ["t2"])
            for kv in range(2):
                hs = slice(kv * 64, (kv + 1) * 64)
                tt("dve", t1[hs, 0:256].rearrange("p (j b t) -> p j b t", j=4, b=16),
                   bdn[kv][0][hs, 0:256].rearrange("p (j b t) -> p j b t", j=4, b=16),
                   t2[hs, :].rearrange("p (kv b j t) -> p kv j b t", b=16, kv=2, j=4)[:, kv],
                   ALU.add, [bdn[kv][1], "t2"], [f"t1{kv}"])
            tt("dve", t1[:, 0:256].rearrange("p (j q) -> p j q", j=4), t1[:, 0:256].rearrange("p (j q) -> p j q", j=4),
               bc(ESK, [128, 4, 64], 2), ALU.add, ["t10", "t11", "esk"], ["t1"])
            S.op("dve", lambda e: e.reciprocal(out=t1[:, 0:256], in_=t1[:, 0:256]), ["t1"], ["t1"])
            tt("dve", oaT[:, 0:256], bo[:, 0:256], t1[:, 0:256], ALU.mult, [bok, "t1"], ["oaT"])
            if SCUT == 8:
                pinned.clear()
                return
            bR, bRk = ps()
            for h in range(4):
                mm(bR[:NP, h * 64:(h + 1) * 64], krT[:, h * 64:(h + 1) * 64], qrT[:, h * 64:(h + 1) * 64], True, True,
                   ["krT", "qrT"], [bRk])
            tt("dve", AT[:NP, 0:256], bR[:NP, 0:256], dms, ALU.mult, [bRk, "cs2"], ["AT"])
            ob = [ps(pin=True) for _ in range(4)]
            for h in range(4):
                mm(ob[h][0][:NP, 0:256], AT[:, h * 64:(h + 1) * 64], vr[:, h * 256:(h + 1) * 256], True, False,
                   ["AT", "vr"], [ob[h][1]])
            if SCUT == 9:
                pinned.clear()
                return
            S.op("dve", lambda e: e.memset(Qb[0], 0.0), (), ["qb0"])
            S.op("dve", lambda e: e.memset(Qb[1], 0.0), (), ["qb1"])
            def load_s0(b_):
                sl_ = b_ % 2
                S.dma("sp", S0B[sl_].rearrange("p (h v) -> p h v", h=4), sr[b_].rearrange("h k v -> k h v"), f"s0{sl_}",
                      writes=[f"s0{sl_}"])

            load_s0(0)
            load_s0(1)
            for b in range(16):
                sl = b % 2
                s0 = S0B[sl]
                sk_ = f"s0{sl}"
                cp("act", Sbf, s0, [sk_], ["Sbf"])
                qv = Qb[sl].rearrange("p (h q) -> p h q", h=4)
                if b >= 2:
                    S.op("dve", (lambda v: lambda e: e.memset(v, 0.0))(qv[:, :, (b - 2) * 4:(b - 1) * 4]), (), [f"qb{sl}"])
                cp("dve", qv[:, :, b * 4:(b + 1) * 4], qrT[:, 256:512].rearrange("p (h q) -> p h q", h=4)[:, :, b * 4:(b + 1) * 4],
                   ["qtT"], [f"qb{sl}"])
                act(Kb[sl][:NP], ktl[:NP], AF.Copy, ["ktl", "cs2"], [f"kb{sl}"] + [f"cb{sl * 4 + i}" for i in range(4)],
                    scale=rowm[:, b:b + 1])
                for h in range(4):
                    mm(ob[h][0][:NP, 0:256], Qb[sl][:, h * 64:(h + 1) * 64], Sbf[:, h * 256:(h + 1) * 256], False, b == 15,
                       [f"qb{sl}", "Sbf"], [ob[h][1]])
                sbanks = [ps(), ps()]
                for h in range(4):
                    sb, sk2 = sbanks[h // 2]
                    c = (h % 2) * 256
                    mm(sb[:, c:c + 256], Kb[sl][:NP, h * 128:(h + 1) * 128], vr[:NP, h * 256:(h + 1) * 256], True, True,
                       [f"kb{sl}", "vr"], [sk2])
                for h in range(4):
                    sb, sk2 = sbanks[h // 2]
                    c = (h % 2) * 256
                    stt(s0[:, h * 256:(h + 1) * 256], s0[:, h * 256:(h + 1) * 256], float(GAM[h] ** 4),
                        sb[:, c:c + 256], ALU.mult, ALU.add, [sk2, sk_, "Sbf"], [sk_])
                S.dma("sp", rs_o[b].rearrange("h k v -> k h v"), s0.rearrange("p (h v) -> p h v", h=4), sk_, reads=[sk_])
                if b + 2 < 16:
                    load_s0(b + 2)
            if SCUT == 10:
                pinned.clear()
                return
            for h in range(4):
                unpin(ob[h][1])
            for h in range(4):
                act(xs[:NP, h * 256:(h + 1) * 256], ob[h][0][:NP, 0:256], AF.Square, [ob[h][1]], ["xs", f"ssr{h}"], accum=st8[:NP, 4 + h:5 + h])
            ts("dve", st8[:NP, 8:12], st8[:NP, 4:8], 4.0 / 256.0, 4.0 * EPS, ALU.mult, ALU.add,
               [f"ssr{h}" for h in range(4)], ["msr"])
            tt("pool", st8[:NP, 12:16], st8[:NP, 8:12], CN05[:NP], ALU.pow, ["msr", "cn05"], ["rstr"])
            for h in range(4):
                stt(og[:NP, h * 256:(h + 1) * 256], ob[h][0][:NP, 0:256], st8[:NP, 12 + h:13 + h], sg[:NP, h * 256:(h + 1) * 256],
                    ALU.mult, ALU.mult, [ob[h][1], "rstr", "sg"], ["og"])
            bank, bk = ps()
            pb = bank[:].bitcast(BF16)
            for k in range(8):
                tp(pb[:, k * 128:k * 128 + 64], og[:NP, k * 128:(k + 1) * 128], ident[:NP, :NP], ["og", "cbf"], [bk], inc=(k == 7))
            cp("act", ogT[:, :, 0:64], pb[:, 0:1024].rearrange("p (k t) -> p k t", k=8)[:, :, 0:64], [bk], ["ogT"])
            for n in range(2):
                bank, bk = ps()
                for j in range(4):
                    mm(bank[:NP, 0:512], oaT[:, j * 64:(j + 1) * 64], Wa[:, j, n * 512:(n + 1) * 512], j == 0, j == 3,
                       ["oaT", "wa"], [bk])
                stt(tha[:NP, n * 512:(n + 1) * 512], tha[:NP, n * 512:(n + 1) * 512], 1.0, bank[:NP, 0:512],
                    ALU.add, ALU.mult, [bk, "tha"], ["tha"])
                bank, bk = ps()
                for k in range(8):
                    mm(bank[:NP, 0:512], ogT[:, k, 0:64], Wr[:, k, n * 512:(n + 1) * 512], k == 0, k == 7,
                       ["ogT", "wr"], [bk])
                stt(thr[:NP, n * 512:(n + 1) * 512], thr[:NP, n * 512:(n + 1) * 512], 1.0, bank[:NP, 0:512],
                    ALU.add, ALU.mult, [bk, "thr"], ["thr"])
            tt("dve", tha[:NP], tha[:NP], thr[:NP], ALU.add, ["tha", "thr"], ["tha"])
            bank, bk = ps()
            pb = bank[:].bitcast(BF16)
            for k in range(8):
                tp(pb[:, k * 128:k * 128 + 64], tha[:NP, k * 128:(k + 1) * 128], ident[:NP, :NP], ["tha", "cbf"], [bk], inc=(k == 7))
            cp("act", ogT[:, :, 0:64], pb[:, 0:1024].rearrange("p (k t) -> p k t", k=8)[:, :, 0:64], [bk], ["ogT"])
            mb = []
            for n in range(2):
                bank, bk = ps()
                for k in range(8):
                    mm(bank[:NP, 0:512], ogT[:, k, 0:64], Wo[:, k, n * 512:(n + 1) * 512], k == 0, k == 7,
                       ["ogT", "wo"], [bk])
                mb.append((bank, bk))
                act(xs[:NP, n * 512:(n + 1) * 512], bank[:NP, 0:512], AF.Square, [bk], ["xs", f"ssm{n}"],
                    accum=st8[:NP, 16 + n:17 + n])
            tt("dve", st8[:NP, 18:19], st8[:NP, 16:17], st8[:NP, 17:18], ALU.add, ["ssm0", "ssm1"], ["ssm"])
            ts("dve", st8[:NP, 19:20], st8[:NP, 18:19], 1.0 / D, 4.0 * EPS, ALU.mult, ALU.add, ["ssm"], ["msm"])
            tt("pool", st8[:NP, 20:21], st8[:NP, 19:20], CN05[:NP, 0:1], ALU.pow, ["msm", "cn05"], ["rstm"])
            for n in range(2):
                bank, bk = mb[n]
                stt(bank[:NP, 0:512], bank[:NP, 0:512], st8[:NP, 20:21], GPM[:NP, n * 512:(n + 1) * 512], ALU.mult, ALU.mult,
                    [bk, "rstm", "gpm"], [bk])
                tt("dve", xb[:NP, n * 512:(n + 1) * 512], xb[:NP, n * 512:(n + 1) * 512], bank[:NP, 0:512], ALU.add, [XK, bk], [XK])
            S.dma("sp", x1s[NMAIN * 128:NMAIN * 128 + 64, :], xb[:NP], XK, reads=[XK], writes=["x1ss"])

        sample_mixer()

        def ffn_phase():
            S.barrier()
            A.off = mark_w
            Wup = A.alloc(128, [8, F2], BF16)
            Wdn = A.alloc(128, [24, D], BF16)
            CW = A.alloc(128, [48, 4], F32)
            Uh = A.alloc(128, [48, 2], F32)
            X1R = A.alloc(128, [4 * D], F32)
            X1 = [X1R[:, i * D:(i + 1) * D] for i in range(4)]
            xs2 = A.alloc(128, [D], BF16)
            XN2 = [A.alloc(128, [8, 256], BF16) for _ in range(2)]
            xn2T = XN2[0]
            Ua = [A.alloc(128, [258], F32) for _ in range(2)]
            Ub = [A.alloc(128, [258], F32) for _ in range(2)]
            CA = [A.alloc(128, [256], F32) for _ in range(2)]
            CB = [A.alloc(128, [256], F32) for _ in range(2)]
            G2s = A.alloc(128, [256], F32)
            G2 = [G2s, G2s]
            G1 = [CA[1], CB[1]]
            ca, cb_, g1, g2 = CA[0], CB[0], CA[1], G2s
            hT = A.alloc(128, [24, 256], BF16)
            tmpf = A.alloc(128, [D], F32)
            stf = A.alloc(128, [16], F32)
            CPO = A.alloc(128, [F2], F32) if False else None
            items = []
            for k in range(8):
                for c in range(6):
                    items.append((Wup[:, k, c * 1024:(c + 1) * 1024], w_up[k * 128:(k + 1) * 128, c * 1024:(c + 1) * 1024], "wup"))
            if not pre_state.get("wup"):
                S.dma_multi("pool", items, "wup")
            S.dma_multi("sp", [(Wdn[:, k, :], wdnbf[k * 128:(k + 1) * 128, :], "wdn") for k in range(24)], "wdn")
            S.dma("sp", CW.rearrange("p a b -> p (a b)"), cwT, "cw", writes=["cw"])
            S.dma("sp", GPM, gpost[1].partition_broadcast(128), "gpm", writes=["gpm"])

            def norm_T2(xt, xk, np_, dest, dkey, part=True, trans=True):
                if part:
                    act(tmpf[:np_], xt[:np_], AF.Square, [xk], ["tmpf", "fssq"], accum=stf[:np_, 0:1])
                    ts("dve", stf[:np_, 1:2], stf[:np_, 0:1], 1.0 / D, EPS, ALU.mult, ALU.add, ["fssq"], ["fms"])
                    tt("pool", stf[:np_, 2:3], stf[:np_, 1:2], CN05[:np_, 0:1], ALU.pow, ["fms", "cn05"], ["frstd"])
                    act(xs2[:np_], xt[:np_], AF.Copy, [xk, "frstd"], ["xs2"], scale=stf[:np_, 2:3])
                if not trans:
                    return
                bank, bk = ps()
                pb = bank[:].bitcast(BF16)
                for k in range(8):
                    tp(pb[:, k * 128:k * 128 + np_], xs2[:np_, k * 128:(k + 1) * 128], ident[:np_, :np_],
                       ["xs2", "cbf"], [bk], inc=(k == 7))
                pv = pb.rearrange("p (k t) -> p k t", k=8)[:, :, :np_]
                tt("dve", dest, pv, bc(GVT[:, 8:16], [128, 8, np_], 2), ALU.mult, [bk, "gvt"], [dkey])

            S.dma("sp", X1[0][0:2], x1s[126:128, :], "x1_0", writes=["x1_0"])
            norm_T2(X1[0], "x1_0", 2, xn2T[:, :, 0:2], "xn2T0")
            bank, bk = ps()
            for t in range(48):
                for k in range(8):
                    mm(bank[:, t * 2:t * 2 + 2], Wup[:, k, t * 128:(t + 1) * 128], xn2T[:, k, 0:2], k == 0, k == 7,
                       ["wup", "xn2T0"], [bk])
            cp("act", Uh.rearrange("p a b -> p (a b)"), bank[:, 0:96], [bk], ["uh"])

            def act_id(out, in_, sc, bi, reads, writes):
                S.op("act", lambda e: e.activation(out=out, in_=in_, func=AF.Identity, scale=sc, bias=bi), reads, writes)

            def conv_gelu(j, ntok, ub_a, ub_b, bka, bkb, hdst, uview, slot):
                tiles = ((Ua[slot], ub_a, bka, j, CA[slot], f"ca{slot}"), (Ub[slot], ub_b, bkb, 24 + j, CB[slot], f"cb{slot}"))
                for (U, bank_, bk_, tix, cdst, ckey) in tiles:
                    uk = f"U{ckey}"
                    cp("pool", U[:, 0:2], Uh[:, tix, :], ["uh"], [uk + "h"])
                    cp("act", U[:, 2:2 + ntok], bank_[:, 0:ntok], [bk_], [uk])
                    act_id(cdst[:, :ntok], bank_[:, 0:ntok], CW[:, tix, 2:3], CW[:, tix, 3:4], [bk_, "cw"], [ckey])
                    cp("pool", Uh[:, tix, :], U[:, ntok:ntok + 2], [uk], ["uh"])
                for tap in (1, 0):
                    for (U, bank_, bk_, tix, cdst, ckey) in tiles:
                        uk = f"U{ckey}"
                        stt(cdst[:, :ntok], U[:, tap:tap + ntok], CW[:, tix, tap:tap + 1], cdst[:, :ntok], ALU.mult, ALU.add,
                            [uk, uk + "h", "cw", ckey], [ckey])
                if uview != "defer":
                    gelu_mul(ntok, hdst, slot, uview if uview else "hT")

            def gelu_mul(ntok, hdst, slot=0, hkey="hT"):
                a = CA[slot][:, :ntok]
                b_ = CB[slot][:, :ntok]
                g2_ = G2s[:, :ntok]
                ak, bk2, g2k = f"ca{slot}", f"cb{slot}", "g2s"
                act(g2_, a, AF.Gelu_apprx_tanh, [ak], [g2k])
                tt("dve", hdst, g2_, b_, ALU.mult, [g2k, bk2], [hkey])

            def down_mm(fb, k, np_, tcol0):
                for n in range(2):
                    bank, bk = fb[n]
                    mm(bank[:np_, 0:512], hT[:, k, tcol0:tcol0 + np_], Wdn[:, k, n * 512:(n + 1) * 512], k == 0, k == 23,
                       [f"hT{k}", "wdn"], [bk])

            def down_tail(fb, xt, xk, np_, out_rows):
                for n in range(2):
                    bank, bk = fb[n]
                    act(tmpf[:np_, n * 512:(n + 1) * 512], bank[:np_, 0:512], AF.Square, [bk], ["tmpf", f"fs{n}"],
                        accum=stf[:np_, 4 + n:5 + n])
                tt("dve", stf[:np_, 6:7], stf[:np_, 4:5], stf[:np_, 5:6], ALU.add, ["fs0", "fs1"], ["fs"])
                ts("dve", stf[:np_, 7:8], stf[:np_, 6:7], 1.0 / D, EPS, ALU.mult, ALU.add, ["fs"], ["fm"])
                tt("pool", stf[:np_, 8:9], stf[:np_, 7:8], CN05[:np_, 0:1], ALU.pow, ["fm", "cn05"], ["fr"])
                for n in range(2):
                    bank, bk = fb[n]
                    act(tmpf[:np_, n * 512:(n + 1) * 512], bank[:np_, 0:512], AF.Copy, [bk, "fr"], ["tmpf"], scale=stf[:np_, 8:9])
                    unpin(bk)
                tt("pool", tmpf[:np_], tmpf[:np_], GPM[:np_], ALU.mult, ["tmpf", "gpm"], ["tmpf"])
                tt("dve", xt[:np_], xt[:np_], tmpf[:np_], ALU.add, [xk, "tmpf"], [xk])
                S.dma("sp", out_rows, xt[:np_], xk, reads=[xk])

            def tail_A(fb, blk):
                for n in range(2):
                    bank, bk = fb[n]
                    act(tmpf[:, n * 512:(n + 1) * 512], bank[:, 0:512], AF.Square, [bk], ["tmpf", f"fsq{blk}{n}"],
                        accum=stf[:, 4 + 2 * blk + n:5 + 2 * blk + n])

            def tail_B(blk):
                tt("dve", stf[:, 8 + blk:9 + blk], stf[:, 4 + 2 * blk:5 + 2 * blk], stf[:, 5 + 2 * blk:6 + 2 * blk], ALU.add,
                   [f"fsq{blk}0", f"fsq{blk}1"], [f"fsum{blk}"])
                ts("dve", stf[:, 10 + blk:11 + blk], stf[:, 8 + blk:9 + blk], 1.0 / D, EPS, ALU.mult, ALU.add,
                   [f"fsum{blk}"], [f"fms{blk}"])
                tt("pool", stf[:, 12 + blk:13 + blk], stf[:, 10 + blk:11 + blk], CN05[:, 0:1], ALU.pow,
                   [f"fms{blk}", "cn05"], [f"frs{blk}"])

            def tail_C(fb, blk, xt, xk, out_rows):
                for n in range(2):
                    bank, bk = fb[n]
                    stt(bank[:, 0:512], bank[:, 0:512], stf[:, 12 + blk:13 + blk], GPM[:, n * 512:(n + 1) * 512],
                        ALU.mult, ALU.mult, [bk, f"frs{blk}", "gpm"], [bk])
                    tt("dve", xt[:, n * 512:(n + 1) * 512], xt[:, n * 512:(n + 1) * 512], bank[:, 0:512], ALU.add, [xk, bk], [xk])
                    unpin(bk)
                S.dma("sp", out_rows, xt, xk, reads=[xk])

            def down_post(xt, xk, np_, tcol0, out_rows, okey_sem):
                fb = [ps(pin=True), ps(pin=True)]
                for k in range(24):
                    down_mm(fb, k, np_, tcol0)
                down_tail(fb, xt, xk, np_, out_rows)

            NG = 8

            def load_group(g):
                for i in range(2):
                    xi = (g % 2) * 2 + i
                    m = 1 + g * 2 + i
                    S.dma("sp", X1[xi], x1s[m * 128:(m + 1) * 128, :], f"x1_{xi}", writes=[f"x1_{xi}"])

            load_group(0)

            def norm_blk(g, i, part, trans):
                xi = (g % 2) * 2 + i
                norm_T2(X1[xi], f"x1_{xi}", 128, XN2[g % 2][:, :, i * 128:(i + 1) * 128], f"xn2T{g % 2}", part=part, trans=trans)

            norm_blk(0, 0, True, True)
            norm_blk(0, 1, True, True)
            LAG = 3
            pending = None
            for g in range(NG):
                if g == 0:
                    load_group(1)
                xn = XN2[g % 2]
                xnk = f"xn2T{g % 2}"
                fbs = None
                for j in range(24):
                    banks = []
                    for tix in (j, 24 + j):
                        bank, bk = ps()
                        for k in range(8):
                            mm(bank[:, 0:256], Wup[:, k, tix * 128:(tix + 1) * 128], xn[:, k, 0:256], k == 0, k == 7,
                               ["wup", xnk], [bk])
                        banks.append((bank, bk))
                    conv_gelu(j, 256, banks[0][0], banks[1][0], banks[0][1], banks[1][1], hT[:, j, 0:256], "defer", j % 2)
                    if j >= 1:
                        gelu_mul(256, hT[:, j - 1, 0:256], (j - 1) % 2, f"hT{j - 1}")
                    if pending is not None and j == 0:
                        tail_B(0)
                        tail_B(1)
                    if pending is not None and j == 1:
                        pf, pg = pending
                        for i in range(2):
                            xi = (pg % 2) * 2 + i
                            row0 = (pg * 2 + i) * 128
                            tail_C(pf[i], i, X1[xi], f"x1_{xi}", y_o[row0:row0 + 128, :])
                        pending = None
                        if g + 1 < NG:
                            load_group(g + 1)
                    if g + 1 < NG:
                        if j == 5:
                            norm_blk(g + 1, 0, True, False)
                        if j == 8:
                            norm_blk(g + 1, 0, False, True)
                        if j == 11:
                            norm_blk(g + 1, 1, True, False)
                        if j == 14:
                            norm_blk(g + 1, 1, False, True)
                    if j - LAG == 0:
                        fbs = [[ps(pin=True), ps(pin=True)] for _ in range(2)]
                    if j - LAG >= 0:
                        for i in range(2):
                            down_mm(fbs[i], j - LAG, 128, i * 128)
                gelu_mul(256, hT[:, 23, 0:256], 23 % 2, "hT23")
                for k in range(24 - LAG, 24):
                    for i in range(2):
                        down_mm(fbs[i], k, 128, i * 128)
                tail_A(fbs[0], 0)
                tail_A(fbs[1], 1)
                pending = (fbs, g)
            tail_B(0)
            tail_B(1)
            pf, pg = pending
            for i in range(2):
                xi = (pg % 2) * 2 + i
                row0 = (pg * 2 + i) * 128
                tail_C(pf[i], i, X1[xi], f"x1_{xi}", y_o[row0:row0 + 128, :])
            CP2 = tmpf
            for q4 in range(12):
                bank, bk = ps()
                for i in range(4):
                    t = q4 * 4 + i
                    tp(bank[0:2, i * 128:(i + 1) * 128], Uh[:, t, :], idf, ["uh", "cst"], [bk], inc=(i == 3))
                cp("act", g1[0:2, 0:256], bank[0:2, 0:256], [bk], ["ca1"])
                cp("act", g2[0:2, 0:256], bank[0:2, 256:512], [bk], ["g2s"])
                S.dma("sp", cp_o[:, q4 * 512:q4 * 512 + 256], g1[0:2, 0:256], "ca1", reads=["ca1"])
                S.dma("sp", cp_o[:, q4 * 512 + 256:q4 * 512 + 512], g2[0:2, 0:256], "g2s", reads=["g2s"])

            if SCUT != 0:
                return
            S.barrier()
            NP = 64
            XS = X1R[:, 0:1024]
            CTX = X1R[:, 1024:2560].rearrange("p (a b) -> p a b", a=48)
            US = X1R[:, 2560:4096].rearrange("p (a b) -> p a b", a=48)
            S.dma("sp", XS[:NP], x1s[NMAIN * 128:NMAIN * 128 + 64, :], "x1_0", writes=["x1_0"])
            for q4 in range(12):
                sc = tmpf[:32, (q4 % 2) * 512:(q4 % 2) * 512 + 512]
                sck = f"sc{q4 % 2}"
                S.dma("sp", sc, scv[:, q4 * 512:(q4 + 1) * 512], sck, writes=[sck])
                bank, bk = ps()
                for i in range(4):
                    tp(bank[:, i * 32:(i + 1) * 32], sc[:, i * 128:(i + 1) * 128], idf[:32, :32], [sck, "cst"], [bk], inc=(i == 3))
                cp("act", CTX[:, q4 * 4:(q4 + 1) * 4, :], bank[:, 0:128].rearrange("p (a b) -> p a b", a=4), [bk], ["ctx"])
            norm_T2(XS, "x1_0", NP, xn2T[:, :, 0:NP], "xn2T")
            for j in range(24):
                slot = j % 2
                banks = []
                for tix in (j, 24 + j):
                    bank, bk = ps()
                    for k in range(8):
                        mm(bank[:, 0:NP], Wup[:, k, tix * 128:(tix + 1) * 128], xn2T[:, k, 0:NP], k == 0, k == 7,
                           ["wup", "xn2T"], [bk])
                    banks.append((bank, bk))
                for (U, (bank_, bk_), tix, cdst, ckey) in ((Ua[slot], banks[0], j, CA[slot], f"ca{slot}"), (Ub[slot], banks[1], 24 + j, CB[slot], f"cb{slot}")):
                    uk = f"U{ckey}"
                    Ue = U[:, 0:96].rearrange("p (b s) -> p b s", b=16)
                    cv_ = cdst[:, 0:64].rearrange("p (b t) -> p b t", b=16)
                    cp("pool", Ue[:, :, 0:2], CTX[:, tix, :].rearrange("p (b c) -> p b c", b=16), ["ctx"], [uk])
                    cp("act", Ue[:, :, 2:6], bank_[:, 0:64].rearrange("p (b t) -> p b t", b=16), [bk_], [uk])
                    cp("pool", US[:, tix, :].rearrange("p (b c) -> p b c", b=16), Ue[:, :, 4:6], [uk], ["us"])
                    ts("dve", cv_, Ue[:, :, 2:6], CW[:, tix, 2:3], CW[:, tix, 3:4], ALU.mult, ALU.add, [uk, "cw"], [ckey])
                    stt(cv_, Ue[:, :, 1:5], CW[:, tix, 1:2], cv_, ALU.mult, ALU.add, [uk, "cw", ckey], [ckey])
                    stt(cv_, Ue[:, :, 0:4], CW[:, tix, 0:1], cv_, ALU.mult, ALU.add, [uk, "cw", ckey], [ckey])
                gelu_mul(NP, hT[:, j, 0:NP], slot, f"hT{j}")
            down_post(XS, "x1_0", NP, 0, ys_o, None)
            for q4 in range(12):
                bank, bk = ps()
                for i in range(4):
                    tp(bank[0:32, i * 128:(i + 1) * 128], US[:, q4 * 4 + i, :], idf, ["us", "cst"], [bk], inc=(i == 3))
                stg = CA[1] if q4 % 2 == 0 else G2s
                sgk = "ca1" if q4 % 2 == 0 else "g2s"
                cp("act", stg[0:32, 0:256], bank[0:32, 0:256], [bk], [sgk])
                S.dma("sp", cs_o[:, q4 * 512:q4 * 512 + 256], stg[0:32, 0:256], sgk, reads=[sgk])
                stg2 = CA[0] if q4 % 2 == 0 else CB[0]
                sgk2 = "ca0" if q4 % 2 == 0 else "cb0"
                cp("act", stg2[0:32, 0:256], bank[0:32, 256:512], [bk], [sgk2])
                S.dma("sp", cs_o[:, q4 * 512 + 256:q4 * 512 + 512], stg2[0:32, 0:256], sgk2, reads=[sgk2])

        ffn_phase()

        S.finish()
    return nc


def _tables(pos):
    pos = np.asarray(pos)
    T = pos.shape[0]
    pf = pos.astype(np.float32)
    out = np.zeros((T, NTAB), np.float32)
    ang = (1.0 / (np.float32(10000.0) ** np.linspace(0.0, 1.0, 64, dtype=np.float32))).astype(np.float32)
    ang = np.repeat(ang, 2)
    th = (pf[:, None] * ang[None, :]).astype(np.float32)
    c = np.cos(th.astype(np.float64))
    s = np.sin(th.astype(np.float64))
    sgn = np.tile(np.array([-1.0, 1.0]), 64)[None, :]
    out[:, 0:128] = c
    out[:, 128:256] = s * sgn
    sc = 128.0 ** -0.5
    out[:, 256:384] = c * sc
    out[:, 384:512] = s * sgn * sc
    half = 8
    inv = (1.0 / (np.float32(500000.0) ** (np.arange(half, dtype=np.float32) / np.float32(half)))).astype(np.float32)
    a = (pf[:, None] * inv[None, :]).astype(np.float32)
    ca = np.cos(a.astype(np.float64))
    sa = np.sin(a.astype(np.float64))
    out[:, 512:520] = ca
    out[:, 520:528] = ca
    out[:, 528:536] = -sa
    out[:, 536:544] = sa
    return out


def _consts(is_b):
    cst = np.zeros((128, 1280), np.float64)
    i = np.arange(128)
    for h in range(4):
        g = GAM[h]
        cst[:, h * 128:(h + 1) * 128] = (g ** (i + 1.0))[None, :]
        cst[:, 512 + h * 128:512 + (h + 1) * 128] = (g ** (-(i + 1.0)))[None, :]
        cst[:, 1024 + h] = g ** (127.0 - i)
    cst[:, 1152:1280] = np.eye(128)
    cbf = np.zeros((128, 1024), np.float32)
    cbf[:, 0:128] = np.eye(128)
    cbf[:, 128:192] = 1.0
    k = np.arange(128)[:, None]
    q = np.arange(128)[None, :]
    cbf[:, 256:384] = (k <= q)
    cbf[:, 384:512] = (k > q)
    cbf[:, 512:640] = (k > q) if is_b else 0.0
    cbf[:, 640:768] = 1.0
    cbf[:, 768:772] = (np.arange(128)[:, None] > np.arange(4)[None, :])
    kk = np.arange(64)[:, None]
    qq = np.arange(64)[None, :]
    cbf[0:64, 832:896] = ((kk // 4) == (qq // 4)) & ((kk % 4) <= (qq % 4))
    cs2 = np.zeros((128, 544), np.float64)
    tok = np.arange(64)
    for h in range(4):
        g = GAM[h]
        cs2[:, h * 64:(h + 1) * 64] = (g ** ((tok % 4) + 1.0))[None, :]
        dm = np.where(((kk // 4) == (qq // 4)) & ((qq % 4) >= (kk % 4)), g ** ((qq % 4) - (kk % 4)).astype(np.float64), 0.0)
        cs2[0:64, 256 + h * 64:256 + (h + 1) * 64] = dm
        cs2[0:64, 512 + h] = g ** (3.0 - (tok % 4))
    cs2[0:64, 516:532] = ((tok[:, None] // 4) == np.arange(16)[None, :])
    return cst.astype(np.float32), cbf.astype(ml_dtypes.bfloat16), cs2.astype(np.float32)


_NC_CACHE = {}


def kernel(x_prompt, x_sample, cache_k, cache_v, state_ret, state_conv,
           w_in, attn_sinks, w_a_proj, w_r_proj, w_o,
           g_pre_mix, g_post_mix, g_pre_ffn, g_post_ffn,
           w_up, conv_w, conv_b, w_down):
    f = lambda a: np.ascontiguousarray(np.asarray(a, dtype=np.float32))
    x_prompt = f(x_prompt); x_sample = f(x_sample)
    cache_k = f(cache_k); cache_v = f(cache_v); state_ret = f(state_ret); state_conv = f(state_conv)
    if "nc" not in _NC_CACHE:
        _NC_CACHE["nc"] = build_program()
    nc = _NC_CACHE["nc"]

    wa = f(w_a_proj)[0].reshape(8, 64, D)
    wa_l = np.zeros((128, 4, D), np.float32)
    for j in range(4):
        wa_l[0:64, j] = wa[j]
        wa_l[64:128, j] = wa[4 + j]
    gvT = np.concatenate([f(g_pre_mix)[0].reshape(8, 128).T, f(g_pre_ffn)[0].reshape(8, 128).T], axis=1)
    gpost = np.stack([f(g_post_mix)[0], f(g_post_ffn)[0]])
    cw = f(conv_w)[0]
    cb = f(conv_b)[0]
    cwT = np.zeros((128, 48, 4), np.float32)
    for t in range(48):
        cwT[:, t, 0:3] = cw[:, t * 128:(t + 1) * 128].T
        cwT[:, t, 3] = cb[t * 128:(t + 1) * 128]
    cwT = cwT.reshape(128, 192)
    sk = f(attn_sinks)[0]
    sinks = np.zeros((128, 4), np.float32)
    sinks[0:64, :] = sk[0:4][None, :]
    sinks[64:128, :] = sk[4:8][None, :]

    in_maps = []
    for c in range(8):
        b, half = c // 2, c % 2
        xe = np.zeros((32 * 128, D), np.float32)
        if half == 0:
            xe[16 * 128:] = x_prompt[b, 0:2048]
            pos = np.concatenate([np.arange(-2048, 2048), 16384 + np.tile(np.arange(4), 16), np.zeros(64, np.int64)])
        else:
            xe[:] = x_prompt[b]
            pos = np.concatenate([np.arange(0, 4096), 16384 + np.tile(np.arange(4), 16), np.zeros(64, np.int64)])
        cst, cbf, cs2 = _consts(half == 1)
        in_maps.append({
            "xe": xe,
            "xsm": x_sample[16 * c:16 * (c + 1)].reshape(64, D),
            "ck": cache_k[0, 16 * c:16 * (c + 1)].reshape(16, 128, 128),
            "cv": cache_v[0, 16 * c:16 * (c + 1)].reshape(16, 128, 128),
            "sr": state_ret[0, 16 * c:16 * (c + 1)],
            "scv": state_conv[0, 16 * c:16 * (c + 1)].reshape(32, F2),
            "w_in": f(w_in)[0], "w_a": wa_l, "w_r": f(w_r_proj)[0], "w_o": f(w_o)[0],
            "w_up": f(w_up)[0], "w_dn": f(w_down)[0],
            "gvT": np.ascontiguousarray(gvT), "gpost": gpost, "cwT": cwT, "sinks": sinks,
            "tabs": _tables(pos), "cst": cst, "cbf": cbf, "cs2": cs2,
        })
    res = run_bass_kernel_spmd(nc, in_maps, core_ids=list(range(8)))
    R = res.results
    y = np.zeros((4, 4096, D), np.float32)
    ys = np.zeros((128, 4, D), np.float32)
    kp = np.zeros((1, 4, 128, 2, 64), np.float32)
    vp = np.zeros((1, 4, 128, 2, 64), np.float32)
    rp = np.zeros((1, 4, 4, 128, 256), np.float32)
    cpo = np.zeros((1, 4, 2, F2), np.float32)
    kso = np.zeros((1, 128, 128, 2, 64), np.float32)
    vso = np.zeros((1, 128, 128, 2, 64), np.float32)
    rso = np.zeros((1, 128, 4, 128, 256), np.float32)
    cso = np.zeros((1, 128, 2, F2), np.float32)
    for c in range(8):
        b, half = c // 2, c % 2
        r = R[c]
        y[b, half * 2048:(half + 1) * 2048] = r["y_o"]
        ys[16 * c:16 * (c + 1)] = r["ys_o"].reshape(16, 4, D)
        if half == 1:
            kp[0, b] = r["kp_o"].reshape(128, 2, 64)
            vp[0, b] = r["vp_o"].reshape(128, 2, 64)
            rp[0, b] = r["rp_o"]
            cpo[0, b] = r["cp_o"]
        kso[0, 16 * c:16 * (c + 1)] = r["ks_o"].reshape(16, 128, 2, 64)
        vso[0, 16 * c:16 * (c + 1)] = r["vs_o"].reshape(16, 128, 2, 64)
        rso[0, 16 * c:16 * (c + 1)] = r["rs_o"]
        cso[0, 16 * c:16 * (c + 1)] = r["cs_o"].reshape(16, 2, F2)
    return (y, ys, kp, vp, rp, cpo, kso, vso, rso, cso)
```

```python
import numpy as np
from contextlib import ExitStack
import ml_dtypes
import concourse.bass as bass
import concourse.mybir as mybir
from concourse.bass_utils import run_bass_kernel_spmd

F32 = mybir.dt.float32
BF16 = mybir.dt.bfloat16
U8 = mybir.dt.uint8
AF = mybir.ActivationFunctionType
ALU = mybir.AluOpType

D = 1024
DIN = 5888
F2 = 6144
DFF = 3072
EPS = 1e-6
NPRE = 15
NMAIN = 17
NTAB = 544
DEBUG_X1 = False
STAGE = 99
CUT = 0
SCUT = 0
GAM = [1.0 - 2.0 ** (-5 - h) for h in range(4)]


class Sched:
    def __init__(self, nc, stack):
        self.nc = nc
        self.stack = stack
        self.eng = {"pe": nc.tensor, "act": nc.scalar, "dve": nc.vector,
                    "pool": nc.gpsimd, "sp": nc.sync}
        self.queues = {e: [] for e in self.eng}
        self.sems = {}
        self.cnt = {}
        self.waited = {e: {} for e in self.eng}
        self.last_w = {}
        self.readers = {}

    def sem(self, name):
        if name not in self.sems:
            self.sems[name] = self.stack.enter_context(
                self.nc.semaphore("s_" + name.replace(":", "_")))
            self.cnt[name] = 0
        return self.sems[name]

    def _deps(self, reads, writes):
        deps = set()
        for k in reads:
            t = self.last_w.get(k)
            if t is not None:
                deps.add(t)
        for k in writes:
            t = self.last_w.get(k)
            if t is not None:
                deps.add(t)
            for t in self.readers.get(k, ()):
                deps.add(t)
        return deps

    def _commit(self, token, reads, writes):
        for k in reads:
            self.readers.setdefault(k, set()).add(token)
        for k in writes:
            self.last_w[k] = token
            self.readers[k] = set()

    def _waits(self, e, deps):
        best = {}
        for (s, v) in deps:
            if e == "pe" and s == "pe":
                continue
            if best.get(s, 0) < v:
                best[s] = v
        w = []
        for s, v in best.items():
            if self.waited[e].get(s, 0) < v:
                self.waited[e][s] = v
                w.append((s, v))
        return w

    def op(self, e, fn, reads=(), writes=(), inc=True):
        self.sem(e)
        deps = self._deps(reads, writes)
        waits = self._waits(e, deps)
        if inc:
            self.cnt[e] += 1
            token = (e, self.cnt[e])
        else:
            token = (e, self.cnt[e] + 1)
        self.queues[e].append((waits, fn, (e, 1) if inc else None))
        self._commit(token, reads, writes)
        return token

    def dma(self, q, out, in_, semkey, reads=(), writes=(), **kw):
        s = "d:" + semkey
        self.sem(s)
        deps = self._deps(reads, writes)
        waits = self._waits(q, deps)
        self.cnt[s] += 16
        token = (s, self.cnt[s])
        self.queues[q].append((waits, lambda eng: eng.dma_start(out=out, in_=in_, **kw), (s, 16)))
        self._commit(token, reads, writes)
        return token

    def dma_multi(self, q, items, semkey, **kw):
        s = "d:" + semkey
        self.sem(s)
        keys = []
        for (out, in_, key) in items:
            deps = self._deps((), [key])
            waits = self._waits(q, deps)
            self.cnt[s] += 16
            self.queues[q].append((waits, (lambda o, i: (lambda eng: eng.dma_start(out=o, in_=i, **kw)))(out, in_), (s, 16)))
            keys.append(key)
        token = (s, self.cnt[s])
        for key in keys:
            self.last_w[key] = token
            self.readers[key] = set()
        return token

    def barrier(self):
        for e in ("pe", "act", "dve", "pool"):
            self.sem(e)
        tot = dict(self.cnt)
        for e in self.eng:
            waits = []
            for s, v in tot.items():
                if v > 0 and self.waited[e].get(s, 0) < v and s != e:
                    self.waited[e][s] = v
                    waits.append((s, v))
            self.queues[e].append((waits, None, None))
        self.last_w = {}
        self.readers = {}

    def finish(self):
        nc = self.nc
        final = []
        for s, v in self.cnt.items():
            if v > 0 and self.waited["sp"].get(s, 0) < v:
                final.append((s, v))
        with nc.Block() as block:
            def emit(e):
                def body(eng):
                    for waits, fn, inc in self.queues[e]:
                        for (s, v) in waits:
                            eng.wait_ge(self.sems[s], v)
                        if fn is None:
                            continue
                        ins = fn(eng)
                        if inc is not None:
                            ins.then_inc(self.sems[inc[0]], inc[1])
                    if e == "sp":
                        for (s, v) in final:
                            eng.wait_ge(self.sems[s], v)
                return body
            block.tensor(emit("pe"))
            block.scalar(emit("act"))
            block.vector(emit("dve"))
            block.gpsimd(emit("pool"))
            block.sync(emit("sp"))


class Arena:
    def __init__(self, ap_u8, size):
        self.a = ap_u8
        self.size = size
        self.off = 0

    def alloc(self, parts, free, dt):
        esz = 4 if dt == F32 else 2
        n = 1
        for f in free:
            n *= f
        nb = n * esz
        nb_al = (nb + 63) // 64 * 64
        assert self.off + nb_al <= self.size, f"SBUF arena overflow {self.off}+{nb_al}>{self.size}"
        v = self.a[0:parts, self.off:self.off + nb].bitcast(dt)
        self.off += nb_al
        if len(free) == 2:
            v = v.rearrange("p (a b) -> p a b", a=free[0])
        elif len(free) == 3:
            v = v.rearrange("p (a b c) -> p a b c", a=free[0], b=free[1])
        return v


def build_program():
    nc = bass.Bass("TRN2", target_bir_lowering=False)

    def din(name, shape, dt=F32):
        return nc.dram_tensor(name, list(shape), dt, kind="ExternalInput").ap()

    def dout(name, shape):
        return nc.dram_tensor(name, list(shape), F32, kind="ExternalOutput").ap()

    xe = din("xe", [32 * 128, D])
    xsm = din("xsm", [64, D])
    ck = din("ck", [16, 128, 128])
    cv = din("cv", [16, 128, 128])
    sr = din("sr", [16, 4, 128, 256])
    scv = din("scv", [32, F2])
    w_in = din("w_in", [D, DIN])
    w_a = din("w_a", [128, 4, D])
    w_r = din("w_r", [D, D])
    w_o = din("w_o", [D, D])
    w_up = din("w_up", [D, F2])
    w_dn = din("w_dn", [DFF, D])
    gvT = din("gvT", [128, 16])
    gpost = din("gpost", [2, D])
    cwT = din("cwT", [128, 48 * 4])
    sinks = din("sinks", [128, 4])
    tabs = din("tabs", [33 * 128, NTAB])
    cst = din("cst", [128, 1280])
    cbf = din("cbf", [128, 1024], BF16)
    cs2 = din("cs2", [128, 544])

    y_o = dout("y_o", [16 * 128, D])
    ys_o = dout("ys_o", [64, D])
    kp_o = dout("kp_o", [128, 128])
    vp_o = dout("vp_o", [128, 128])
    rp_o = dout("rp_o", [4, 128, 256])
    cp_o = dout("cp_o", [2, F2])
    ks_o = dout("ks_o", [16, 128, 128])
    vs_o = dout("vs_o", [16, 128, 128])
    rs_o = dout("rs_o", [16, 4, 128, 256])
    cs_o = dout("cs_o", [32, F2])

    x1s = nc.dram_tensor("x1s", [NMAIN * 128 + 64, D], F32, kind="Internal").ap()
    wupbf = nc.dram_tensor("wupbf", [D, F2], BF16, kind="Internal").ap()
    wdnbf = nc.dram_tensor("wdnbf", [DFF, D], BF16, kind="Internal").ap()

    with ExitStack() as st:
        S = Sched(nc, st)
        ARENA = 212800
        arena_t = st.enter_context(nc.sbuf_tensor("arena", [128, ARENA], U8))
        A = Arena(arena_t, ARENA)
        psb = [st.enter_context(nc.psum_tensor(f"ps{i}", [128, 512], F32)) for i in range(8)]
        psctr = [0]

        pinned = set()

        def ps(pin=False):
            while True:
                i = psctr[0] % 8
                psctr[0] += 1
                if i not in pinned:
                    break
            if pin:
                pinned.add(i)
            return psb[i], f"ps{i}"

        def unpin(bk):
            pinned.discard(int(bk[2:]))

        def act(out, in_, func, reads, writes, scale=1.0, accum=None):
            if accum is None:
                S.op("act", lambda e: e.activation(out=out, in_=in_, func=func, scale=scale), reads, writes)
            else:
                S.op("act", lambda e: e.activation(out=out, in_=in_, func=func, scale=scale, accum_out=accum), reads, writes)

        def tt(eng, out, in0, in1, op, reads, writes):
            S.op(eng, lambda e: e.tensor_tensor(out=out, in0=in0, in1=in1, op=op), reads, writes)

        def ts(eng, out, in0, s1, s2, op0, op1, reads, writes):
            S.op(eng, lambda e: e.tensor_scalar(out=out, in0=in0, scalar1=s1, scalar2=s2, op0=op0, op1=op1), reads, writes)

        def stt(out, in0, scalar, in1, op0, op1, reads, writes):
            S.op("dve", lambda e: e.scalar_tensor_tensor(out=out, in0=in0, scalar=scalar, in1=in1, op0=op0, op1=op1), reads, writes)

        def cp(eng, out, in_, reads, writes):
            if eng == "act":
                S.op("act", lambda e: e.copy(out=out, in_=in_), reads, writes)
            else:
                S.op(eng, lambda e: e.tensor_copy(out=out, in_=in_), reads, writes)

        def mm(out, lhsT, rhs, start, stop, reads, writes, inc=None):
            S.op("pe", lambda e: e.matmul(out, lhsT, rhs, start=start, stop=stop), reads, writes,
                 inc=(stop if inc is None else inc))

        def tp(out, in_, ident, reads, writes, inc):
            S.op("pe", lambda e: e.transpose(out, in_, ident), reads, writes, inc=inc)

        def bc(ap, shape, axis):
            return ap.unsqueeze(axis).to_broadcast(shape)

        CST = A.alloc(128, [1280], F32)
        CBF = A.alloc(128, [1024], BF16)
        GVT = A.alloc(128, [16], F32)
        GPM = A.alloc(128, [D], F32)
        SNK = A.alloc(128, [4], F32)
        ESK = A.alloc(128, [4], F32)
        CN05 = A.alloc(128, [4], F32)
        S.dma("sp", CST, cst, "cst", writes=["cst"])
        S.dma("sp", CBF, cbf, "cbf", writes=["cbf"])
        S.dma("sp", GVT, gvT, "gvt", writes=["gvt"])
        S.dma("sp", GPM, gpost[0:1, :].to_broadcast([128, D]) if False else gpost[0].partition_broadcast(128), "gpm", writes=["gpm"])
        S.dma("sp", SNK, sinks, "snk", writes=["snk"])
        act(ESK, SNK, AF.Exp, ["snk"], ["esk"])
        S.op("pool", lambda e: e.memset(CN05, -0.5), (), ["cn05"])
        EPSC = A.alloc(128, [4], F32)
        S.op("dve", lambda e: e.memset(EPSC, EPS), (), ["epsc"])
        dqT = CST[:, 0:512]
        dkT = CST[:, 512:1024]
        ktok = CST[:, 1024:1028]
        idf = CST[:, 1152:1280]
        ident = CBF[:, 0:128]
        ones64 = CBF[:, 128:192]
        mown = CBF[:, 256:384]
        mprev = CBF[:, 384:512]
        mprev1 = CBF[:, 512:640]

        mark_w = A.off
        Win = A.alloc(128, [8, DIN], BF16)
        X2buf = A.alloc(128, [D], F32)
        Wa = A.alloc(128, [4, D], BF16)
        Wr = A.alloc(128, [8, D], BF16)
        Wo = A.alloc(128, [8, D], BF16)
        itemsA, itemsB = [], []
        for k in range(8):
            for (c0, c1) in ((512, 768), (1280, 2048), (2048, 2816)):
                itemsA.append((Win[:, k, c0:c1], w_in[k * 128:(k + 1) * 128, c0:c1], "winA"))
        for k in range(8):
            for (c0, c1) in ((0, 512), (768, 1280), (2816, 3840), (3840, 4864), (4864, 5888)):
                itemsB.append((Win[:, k, c0:c1], w_in[k * 128:(k + 1) * 128, c0:c1], "winB"))
        S.dma_multi("pool", itemsA, "winA")
        S.dma_multi("pool", itemsB, "winB")
        S.dma_multi("pool", [(Wa[:, j, :], w_a[:, j, :], "wa") for j in range(4)], "wa")
        S.dma_multi("pool", [(Wr[:, k, :], w_r[k * 128:(k + 1) * 128, :], "wr") for k in range(8)], "wr")
        S.dma_multi("pool", [(Wo[:, k, :], w_o[k * 128:(k + 1) * 128, :], "wo") for k in range(8)], "wo")
        mark_mix = A.off
        win_b_pending = [True]

        X = [A.alloc(128, [D], F32) for _ in range(2)] + [X2buf]
        TAB = [A.alloc(128, [NTAB], F32) for _ in range(2)]
        xs = A.alloc(128, [D], BF16)
        xnT = A.alloc(128, [8, 128], BF16)
        qa = A.alloc(128, [512], BF16)
        qaT = A.alloc(128, [512], BF16)
        ka32 = A.alloc(128, [128], F32)
        kabf = A.alloc(128, [128], BF16)
        kT = [A.alloc(128, [128], BF16) for _ in range(2)]
        va32 = A.alloc(128, [128], F32)
        vbf = [A.alloc(128, [128], BF16) for _ in range(2)]
        t1 = A.alloc(128, [512], F32)
        t2 = A.alloc(128, [512], F32)
        qr = A.alloc(128, [512], BF16)
        kr = A.alloc(128, [512], BF16)
        ktl = A.alloc(128, [512], BF16)
        qrT = A.alloc(128, [512], BF16)
        krT = A.alloc(128, [512], BF16)
        vr = A.alloc(128, [1024], BF16)
        sg = A.alloc(128, [1024], BF16)
        PT = [A.alloc(128, [512], BF16) for _ in range(4)]
        oaT = A.alloc(128, [512], BF16)
        AT = A.alloc(128, [512], BF16)
        S32 = A.alloc(128, [1024], F32)
        Sbf = A.alloc(128, [1024], BF16)
        og = A.alloc(128, [1024], BF16)
        ogT = A.alloc(128, [8, 128], BF16)
        tha = A.alloc(128, [1024], BF16)
        thr = A.alloc(128, [1024], BF16)
        st8 = A.alloc(128, [32], F32)
        tA1 = A.alloc(128, [8, 16], F32)
        tA2 = A.alloc(128, [8, 16], F32)

        def load_x(slot, src_rows, tab_rows, np_=128):
            S.dma("sp", X[slot][:np_], src_rows, f"x{slot}", writes=[f"x{slot}"])
            S.dma("sp", TAB[slot][:np_], tab_rows, f"tab{slot}", writes=[f"tab{slot}"])

        def norm_part(slot, np_, a=True, b=True, nopool=False):
            xb = X[slot]
            xk = f"x{slot}"
            if a and nopool:
                act(xs[:np_], xb[:np_], AF.Square, [xk], ["xs", "ssq"], accum=st8[:np_, 0:1])
                S.op("act", lambda e: e.activation(out=st8[:np_, 1:2], in_=st8[:np_, 0:1], func=AF.Sqrt, scale=1.0 / D, bias=EPSC[:np_, 0:1]),
                     ["ssq", "epsc"], ["ms"])
                S.op("dve", lambda e: e.reciprocal(out=st8[:np_, 2:3], in_=st8[:np_, 1:2]), ["ms"], ["rstd"])
            elif a:
                act(xs[:np_], xb[:np_], AF.Square, [xk], ["xs", "ssq"], accum=st8[:np_, 0:1])
                ts("dve", st8[:np_, 1:2], st8[:np_, 0:1], 1.0 / D, EPS, ALU.mult, ALU.add, ["ssq"], ["ms"])
                tt("pool", st8[:np_, 2:3], st8[:np_, 1:2], CN05[:np_, 0:1], ALU.pow, ["ms", "cn05"], ["rstd"])
            if b:
                act(xs[:np_], xb[:np_], AF.Copy, [xk, "rstd"], ["xs"], scale=st8[:np_, 2:3])

        def norm_T(slot, np_, gcol0, wkeys=("gvt",), do_part=True):
            if do_part:
                norm_part(slot, np_)
            bank, bk = ps()
            pb = bank[:].bitcast(BF16)
            for k in range(8):
                tp(pb[:, k * 128:k * 128 + np_], xs[:np_, k * 128:(k + 1) * 128], ident[:np_, :np_],
                   ["xs", "cbf"], [bk], inc=(k == 7))
            pv = pb.rearrange("p (k t) -> p k t", k=8)[:, :, :np_]
            tt("dve", xnT[:, :, :np_], pv, bc(GVT[:, gcol0:gcol0 + 8], [128, 8, np_], 2), ALU.mult,
               [bk, "gvt"], ["xnT"])

        def inproj(c0, n, np_):
            bank, bk = ps()
            for k in range(8):
                mm(bank[:np_, :n], xnT[:, k, :np_], Win[:, k, c0:c0 + n], k == 0, k == 7,
                   ["xnT", "winA" if (512 <= c0 < 768 or 1280 <= c0 < 2816) else "winB"], [bk])
            return bank, bk

        def rope_small(psv, nh, np_, tab, outv, bk, okey):
            cosA = tab[:np_, 512:528]
            ssinA = tab[:np_, 528:544]
            a1 = tA1[:np_, :nh, :]
            a2 = tA2[:np_, :nh, :]
            tk = [bk, "tabcur"]
            tt("dve", a1, psv[:, :, 0:16], bc(cosA, [np_, nh, 16], 1), ALU.mult, tk, ["tA1"])
            tt("dve", a2[:, :, 0:8], psv[:, :, 8:16], bc(ssinA[:, 0:8], [np_, nh, 8], 1), ALU.mult, tk, ["tA2"])
            tt("dve", a2[:, :, 8:16], psv[:, :, 0:8], bc(ssinA[:, 8:16], [np_, nh, 8], 1), ALU.mult, tk, ["tA2b"])
            tt("dve", outv, a1, a2, ALU.add, ["tA1", "tA2", "tA2b"], [okey])

        def rope_ret(bank, bk, np_, tab, tcol, outbf, okey):
            cos = tab[:np_, tcol:tcol + 128]
            ssin = tab[:np_, tcol + 128:tcol + 256].rearrange("p (i two) -> p i two", two=2)
            pv = bank[:np_, 0:512].rearrange("p (h d) -> p h d", h=4)
            pp = bank[:np_, 0:512].rearrange("p (h i two) -> p h i two", h=4, two=2)
            t1v = t1[:np_].rearrange("p (h d) -> p h d", h=4)
            t2p = t2[:np_].rearrange("p (h i two) -> p h i two", h=4, two=2)
            tk = [bk, "tabcur"]
            tt("dve", t1v, pv, bc(cos, [np_, 4, 128], 1), ALU.mult, tk, ["t1"])
            tt("dve", t2p[:, :, :, 0], pp[:, :, :, 1], bc(ssin[:, :, 0], [np_, 4, 64], 1), ALU.mult, tk, ["t2a"])
            tt("dve", t2p[:, :, :, 1], pp[:, :, :, 0], bc(ssin[:, :, 1], [np_, 4, 64], 1), ALU.mult, tk, ["t2b"])
            tt("dve", outbf[:np_], t1[:np_], t2[:np_], ALU.add, ["t1", "t2a", "t2b"], [okey])

        def kv_tiles(slot, tabslot, np_, ka_out=True):
            tab = TAB[tabslot]
            S.readers.setdefault("tabcur", set())
            bank, bk = inproj(512, 256, np_)
            cp("act", ka32[:np_], bank[:np_, 0:128], [bk], ["ka32"])
            rope_small(bank[:np_, 0:128].rearrange("p (h d) -> p h d", h=2), 2, np_, tab,
                       ka32[:np_].rearrange("p (h d) -> p h d", h=2)[:, :, 0:16], bk, "ka32")
            cp("act", va32[:np_], bank[:np_, 128:256], [bk], ["va32"])
            cp("dve", kabf[:np_], ka32[:np_], ["ka32"], ["kabf"])
            cp("act", vbf[slot][:np_], va32[:np_], ["va32"], [f"vbf{slot}"])

        def state_update(np_=128):
            sbanks = [ps(), ps()]
            for h in range(4):
                sb, sk = sbanks[h // 2]
                c = (h % 2) * 256
                mm(sb[:, c:c + 256], ktl[:np_, h * 128:(h + 1) * 128], vr[:np_, h * 256:(h + 1) * 256], True, True,
                   ["ktl", "vr"], [sk])
            for h in range(4):
                sb, sk = sbanks[h // 2]
                c = (h % 2) * 256
                stt(S32[:, h * 256:(h + 1) * 256], S32[:, h * 256:(h + 1) * 256], float(GAM[h] ** 128),
                    sb[:, c:c + 256], ALU.mult, ALU.add, [sk, "S32"], ["S32"])

        def kr_vr_tiles(tabslot, np_=128):
            tab = TAB[tabslot]
            bank, bk = inproj(1280, 512, np_)
            rope_ret(bank, bk, np_, tab, 256, kr, "kr")
            tt("pool", ktl[:np_].rearrange("p (h d) -> p h d", h=4), kr[:np_].rearrange("p (h d) -> p h d", h=4),
               bc(ktok[:np_], [np_, 4, 128], 2), ALU.mult, ["kr", "cst"], ["ktl"])
            for n in range(2):
                bank, bk = inproj(1792 + n * 512, 512, np_)
                cp("act", vr[:np_, n * 512:(n + 1) * 512], bank[:np_, 0:512], [bk], ["vr"])


        S.op("dve", lambda e: e.memset(S32, 0.0), (), ["S32"])

        def load_blk(slot, gb):
            S.dma("sp", X[slot], xe[gb * 128:(gb + 1) * 128, :], f"x{slot}", writes=[f"x{slot}"])
            S.dma("sp", TAB[slot], tabs[gb * 128:(gb + 1) * 128, :], f"tab{slot}", writes=[f"tab{slot}"])

        cur = {"tab": None}

        def pxi(p_):
            return (p_ + 1) % 3

        def pload_x(p_):
            S.dma("sp", X[pxi(p_)], xe[p_ * 128:(p_ + 1) * 128, :], f"x{pxi(p_)}", writes=[f"x{pxi(p_)}"])

        def pload_tab(p_):
            S.dma("sp", TAB[p_ % 2], tabs[p_ * 128:(p_ + 1) * 128, :], f"tab{p_ % 2}", writes=[f"tab{p_ % 2}"])

        pload_x(0)
        pload_tab(0)
        pload_x(1)
        norm_part(pxi(0), 128, nopool=True)
        norm_T(pxi(0), 128, 0, do_part=False)
        for p in range(NPRE):
            slot = p % 2
            pload_tab(p + 1)
            if p + 2 <= NPRE:
                pload_x(p + 2)
            TK = f"tab{slot}"
            if p + 1 < NPRE:
                norm_part(pxi(p + 1), 128, nopool=True)
            tab = TAB[slot]
            bank, bk = inproj(1280, 512, 128)
            cos = tab[:, 256:384]
            rope_ret_keys = [bk, TK]
            ssin = tab[:, 384:512].rearrange("p (i two) -> p i two", two=2)
            pv = bank[:, 0:512].rearrange("p (h d) -> p h d", h=4)
            pp = bank[:, 0:512].rearrange("p (h i two) -> p h i two", h=4, two=2)
            t1v = t1.rearrange("p (h d) -> p h d", h=4)
            t2p = t2.rearrange("p (h i two) -> p h i two", h=4, two=2)
            tt("dve", t1v, pv, bc(cos, [128, 4, 128], 1), ALU.mult, rope_ret_keys, ["t1"])
            tt("dve", t2p[:, :, :, 0], pp[:, :, :, 1], bc(ssin[:, :, 0], [128, 4, 64], 1), ALU.mult, rope_ret_keys, ["t2a"])
            tt("dve", t2p[:, :, :, 1], pp[:, :, :, 0], bc(ssin[:, :, 1], [128, 4, 64], 1), ALU.mult, rope_ret_keys, ["t2b"])
            tt("dve", kr, t1, t2, ALU.add, ["t1", "t2a", "t2b"], ["kr"])
            tt("dve", ktl.rearrange("p (h d) -> p h d", h=4), kr.rearrange("p (h d) -> p h d", h=4),
               bc(ktok, [128, 4, 128], 2), ALU.mult, ["kr", "cst"], ["ktl"])
            for n in range(2):
                bank, bk = inproj(1792 + n * 512, 512, 128)
                cp("act", vr[:, n * 512:(n + 1) * 512], bank[:, 0:512], [bk], ["vr"])
            if p + 1 < NPRE:
                norm_T(pxi(p + 1), 128, 0, do_part=False)
            state_update()
            if p == NPRE - 1:
                bank, bk = inproj(512, 256, 128)
                cp("act", ka32, bank[:, 0:128], [bk], ["ka32"])
                psv = bank[:, 0:128].rearrange("p (h d) -> p h d", h=2)
                cosA = tab[:, 512:528]
                ssinA = tab[:, 528:544]
                tk = [bk, TK]
                tt("dve", tA1[:, :2, :], psv[:, :, 0:16], bc(cosA, [128, 2, 16], 1), ALU.mult, tk, ["tA1"])
                tt("dve", tA2[:, :2, 0:8], psv[:, :, 8:16], bc(ssinA[:, 0:8], [128, 2, 8], 1), ALU.mult, tk, ["tA2"])
                tt("dve", tA2[:, :2, 8:16], psv[:, :, 0:8], bc(ssinA[:, 8:16], [128, 2, 8], 1), ALU.mult, tk, ["tA2b"])
                tt("dve", ka32.rearrange("p (h d) -> p h d", h=2)[:, :, 0:16], tA1[:, :2, :], tA2[:, :2, :], ALU.add,
                   ["tA1", "tA2", "tA2b", "ka32"], ["ka32"])
                cp("dve", kabf, ka32, ["ka32"], ["kabf"])
                cp("act", vbf[1], bank[:, 128:256], [bk], ["vbf1"])
                bank2, bk2 = ps()
                pb2 = bank2[:].bitcast(BF16)
                tp(pb2[:, 0:128], kabf, ident, ["kabf", "cbf"], [bk2], inc=True)
                cp("act", kT[1], pb2[:, 0:128], [bk2], ["kT1"])
        cp("act", Sbf, S32, ["S32"], ["Sbf"])

        def blk_slot(m):
            return (NPRE + m) % 2

        def tile_qa(m):
            slot = blk_slot(m)
            tab = TAB[slot]
            TK = f"tab{slot}"
            bank, bk = inproj(0, 512, 128)
            cp("act", t1, bank[:, 0:512], [bk], ["t1"])
            for g in range(2):
                cp("pool", qa.rearrange("p (j g d) -> p j g d", j=4, g=2)[:, :, g, :],
                   t1[:, g * 256:(g + 1) * 256].rearrange("p (j d) -> p j d", j=4), ["t1"], ["qa"])
            psv = t1.rearrange("p (h d) -> p h d", h=8)
            cosA = tab[:, 512:528]
            ssinA = tab[:, 528:544]
            tk = ["t1", TK]
            tt("dve", tA1, psv[:, :, 0:16], bc(cosA, [128, 8, 16], 1), ALU.mult, tk, ["tA1"])
            tt("dve", tA2[:, :, 0:8], psv[:, :, 8:16], bc(ssinA[:, 0:8], [128, 8, 8], 1), ALU.mult, tk, ["tA2"])
            tt("dve", tA2[:, :, 8:16], psv[:, :, 0:8], bc(ssinA[:, 8:16], [128, 8, 8], 1), ALU.mult, tk, ["tA2b"])
            for g in range(2):
                tt("dve", qa.rearrange("p (j g d) -> p j g d", j=4, g=2)[:, :, g, 0:16],
                   tA1[:, g * 4:(g + 1) * 4, :], tA2[:, g * 4:(g + 1) * 4, :], ALU.add,
                   ["tA1", "tA2", "tA2b", "qa"], ["qa"])

        def tile_kv(m):
            slot = blk_slot(m)
            tab = TAB[slot]
            TK = f"tab{slot}"
            cslot = m % 2
            cosA = tab[:, 512:528]
            ssinA = tab[:, 528:544]
            bank, bk = inproj(512, 256, 128)
            cp("act", ka32, bank[:, 0:128], [bk], ["ka32"])
            psv = bank[:, 0:128].rearrange("p (h d) -> p h d", h=2)
            tk = [bk, TK]
            tt("dve", tA1[:, :2, :], psv[:, :, 0:16], bc(cosA, [128, 2, 16], 1), ALU.mult, tk, ["tA1"])
            tt("dve", tA2[:, :2, 0:8], psv[:, :, 8:16], bc(ssinA[:, 0:8], [128, 2, 8], 1), ALU.mult, tk, ["tA2"])
            tt("dve", tA2[:, :2, 8:16], psv[:, :, 0:8], bc(ssinA[:, 8:16], [128, 2, 8], 1), ALU.mult, tk, ["tA2b"])
            tt("dve", ka32.rearrange("p (h d) -> p h d", h=2)[:, :, 0:16], tA1[:, :2, :], tA2[:, :2, :], ALU.add,
               ["tA1", "tA2", "tA2b", "ka32"], ["ka32"])
            cp("dve", kabf, ka32, ["ka32"], ["kabf"])
            cp("act", va32, bank[:, 128:256], [bk], ["va32"])
            cp("act", vbf[cslot], bank[:, 128:256], [bk], [f"vbf{cslot}"])

        def tile_rope(m, c0, tcol, dst, dkey):
            slot = blk_slot(m)
            tab = TAB[slot]
            TK = f"tab{slot}"
            bank, bk = inproj(c0, 512, 128)
            rk = [bk, TK]
            cos = tab[:, tcol:tcol + 128]
            ssin = tab[:, tcol + 128:tcol + 256].rearrange("p (i two) -> p i two", two=2)
            pv = bank[:, 0:512].rearrange("p (h d) -> p h d", h=4)
            pp = bank[:, 0:512].rearrange("p (h i two) -> p h i two", h=4, two=2)
            t1v = t1.rearrange("p (h d) -> p h d", h=4)
            t2p = t2.rearrange("p (h i two) -> p h i two", h=4, two=2)
            tt("dve", t1v, pv, bc(cos, [128, 4, 128], 1), ALU.mult, rk, ["t1"])
            tt("dve", t2p[:, :, :, 0], pp[:, :, :, 1], bc(ssin[:, :, 0], [128, 4, 64], 1), ALU.mult, rk, ["t2a"])
            tt("dve", t2p[:, :, :, 1], pp[:, :, :, 0], bc(ssin[:, :, 1], [128, 4, 64], 1), ALU.mult, rk, ["t2b"])
            tt("dve", dst, t1, t2, ALU.add, ["t1", "t2a", "t2b"], [dkey])

        def tile_qr(m):
            tile_rope(m, 768, 0, qr, "qr")

        def tile_kr(m):
            tile_rope(m, 1280, 256, kr, "kr")
            tt("pool", ktl.rearrange("p (h d) -> p h d", h=4), kr.rearrange("p (h d) -> p h d", h=4),
               bc(ktok, [128, 4, 128], 2), ALU.mult, ["kr", "cst"], ["ktl"])

        def tile_vr(m):
            for n in range(2):
                bank, bk = inproj(1792 + n * 512, 512, 128)
                cp("act", vr[:, n * 512:(n + 1) * 512], bank[:, 0:512], [bk], ["vr"])

        def tile_gate(m):
            for n in range(2):
                bank, bk = inproj(2816 + n * 512, 512, 128)
                act(og[:, n * 512:(n + 1) * 512], bank[:, 0:512], AF.Tanh, [bk], ["og"], scale=0.5)
                stt(sg[:, n * 512:(n + 1) * 512], og[:, n * 512:(n + 1) * 512], 1.0, bank[:, 0:512],
                    ALU.add, ALU.mult, [bk, "og"], ["sg"])

        def tile_gm(m):
            for n in range(2):
                bank, bk = inproj(3840 + n * 512, 512, 128)
                act(tha[:, n * 512:(n + 1) * 512], bank[:, 0:512], AF.Tanh, [bk], ["tha"], scale=0.5)
            for n in range(2):
                bank, bk = inproj(4864 + n * 512, 512, 128)
                act(thr[:, n * 512:(n + 1) * 512], bank[:, 0:512], AF.Tanh, [bk], ["thr"], scale=0.5)

        def head_norm(m):
            norm_T(blk_slot(m), 128, 0)

        def xi(m):
            return (m + 1) % 3

        def load_x(m):
            gb = NPRE + m
            S.dma("sp", X[xi(m)], xe[gb * 128:(gb + 1) * 128, :], f"x{xi(m)}", writes=[f"x{xi(m)}"])

        def load_tab(m):
            gb = NPRE + m
            sl = gb % 2
            S.dma("sp", TAB[sl], tabs[gb * 128:(gb + 1) * 128, :], f"tab{sl}", writes=[f"tab{sl}"])

        def head_norm_x(m, part=True, trans=True, a=True, b=True):
            if part:
                norm_part(xi(m), 128, a=a, b=b)
            if trans:
                norm_T(xi(m), 128, 0, do_part=False)

        tail_state = {}

        def p1(m):
            for n in range(2):
                bank, bk = ps()
                for j in range(4):
                    mm(bank[:, 0:512], oaT[:, j * 128:(j + 1) * 128], Wa[:, j, n * 512:(n + 1) * 512], j == 0, j == 3,
                       ["oaT", "wa"], [bk])
                stt(tha[:, n * 512:(n + 1) * 512], tha[:, n * 512:(n + 1) * 512], 1.0, bank[:, 0:512],
                    ALU.add, ALU.mult, [bk, "tha"], ["tha"])
                bank, bk = ps()
                for k in range(8):
                    mm(bank[:, 0:512], ogT[:, k, :], Wr[:, k, n * 512:(n + 1) * 512], k == 0, k == 7,
                       ["ogT", "wr"], [bk])
                stt(thr[:, n * 512:(n + 1) * 512], thr[:, n * 512:(n + 1) * 512], 1.0, bank[:, 0:512],
                    ALU.add, ALU.mult, [bk, "thr"], ["thr"])

        def p2(m):
            tt("dve", tha, tha, thr, ALU.add, ["tha", "thr"], ["tha"])
            bank, bk = ps()
            pb = bank[:].bitcast(BF16)
            for k in range(8):
                tp(pb[:, k * 128:(k + 1) * 128], tha[:, k * 128:(k + 1) * 128], ident, ["tha", "cbf"], [bk], inc=(k == 7))
            cp("act", ogT.rearrange("p k t -> p (k t)"), pb[:, 0:1024], [bk], ["ogT"])

        def p3(m):
            mb = []
            for n in range(2):
                bank, bk = ps(pin=True)
                for k in range(8):
                    mm(bank[:, 0:512], ogT[:, k, :], Wo[:, k, n * 512:(n + 1) * 512], k == 0, k == 7,
                       ["ogT", "wo"], [bk])
                mb.append((bank, bk))
                act(xs[:, n * 512:(n + 1) * 512], bank[:, 0:512], AF.Square, [bk], ["xs", f"ssm{n}"],
                    accum=st8[:, 16 + n:17 + n])
            tail_state["mb"] = mb

        def p4a(m):
            tt("dve", st8[:, 18:19], st8[:, 16:17], st8[:, 17:18], ALU.add, ["ssm0", "ssm1"], ["ssm"])
            ts("dve", st8[:, 19:20], st8[:, 18:19], 1.0 / D, 4.0 * EPS, ALU.mult, ALU.add, ["ssm"], ["msm"])
            tt("pool", st8[:, 20:21], st8[:, 19:20], CN05[:, 0:1], ALU.pow, ["msm", "cn05"], ["rstm"])

        def p4(m, last):
            mb = tail_state["mb"]
            xb = X[xi(m)]
            XK = f"x{xi(m)}"
            for n in range(2):
                bank, bk = mb[n]
                stt(bank[:, 0:512], bank[:, 0:512], st8[:, 20:21], GPM[:, n * 512:(n + 1) * 512], ALU.mult, ALU.mult,
                    [bk, "rstm", "gpm"], [bk])
                tt("dve", xb[:, n * 512:(n + 1) * 512], xb[:, n * 512:(n + 1) * 512], bank[:, 0:512], ALU.add, [XK, bk], [XK])
                unpin(bk)
            S.dma("sp", x1s[m * 128:(m + 1) * 128, :], xb, XK, reads=[XK], writes=[f"x1s{m}"])
            if last:
                S.dma("sp", kp_o, ka32, "ka32o", reads=["ka32"])
                S.dma("sp", vp_o, va32, "va32o", reads=["va32"])
            if m + 3 < NMAIN:
                load_x(m + 3)

        def mixer_block(m, last):
            pslot = (m - 1) % 2
            cslot = m % 2
            nxt = (not last)
            prev = m >= 1
            if m + 2 < NMAIN:
                load_tab(m + 2)
            bank, bk = ps()
            pb = bank[:].bitcast(BF16)
            for j in range(4):
                tp(pb[:, j * 128:(j + 1) * 128], qa[:, j * 128:(j + 1) * 128], ident, ["qa", "cbf"], [bk], inc=False)
            tp(pb[:, 512:640], kabf, ident, ["kabf", "cbf"], [bk], inc=True)
            cp("act", qaT, pb[:, 0:512], [bk], ["qaT"])
            cp("act", kT[cslot], pb[:, 512:640], [bk], [f"kT{cslot}"])
            bank, bk = ps()
            pb = bank[:].bitcast(BF16)
            for h in range(4):
                tp(pb[:, h * 128:(h + 1) * 128], qr[:, h * 128:(h + 1) * 128], ident, ["qr", "cbf"], [bk], inc=False)
            for h in range(4):
                tp(pb[:, 512 + h * 128:512 + (h + 1) * 128], kr[:, h * 128:(h + 1) * 128], ident, ["kr", "cbf"], [bk], inc=(h == 3))
            tt("dve", qrT, pb[:, 0:512], dqT, ALU.mult, [bk, "cst"], ["qrT"])
            tt("dve", krT, pb[:, 512:1024], dkT, ALU.mult, [bk, "cst"], ["krT"])
            if prev:
                p1(m - 1)
            if nxt:
                head_norm_x(m + 1, part=True, trans=False, a=True, b=False)
            sbanks = [ps(pin=True), ps(pin=True)]
            for h in range(4):
                sb, sk = sbanks[h // 2]
                c = (h % 2) * 256
                mm(sb[:, c:c + 256], ktl[:, h * 128:(h + 1) * 128], vr[:, h * 256:(h + 1) * 256], True, True,
                   ["ktl", "vr"], [sk])
            pm = mprev1 if m == 1 else mprev
            srcs = [(pslot, pm), (cslot, mown)]
            idx = 0
            for kv in range(2):
                for (sl, msk) in srcs:
                    bank, bk = ps()
                    mm(bank[:, 0:512], kT[sl][kv * 64:(kv + 1) * 64, :], qaT[kv * 64:(kv + 1) * 64, :], True, True,
                       [f"kT{sl}", "qaT"], [bk])
                    act(PT[idx], bank[:, 0:512], AF.Exp, [bk], [f"PT{idx}"], scale=0.125)
                    tt("pool", PT[idx].rearrange("p (j q) -> p j q", j=4), PT[idx].rearrange("p (j q) -> p j q", j=4),
                       bc(msk, [128, 4, 128], 1), ALU.mult, [f"PT{idx}", "cbf"], [f"PT{idx}"])
                    idx += 1
            if prev:
                p2(m - 1)
            tile_gm(m)
            if nxt:
                head_norm_x(m + 1, part=True, trans=True, a=False, b=True)
            bank, bk = ps()
            for h in range(4):
                mm(bank[:, h * 128:(h + 1) * 128], krT[:, h * 128:(h + 1) * 128], qrT[:, h * 128:(h + 1) * 128], True, True,
                   ["krT", "qrT"], [bk])
            tt("dve", AT.rearrange("p (h i) -> p h i", h=4), bank[:, 0:512].rearrange("p (h i) -> p h i", h=4),
               bc(mown, [128, 4, 128], 1), ALU.mult, [bk, "cbf"], ["AT"])
            bo, bok = ps()
            bd, bdk = ps()
            idx = 0
            for kv in range(2):
                for i, (sl, msk) in enumerate(srcs):
                    mm(bo[kv * 64:(kv + 1) * 64, 0:512], vbf[sl][:, kv * 64:(kv + 1) * 64], PT[idx], i == 0, i == 1,
                       [f"vbf{sl}", f"PT{idx}"], [bok])
                    idx += 1
            idx = 0
            for kv in range(2):
                for i, (sl, msk) in enumerate(srcs):
                    mm(bd[kv * 64:(kv + 1) * 64, 0:512], ones64, PT[idx], i == 0, i == 1,
                       ["cbf", f"PT{idx}"], [bdk])
                    idx += 1
            tt("dve", t1.rearrange("p (j q) -> p j q", j=4), bd[:, 0:512].rearrange("p (j q) -> p j q", j=4),
               bc(ESK, [128, 4, 128], 2), ALU.add, [bdk, "esk"], ["t1"])
            S.op("dve", lambda e: e.reciprocal(out=t1, in_=t1), ["t1"], ["t1"])
            tt("dve", oaT, bo[:, 0:512], t1, ALU.mult, [bok, "t1"], ["oaT"])
            if nxt:
                tile_qa(m + 1)
            if prev:
                p3(m - 1)
                p4a(m - 1)
            obanks = [ps(), ps()]
            for h in range(4):
                ob, ok = obanks[h // 2]
                c = (h % 2) * 256
                mm(ob[:, c:c + 256], AT[:, h * 128:(h + 1) * 128], vr[:, h * 256:(h + 1) * 256], True, False,
                   ["AT", "vr"], [ok])
                mm(ob[:, c:c + 256], qrT[:, h * 128:(h + 1) * 128], Sbf[:, h * 256:(h + 1) * 256], False, True,
                   ["qrT", "Sbf"], [ok])
            for h in range(4):
                ob, ok = obanks[h // 2]
                c = (h % 2) * 256
                act(xs[:, h * 256:(h + 1) * 256], ob[:, c:c + 256], AF.Square, [ok], ["xs", f"ssr{h}"], accum=st8[:, 4 + h:5 + h])
            ts("dve", st8[:, 8:12], st8[:, 4:8], 4.0 / 256.0, 4.0 * EPS, ALU.mult, ALU.add,
               [f"ssr{h}" for h in range(4)], ["msr"])
            tt("pool", st8[:, 12:16], st8[:, 8:12], CN05, ALU.pow, ["msr", "cn05"], ["rstr"])
            if nxt:
                tile_kr(m + 1)
            for h in range(4):
                ob, ok = obanks[h // 2]
                c = (h % 2) * 256
                stt(og[:, h * 256:(h + 1) * 256], ob[:, c:c + 256], st8[:, 12 + h:13 + h], sg[:, h * 256:(h + 1) * 256],
                    ALU.mult, ALU.mult, [ok, "rstr", "sg"], ["og"])
            if prev:
                p4(m - 1, False)
            for h in range(4):
                sb, sk = sbanks[h // 2]
                c = (h % 2) * 256
                stt(S32[:, h * 256:(h + 1) * 256], S32[:, h * 256:(h + 1) * 256], float(GAM[h] ** 128),
                    sb[:, c:c + 256], ALU.mult, ALU.add, [sk, "S32"], ["S32"])
            unpin(sbanks[0][1])
            unpin(sbanks[1][1])
            cp("act", Sbf, S32, ["S32"], ["Sbf"])
            if nxt:
                tile_qr(m + 1)
                tile_kv(m + 1)
                tile_vr(m + 1)
            bank, bk = ps()
            pb = bank[:].bitcast(BF16)
            for k in range(8):
                tp(pb[:, k * 128:(k + 1) * 128], og[:, k * 128:(k + 1) * 128], ident, ["og", "cbf"], [bk], inc=(k == 7))
            cp("act", ogT.rearrange("p k t -> p (k t)"), pb[:, 0:1024], [bk], ["ogT"])
            if nxt:
                tile_gate(m + 1)
            if m < 16:
                r0 = m * 64
                r1 = m * 192
                its = [(wupbf[r0:r0 + 64, :].rearrange("r (c n) -> r c n", c=6), w_up[r0:r0 + 64, :].rearrange("r (c n) -> r c n", c=6), f"cvu{m}"),
                       (wdnbf[r1:r1 + 192, :], w_dn[r1:r1 + 192, :], f"cvd{m}")]
                S.dma_multi("pool", its, "cvt")

        load_x(1)
        load_tab(1)
        load_x(2)
        head_norm_x(0)
        tile_qa(0)
        tile_kv(0)
        tile_qr(0)
        tile_kr(0)
        tile_vr(0)
        tile_gate(0)
        for m in range(NMAIN):
            mixer_block(m, m == NMAIN - 1)
        p1(NMAIN - 1)
        p2(NMAIN - 1)
        p3(NMAIN - 1)
        p4a(NMAIN - 1)
        p4(NMAIN - 1, True)
        S.dma("sp", rp_o.rearrange("h k v -> k h v"), S32.rearrange("p (h v) -> p h v", h=4), "s32o", reads=["S32"])


        pre_state = {}

        def sample_mixer():
            S.barrier()
            NP = 64
            CS2 = A.alloc(128, [544], F32)
            CKB = [kT[1], A.alloc(128, [128], BF16)]
            CVB = [vbf[1], A.alloc(128, [128], BF16)]
            CKT = [A.alloc(128, [128], BF16) for _ in range(2)]
            ones128 = CBF[:, 640:768]
            maskc = CBF[:, 768:772]
            mnew = CBF[0:64, 832:896]
            dqs = CS2[:, 0:256]
            dms = CS2[0:64, 256:512]
            ktoks = CS2[0:64, 512:516]
            rowm = CS2[0:64, 516:532]
            S0B = [X[1], S32]
            Qb = [qaT[:, 256:512], krT[:, 256:512]]
            Kb = [PT[2], PT[3]]
            S.dma("sp", CS2, cs2, "cs2", writes=["cs2"])
            S.dma("sp", X[0][:NP], xsm, "x0", writes=["x0"])
            S.dma("sp", TAB[0][:NP], tabs[4096:4160, :], "tab0", writes=["tab0"])
            S.dma("sp", ks_o[:, 0:124, :], ck[:, 4:128, :], "kcpy")
            S.dma("sp", vs_o[:, 0:124, :], cv[:, 4:128, :], "vcpy")
            tab = TAB[0]
            TK = "tab0"
            xb = X[0]
            XK = "x0"
            norm_T(0, NP, 0)
            if SCUT == 1:
                pinned.clear()
                return
            bank, bk = inproj(0, 512, NP)
            cp("act", t1[:NP], bank[:NP, 0:512], [bk], ["t1"])
            for g in range(2):
                cp("pool", qa[:NP].rearrange("p (j g d) -> p j g d", j=4, g=2)[:, :, g, :],
                   t1[:NP, g * 256:(g + 1) * 256].rearrange("p (j d) -> p j d", j=4), ["t1"], ["qa"])
            psv = t1[:NP].rearrange("p (h d) -> p h d", h=8)
            cosA = tab[:NP, 512:528]
            ssinA = tab[:NP, 528:544]
            tk = ["t1", TK]
            tt("dve", tA1[:NP], psv[:, :, 0:16], bc(cosA, [NP, 8, 16], 1), ALU.mult, tk, ["tA1"])
            tt("dve", tA2[:NP, :, 0:8], psv[:, :, 8:16], bc(ssinA[:, 0:8], [NP, 8, 8], 1), ALU.mult, tk, ["tA2"])
            tt("dve", tA2[:NP, :, 8:16], psv[:, :, 0:8], bc(ssinA[:, 8:16], [NP, 8, 8], 1), ALU.mult, tk, ["tA2b"])
            for g in range(2):
                tt("dve", qa[:NP].rearrange("p (j g d) -> p j g d", j=4, g=2)[:, :, g, 0:16],
                   tA1[:NP, g * 4:(g + 1) * 4, :], tA2[:NP, g * 4:(g + 1) * 4, :], ALU.add,
                   ["tA1", "tA2", "tA2b", "qa"], ["qa"])
            bank, bk = inproj(512, 256, NP)
            cp("act", ka32[:NP], bank[:NP, 0:128], [bk], ["ka32"])
            cp("act", t2[:NP, 0:128], bank[:NP, 0:128], [bk], ["t2"])
            psv = t2[:NP, 0:128].rearrange("p (h d) -> p h d", h=2)
            tk = ["t2", TK]
            tt("dve", tA1[:NP, :2, :], psv[:, :, 0:16], bc(cosA, [NP, 2, 16], 1), ALU.mult, tk, ["tA1"])
            tt("dve", tA2[:NP, :2, 0:8], psv[:, :, 8:16], bc(ssinA[:, 0:8], [NP, 2, 8], 1), ALU.mult, tk, ["tA2"])
            tt("dve", tA2[:NP, :2, 8:16], psv[:, :, 0:8], bc(ssinA[:, 8:16], [NP, 2, 8], 1), ALU.mult, tk, ["tA2b"])
            tt("dve", ka32[:NP].rearrange("p (h d) -> p h d", h=2)[:, :, 0:16], tA1[:NP, :2, :], tA2[:NP, :2, :], ALU.add,
               ["tA1", "tA2", "tA2b", "ka32"], ["ka32"])
            cp("dve", kabf[:NP], ka32[:NP], ["ka32"], ["kabf"])
            cp("act", va32[:NP], bank[:NP, 128:256], [bk], ["va32"])
            cp("act", vbf[0][:NP], bank[:NP, 128:256], [bk], ["vbf0"])
            for b in range(16):
                S.dma("sp", ks_o[b, 124:128, :], ka32[b * 4:(b + 1) * 4, :], "ka32o", reads=["ka32"])
                S.dma("sp", vs_o[b, 124:128, :], va32[b * 4:(b + 1) * 4, :], "va32o", reads=["va32"])
            if SCUT == 2:
                pinned.clear()
                return
            for (c0, tcol, dst, dk_) in ((768, 0, qr, "qr"), (1280, 256, kr, "kr")):
                bank, bk = inproj(c0, 512, NP)
                rk = [bk, TK]
                cos = tab[:NP, tcol:tcol + 128]
                ssin = tab[:NP, tcol + 128:tcol + 256].rearrange("p (i two) -> p i two", two=2)
                pv = bank[:NP, 0:512].rearrange("p (h d) -> p h d", h=4)
                pp = bank[:NP, 0:512].rearrange("p (h i two) -> p h i two", h=4, two=2)
                t1v = t1[:NP].rearrange("p (h d) -> p h d", h=4)
                t2p = t2[:NP].rearrange("p (h i two) -> p h i two", h=4, two=2)
                tt("dve", t1v, pv, bc(cos, [NP, 4, 128], 1), ALU.mult, rk, ["t1"])
                tt("dve", t2p[:, :, :, 0], pp[:, :, :, 1], bc(ssin[:, :, 0], [NP, 4, 64], 1), ALU.mult, rk, ["t2"])
                tt("dve", t2p[:, :, :, 1], pp[:, :, :, 0], bc(ssin[:, :, 1], [NP, 4, 64], 1), ALU.mult, rk, ["t2b"])
                tt("dve", dst[:NP], t1[:NP], t2[:NP], ALU.add, ["t1", "t2", "t2b"], [dk_])
            tt("pool", ktl[:NP].rearrange("p (h d) -> p h d", h=4), kr[:NP].rearrange("p (h d) -> p h d", h=4),
               bc(ktoks, [NP, 4, 128], 2), ALU.mult, ["kr", "cs2"], ["ktl"])
            for n in range(2):
                bank, bk = inproj(1792 + n * 512, 512, NP)
                cp("act", vr[:NP, n * 512:(n + 1) * 512], bank[:NP, 0:512], [bk], ["vr"])
            for n in range(2):
                bank, bk = inproj(2816 + n * 512, 512, NP)
                act(tha[:NP, n * 512:(n + 1) * 512], bank[:NP, 0:512], AF.Tanh, [bk], ["tha"], scale=0.5)
                stt(sg[:NP, n * 512:(n + 1) * 512], tha[:NP, n * 512:(n + 1) * 512], 1.0, bank[:NP, 0:512],
                    ALU.add, ALU.mult, [bk, "tha"], ["sg"])
            if SCUT == 3:
                pinned.clear()
                return
            for n in range(2):
                bank, bk = inproj(3840 + n * 512, 512, NP)
                act(tha[:NP, n * 512:(n + 1) * 512], bank[:NP, 0:512], AF.Tanh, [bk], ["tha"], scale=0.5)
            for n in range(2):
                bank, bk = inproj(4864 + n * 512, 512, NP)
                act(thr[:NP, n * 512:(n + 1) * 512], bank[:NP, 0:512], AF.Tanh, [bk], ["thr"], scale=0.5)
            off_save = A.off
            A.off = mark_w
            WupE = A.alloc(128, [8, F2], BF16)
            A.off = off_save
            for k in range(8):
                S.dma("sp", WupE[:, k, :], wupbf[k * 128:(k + 1) * 128, :], "wup",
                      writes=(["winA", "winB", "x2", "wupk0"] if k == 0 else [f"wupk{k}"]))
            pre_state["wup"] = True
            bank, bk = ps()
            pb = bank[:].bitcast(BF16)
            for j in range(4):
                tp(pb[:, j * 128:j * 128 + 64], qa[:NP, j * 128:(j + 1) * 128], ident[:NP, :NP], ["qa", "cbf"], [bk], inc=False)
            tp(pb[:, 512:576], kabf[:NP], ident[:NP, :NP], ["kabf", "cbf"], [bk], inc=True)
            v4 = lambda ap: ap.rearrange("p (j q) -> p j q", j=4)
            cp("act", v4(qaT[:, 0:256]), v4(pb[:, 0:512])[:, :, 0:64], [bk], ["qaT"])
            cp("act", kT[0][:, 0:64], pb[:, 512:576], [bk], ["kT0"])
            if SCUT == 31:
                return
            bank, bk = ps()
            pb = bank[:].bitcast(BF16)
            for h in range(4):
                tp(pb[:, h * 128:h * 128 + 64], qr[:NP, h * 128:(h + 1) * 128], ident[:NP, :NP], ["qr", "cbf"], [bk], inc=False)
            for h in range(4):
                tp(pb[:, 512 + h * 128:512 + h * 128 + 64], kr[:NP, h * 128:(h + 1) * 128], ident[:NP, :NP], ["kr", "cbf"], [bk], inc=(h == 3))
            if SCUT == 32:
                S.op("pe", lambda e: e.transpose(pb[:, 0:64], qr[:NP, 0:128], ident[:NP, :NP]), ["qr"], [bk])
                return
            cp("act", v4(qrT[:, 0:256]), v4(pb[:, 0:512])[:, :, 0:64], [bk], ["qrT"])
            if SCUT == 33:
                return
            tt("dve", qrT[:, 256:512], qrT[:, 0:256], dqs, ALU.mult, ["qrT", "cs2"], ["qtT"])
            cp("act", v4(krT[:, 0:256]), v4(pb[:, 512:1024])[:, :, 0:64], [bk], ["krT"])
            if SCUT == 4:
                pinned.clear()
                return
            bS2 = [ps(pin=True), ps(pin=True)]
            CB = [PT[2][:, i * 128:(i + 1) * 128] for i in range(4)] + [PT[3][:, i * 128:(i + 1) * 128] for i in range(4)]
            for b in range(16):
                sl = b % 2
                c8 = b % 8
                S.dma("pool", CB[c8], ck[b], f"cb{c8}", writes=[f"cb{c8}"])
                bank, bk = ps()
                pb = bank[:].bitcast(BF16)
                tp(pb[:, 0:128], CB[c8], ident, [f"cb{c8}", "cbf"], [bk], inc=True)
                cp("act", CKT[sl], pb[:, 0:128], [bk], [f"ckt{sl}"])
                for kv in range(2):
                    off = b * 16
                    mm(bS2[kv][0][:, off:off + 16].rearrange("p (j t) -> p j t", j=4),
                       CKT[sl][kv * 64:(kv + 1) * 64, :],
                       qaT[kv * 64:(kv + 1) * 64, 0:256].rearrange("p (j q) -> p j q", j=4)[:, :, b * 4:(b + 1) * 4],
                       True, True, [f"ckt{sl}", "qaT"], [bS2[kv][1]])
            if SCUT == 5:
                pinned.clear()
                return
            PTc = PT[0]
            PTn = PT[1]
            for kv in range(2):
                act(PTc[:, kv * 256:(kv + 1) * 256], bS2[kv][0][:, 0:256], AF.Exp, [bS2[kv][1]], ["PTc"], scale=0.125)
                unpin(bS2[kv][1])
            tt("pool", PTc.rearrange("p (g t) -> p g t", t=4), PTc.rearrange("p (g t) -> p g t", t=4),
               bc(maskc, [128, 128, 4], 1), ALU.mult, ["PTc", "cbf"], ["PTc"])
            if SCUT == 51:
                return
            S.op("pool", lambda e: e.memset(PTn, 0.0), (), ["PTn"])
            S.op("pool", lambda e: e.memset(AT, 0.0), (), ["AT"])
            for kv in range(2):
                bN, bNk = ps()
                mm(bN[:, 0:256], kT[0][kv * 64:(kv + 1) * 64, 0:128], qaT[kv * 64:(kv + 1) * 64, 0:256],
                   True, True, ["kT0", "qaT"], [bNk])
                act(PTn[:NP, kv * 256:(kv + 1) * 256], bN[:NP, 0:256], AF.Exp, [bNk], ["PTn"], scale=0.125)
            if SCUT == 52:
                return
            tt("pool", PTn[:NP].rearrange("p (g q) -> p g q", g=8), PTn[:NP].rearrange("p (g q) -> p g q", g=8),
               bc(mnew, [NP, 8, 64], 1), ALU.mult, ["PTn", "cbf"], ["PTn"])
            if SCUT == 6:
                pinned.clear()
                return
            bo, bok = ps()
            bdn = [ps(), ps()]
            bdc, bdck = ps()
            mm(bdc[:, 0:512], ones128, PTc, True, True, ["cbf", "PTc"], [bdck])
            for kv in range(2):
                mm(bdn[kv][0][:, 0:256], ones128, PTn[:, kv * 256:(kv + 1) * 256], True, True,
                   ["cbf", "PTn"], [bdn[kv][1]])
            for kv in range(2):
                mm(bo[kv * 64:(kv + 1) * 64, 0:256], vbf[0][:, kv * 64:(kv + 1) * 64], PTn[:, kv * 256:(kv + 1) * 256],
                   True, False, ["vbf0", "PTn"], [bok])
            for b in range(16):
                c8 = b % 8
                S.dma("pool", CB[c8], cv[b], f"cb{c8}", writes=[f"cb{c8}"])
                for kv in range(2):
                    off = kv * 256 + b * 16
                    mm(bo[kv * 64:(kv + 1) * 64, 0:256].rearrange("p (j q) -> p j q", j=4)[:, :, b * 4:(b + 1) * 4],
                       CB[c8][:, kv * 64:(kv + 1) * 64],
                       PTc[:, off:off + 16].rearrange("p (j t) -> p j t", j=4),
                       False, b == 15, [f"cb{c8}", "PTc"], [bok], inc=True)
            if SCUT == 7:
                pinned.clear()
                return
            cp("act", t2, bdc[:, 0:512], [bdck], ["t2"])
            for kv in range(2):
                hs = slice(kv * 64, (kv + 1) * 64)
                tt("dve", t1[hs, 0:256].rearrange("p (j b t) -> p j b t", j=4, b=16),
                   bdn[kv][0][hs, 0:256].rearrange("p (j b t) -> p j b t", j=4, b=16),
                   t2[hs, :].rearrange("p (kv b j t) -> p kv j b t", b=16, kv=2, j=4)[:, kv],
                   ALU.add, [bdn[kv][1], "t2"], [f"t1{kv}"])
            tt("dve", t1[:, 0:256].rearrange("p (j q) -> p j q", j=4), t1[:, 0:256].rearrange("p (j q) -> p j q", j=4),
               bc(ESK, [128, 4, 64], 2), ALU.add, ["t10", "t11", "esk"], ["t1"])
            S.op("dve", lambda e: e.reciprocal(out=t1[:, 0:256], in_=t1[:, 0:256]), ["t1"], ["t1"])
            tt("dve", oaT[:, 0:256], bo[:, 0:256], t1[:, 0:256], ALU.mult, [bok, "t1"], ["oaT"])
            if SCUT == 8:
                pinned.clear()
                return
            bR, bRk = ps()
            for h in range(4):
                mm(bR[:NP, h * 64:(h + 1) * 64], krT[:, h * 64:(h + 1) * 64], qrT[:, h * 64:(h + 1) * 64], True, True,
                   ["krT", "qrT"], [bRk])
            tt("dve", AT[:NP, 0:256], bR[:NP, 0:256], dms, ALU.mult, [bRk, "cs2"], ["AT"])
            ob = [ps(pin=True) for _ in range(4)]
            for h in range(4):
                mm(ob[h][0][:NP, 0:256], AT[:, h * 64:(h + 1) * 64], vr[:, h * 256:(h + 1) * 256], True, False,
                   ["AT", "vr"], [ob[h][1]])
            if SCUT == 9:
                pinned.clear()
                return
            S.op("dve", lambda e: e.memset(Qb[0], 0.0), (), ["qb0"])
            S.op("dve", lambda e: e.memset(Qb[1], 0.0), (), ["qb1"])
            def load_s0(b_):
                sl_ = b_ % 2
                S.dma("sp", S0B[sl_].rearrange("p (h v) -> p h v", h=4), sr[b_].rearrange("h k v -> k h v"), f"s0{sl_}",
                      writes=[f"s0{sl_}"])

            load_s0(0)
            load_s0(1)
            for b in range(16):
                sl = b % 2
                s0 = S0B[sl]
                sk_ = f"s0{sl}"
                cp("act", Sbf, s0, [sk_], ["Sbf"])
                qv = Qb[sl].rearrange("p (h q) -> p h q", h=4)
                if b >= 2:
                    S.op("dve", (lambda v: lambda e: e.memset(v, 0.0))(qv[:, :, (b - 2) * 4:(b - 1) * 4]), (), [f"qb{sl}"])
                cp("dve", qv[:, :, b * 4:(b + 1) * 4], qrT[:, 256:512].rearrange("p (h q) -> p h q", h=4)[:, :, b * 4:(b + 1) * 4],
                   ["qtT"], [f"qb{sl}"])
                act(Kb[sl][:NP], ktl[:NP], AF.Copy, ["ktl", "cs2"], [f"kb{sl}"] + [f"cb{sl * 4 + i}" for i in range(4)],
                    scale=rowm[:, b:b + 1])
                for h in range(4):
                    mm(ob[h][0][:NP, 0:256], Qb[sl][:, h * 64:(h + 1) * 64], Sbf[:, h * 256:(h + 1) * 256], False, b == 15,
                       [f"qb{sl}", "Sbf"], [ob[h][1]])
                sbanks = [ps(), ps()]
                for h in range(4):
                    sb, sk2 = sbanks[h // 2]
                    c = (h % 2) * 256
                    mm(sb[:, c:c + 256], Kb[sl][:NP, h * 128:(h + 1) * 128], vr[:NP, h * 256:(h + 1) * 256], True, True,
                       [f"kb{sl}", "vr"], [sk2])
                for h in range(4):
                    sb, sk2 = sbanks[h // 2]
                    c = (h % 2) * 256
                    stt(s0[:, h * 256:(h + 1) * 256], s0[:, h * 256:(h + 1) * 256], float(GAM[h] ** 4),
                        sb[:, c:c + 256], ALU.mult, ALU.add, [sk2, sk_, "Sbf"], [sk_])
                S.dma("sp", rs_o[b].rearrange("h k v -> k h v"), s0.rearrange("p (h v) -> p h v", h=4), sk_, reads=[sk_])
                if b + 2 < 16:
                    load_s0(b + 2)
            if SCUT == 10:
                pinned.clear()
                return
            for h in range(4):
                unpin(ob[h][1])
            for h in range(4):
                act(xs[:NP, h * 256:(h + 1) * 256], ob[h][0][:NP, 0:256], AF.Square, [ob[h][1]], ["xs", f"ssr{h}"], accum=st8[:NP, 4 + h:5 + h])
            ts("dve", st8[:NP, 8:12], st8[:NP, 4:8], 4.0 / 256.0, 4.0 * EPS, ALU.mult, ALU.add,
               [f"ssr{h}" for h in range(4)], ["msr"])
            tt("pool", st8[:NP, 12:16], st8[:NP, 8:12], CN05[:NP], ALU.pow, ["msr", "cn05"], ["rstr"])
            for h in range(4):
                stt(og[:NP, h * 256:(h + 1) * 256], ob[h][0][:NP, 0:256], st8[:NP, 12 + h:13 + h], sg[:NP, h * 256:(h + 1) * 256],
                    ALU.mult, ALU.mult, [ob[h][1], "rstr", "sg"], ["og"])
            bank, bk = ps()
            pb = bank[:].bitcast(BF16)
            for k in range(8):
                tp(pb[:, k * 128:k * 128 + 64], og[:NP, k * 128:(k + 1) * 128], ident[:NP, :NP], ["og", "cbf"], [bk], inc=(k == 7))
            cp("act", ogT[:, :, 0:64], pb[:, 0:1024].rearrange("p (k t) -> p k t", k=8)[:, :, 0:64], [bk], ["ogT"])
            for n in range(2):
                bank, bk = ps()
                for j in range(4):
                    mm(bank[:NP, 0:512], oaT[:, j * 64:(j + 1) * 64], Wa[:, j, n * 512:(n + 1) * 512], j == 0, j == 3,
                       ["oaT", "wa"], [bk])
                stt(tha[:NP, n * 512:(n + 1) * 512], tha[:NP, n * 512:(n + 1) * 512], 1.0, bank[:NP, 0:512],
                    ALU.add, ALU.mult, [bk, "tha"], ["tha"])
                bank, bk = ps()
                for k in range(8):
                    mm(bank[:NP, 0:512], ogT[:, k, 0:64], Wr[:, k, n * 512:(n + 1) * 512], k == 0, k == 7,
                       ["ogT", "wr"], [bk])
                stt(thr[:NP, n * 512:(n + 1) * 512], thr[:NP, n * 512:(n + 1) * 512], 1.0, bank[:NP, 0:512],
                    ALU.add, ALU.mult, [bk, "thr"], ["thr"])
            tt("dve", tha[:NP], tha[:NP], thr[:NP], ALU.add, ["tha", "thr"], ["tha"])
            bank, bk = ps()
            pb = bank[:].bitcast(BF16)
            for k in range(8):
                tp(pb[:, k * 128:k * 128 + 64], tha[:NP, k * 128:(k + 1) * 128], ident[:NP, :NP], ["tha", "cbf"], [bk], inc=(k == 7))
            cp("act", ogT[:, :, 0:64], pb[:, 0:1024].rearrange("p (k t) -> p k t", k=8)[:, :, 0:64], [bk], ["ogT"])
            mb = []
            for n in range(2):
                bank, bk = ps()
                for k in range(8):
                    mm(bank[:NP, 0:512], ogT[:, k, 0:64], Wo[:, k, n * 512:(n + 1) * 512], k == 0, k == 7,
                       ["ogT", "wo"], [bk])
                mb.append((bank, bk))
                act(xs[:NP, n * 512:(n + 1) * 512], bank[:NP, 0:512], AF.Square, [bk], ["xs", f"ssm{n}"],
                    accum=st8[:NP, 16 + n:17 + n])
            tt("dve", st8[:NP, 18:19], st8[:NP, 16:17], st8[:NP, 17:18], ALU.add, ["ssm0", "ssm1"], ["ssm"])
            ts("dve", st8[:NP, 19:20], st8[:NP, 18:19], 1.0 / D, 4.0 * EPS, ALU.mult, ALU.add, ["ssm"], ["msm"])
            tt("pool", st8[:NP, 20:21], st8[:NP, 19:20], CN05[:NP, 0:1], ALU.pow, ["msm", "cn05"], ["rstm"])
            for n in range(2):
                bank, bk = mb[n]
                stt(bank[:NP, 0:512], bank[:NP, 0:512], st8[:NP, 20:21], GPM[:NP, n * 512:(n + 1) * 512], ALU.mult, ALU.mult,
                    [bk, "rstm", "gpm"], [bk])
                tt("dve", xb[:NP, n * 512:(n + 1) * 512], xb[:NP, n * 512:(n + 1) * 512], bank[:NP, 0:512], ALU.add, [XK, bk], [XK])
            S.dma("sp", x1s[NMAIN * 128:NMAIN * 128 + 64, :], xb[:NP], XK, reads=[XK], writes=["x1ss"])

        sample_mixer()

        def ffn_phase():
            S.barrier()
            A.off = mark_w
            Wup = A.alloc(128, [8, F2], BF16)
            Wdn = A.alloc(128, [24, D], BF16)
            CW = A.alloc(128, [48, 4], F32)
            Uh = A.alloc(128, [48, 2], F32)
            X1R = A.alloc(128, [4 * D], F32)
            X1 = [X1R[:, i * D:(i + 1) * D] for i in range(4)]
            xs2 = A.alloc(128, [D], BF16)
            XN2 = [A.alloc(128, [8, 256], BF16) for _ in range(2)]
            xn2T = XN2[0]
            Ua = [A.alloc(128, [258], F32) for _ in range(2)]
            Ub = [A.alloc(128, [258], F32) for _ in range(2)]
            CA = [A.alloc(128, [256], F32) for _ in range(2)]
            CB = [A.alloc(128, [256], F32) for _ in range(2)]
            G2s = A.alloc(128, [256], F32)
            G2 = [G2s, G2s]
            G1 = [CA[1], CB[1]]
            ca, cb_, g1, g2 = CA[0], CB[0], CA[1], G2s
            hT = A.alloc(128, [24, 256], BF16)
            tmpf = A.alloc(128, [D], F32)
            stf = A.alloc(128, [16], F32)
            CPO = A.alloc(128, [F2], F32) if False else None
            items = []
            for k in range(8):
                for c in range(6):
                    items.append((Wup[:, k, c * 1024:(c + 1) * 1024], w_up[k * 128:(k + 1) * 128, c * 1024:(c + 1) * 1024], "wup"))
            if not pre_state.get("wup"):
                S.dma_multi("pool", items, "wup")
            S.dma_multi("sp", [(Wdn[:, k, :], wdnbf[k * 128:(k + 1) * 128, :], "wdn") for k in range(24)], "wdn")
            S.dma("sp", CW.rearrange("p a b -> p (a b)"), cwT, "cw", writes=["cw"])
            S.dma("sp", GPM, gpost[1].partition_broadcast(128), "gpm", writes=["gpm"])

            def norm_T2(xt, xk, np_, dest, dkey, part=True, trans=True):
                if part:
                    act(tmpf[:np_], xt[:np_], AF.Square, [xk], ["tmpf", "fssq"], accum=stf[:np_, 0:1])
                    ts("dve", stf[:np_, 1:2], stf[:np_, 0:1], 1.0 / D, EPS, ALU.mult, ALU.add, ["fssq"], ["fms"])
                    tt("pool", stf[:np_, 2:3], stf[:np_, 1:2], CN05[:np_, 0:1], ALU.pow, ["fms", "cn05"], ["frstd"])
                    act(xs2[:np_], xt[:np_], AF.Copy, [xk, "frstd"], ["xs2"], scale=stf[:np_, 2:3])
                if not trans:
                    return
                bank, bk = ps()
                pb = bank[:].bitcast(BF16)
                for k in range(8):
                    tp(pb[:, k * 128:k * 128 + np_], xs2[:np_, k * 128:(k + 1) * 128], ident[:np_, :np_],
                       ["xs2", "cbf"], [bk], inc=(k == 7))
                pv = pb.rearrange("p (k t) -> p k t", k=8)[:, :, :np_]
                tt("dve", dest, pv, bc(GVT[:, 8:16], [128, 8, np_], 2), ALU.mult, [bk, "gvt"], [dkey])

            S.dma("sp", X1[0][0:2], x1s[126:128, :], "x1_0", writes=["x1_0"])
            norm_T2(X1[0], "x1_0", 2, xn2T[:, :, 0:2], "xn2T0")
            bank, bk = ps()
            for t in range(48):
                for k in range(8):
                    mm(bank[:, t * 2:t * 2 + 2], Wup[:, k, t * 128:(t + 1) * 128], xn2T[:, k, 0:2], k == 0, k == 7,
                       ["wup", "xn2T0"], [bk])
            cp("act", Uh.rearrange("p a b -> p (a b)"), bank[:, 0:96], [bk], ["uh"])

            def act_id(out, in_, sc, bi, reads, writes):
                S.op("act", lambda e: e.activation(out=out, in_=in_, func=AF.Identity, scale=sc, bias=bi), reads, writes)

            def conv_gelu(j, ntok, ub_a, ub_b, bka, bkb, hdst, uview, slot):
                tiles = ((Ua[slot], ub_a, bka, j, CA[slot], f"ca{slot}"), (Ub[slot], ub_b, bkb, 24 + j, CB[slot], f"cb{slot}"))
                for (U, bank_, bk_, tix, cdst, ckey) in tiles:
                    uk = f"U{ckey}"
                    cp("pool", U[:, 0:2], Uh[:, tix, :], ["uh"], [uk + "h"])
                    cp("act", U[:, 2:2 + ntok], bank_[:, 0:ntok], [bk_], [uk])
                    act_id(cdst[:, :ntok], bank_[:, 0:ntok], CW[:, tix, 2:3], CW[:, tix, 3:4], [bk_, "cw"], [ckey])
                    cp("pool", Uh[:, tix, :], U[:, ntok:ntok + 2], [uk], ["uh"])
                for tap in (1, 0):
                    for (U, bank_, bk_, tix, cdst, ckey) in tiles:
                        uk = f"U{ckey}"
                        stt(cdst[:, :ntok], U[:, tap:tap + ntok], CW[:, tix, tap:tap + 1], cdst[:, :ntok], ALU.mult, ALU.add,
                            [uk, uk + "h", "cw", ckey], [ckey])
                if uview != "defer":
                    gelu_mul(ntok, hdst, slot, uview if uview else "hT")

            def gelu_mul(ntok, hdst, slot=0, hkey="hT"):
                a = CA[slot][:, :ntok]
                b_ = CB[slot][:, :ntok]
                g2_ = G2s[:, :ntok]
                ak, bk2, g2k = f"ca{slot}", f"cb{slot}", "g2s"
                act(g2_, a, AF.Gelu_apprx_tanh, [ak], [g2k])
                tt("dve", hdst, g2_, b_, ALU.mult, [g2k, bk2], [hkey])

            def down_mm(fb, k, np_, tcol0):
                for n in range(2):
                    bank, bk = fb[n]
                    mm(bank[:np_, 0:512], hT[:, k, tcol0:tcol0 + np_], Wdn[:, k, n * 512:(n + 1) * 512], k == 0, k == 23,
                       [f"hT{k}", "wdn"], [bk])

            def down_tail(fb, xt, xk, np_, out_rows):
                for n in range(2):
                    bank, bk = fb[n]
                    act(tmpf[:np_, n * 512:(n + 1) * 512], bank[:np_, 0:512], AF.Square, [bk], ["tmpf", f"fs{n}"],
                        accum=stf[:np_, 4 + n:5 + n])
                tt("dve", stf[:np_, 6:7], stf[:np_, 4:5], stf[:np_, 5:6], ALU.add, ["fs0", "fs1"], ["fs"])
                ts("dve", stf[:np_, 7:8], stf[:np_, 6:7], 1.0 / D, EPS, ALU.mult, ALU.add, ["fs"], ["fm"])
                tt("pool", stf[:np_, 8:9], stf[:np_, 7:8], CN05[:np_, 0:1], ALU.pow, ["fm", "cn05"], ["fr"])
                for n in range(2):
                    bank, bk = fb[n]
                    act(tmpf[:np_, n * 512:(n + 1) * 512], bank[:np_, 0:512], AF.Copy, [bk, "fr"], ["tmpf"], scale=stf[:np_, 8:9])
                    unpin(bk)
                tt("pool", tmpf[:np_], tmpf[:np_], GPM[:np_], ALU.mult, ["tmpf", "gpm"], ["tmpf"])
                tt("dve", xt[:np_], xt[:np_], tmpf[:np_], ALU.add, [xk, "tmpf"], [xk])
                S.dma("sp", out_rows, xt[:np_], xk, reads=[xk])

            def tail_A(fb, blk):
                for n in range(2):
                    bank, bk = fb[n]
                    act(tmpf[:, n * 512:(n + 1) * 512], bank[:, 0:512], AF.Square, [bk], ["tmpf", f"fsq{blk}{n}"],
                        accum=stf[:, 4 + 2 * blk + n:5 + 2 * blk + n])

            def tail_B(blk):
                tt("dve", stf[:, 8 + blk:9 + blk], stf[:, 4 + 2 * blk:5 + 2 * blk], stf[:, 5 + 2 * blk:6 + 2 * blk], ALU.add,
                   [f"fsq{blk}0", f"fsq{blk}1"], [f"fsum{blk}"])
                ts("dve", stf[:, 10 + blk:11 + blk], stf[:, 8 + blk:9 + blk], 1.0 / D, EPS, ALU.mult, ALU.add,
                   [f"fsum{blk}"], [f"fms{blk}"])
                tt("pool", stf[:, 12 + blk:13 + blk], stf[:, 10 + blk:11 + blk], CN05[:, 0:1], ALU.pow,
                   [f"fms{blk}", "cn05"], [f"frs{blk}"])

            def tail_C(fb, blk, xt, xk, out_rows):
                for n in range(2):
                    bank, bk = fb[n]
                    stt(bank[:, 0:512], bank[:, 0:512], stf[:, 12 + blk:13 + blk], GPM[:, n * 512:(n + 1) * 512],
                        ALU.mult, ALU.mult, [bk, f"frs{blk}", "gpm"], [bk])
                    tt("dve", xt[:, n * 512:(n + 1) * 512], xt[:, n * 512:(n + 1) * 512], bank[:, 0:512], ALU.add, [xk, bk], [xk])
                    unpin(bk)
                S.dma("sp", out_rows, xt, xk, reads=[xk])

            def down_post(xt, xk, np_, tcol0, out_rows, okey_sem):
                fb = [ps(pin=True), ps(pin=True)]
                for k in range(24):
                    down_mm(fb, k, np_, tcol0)
                down_tail(fb, xt, xk, np_, out_rows)

            NG = 8

            def load_group(g):
                for i in range(2):
                    xi = (g % 2) * 2 + i
                    m = 1 + g * 2 + i
                    S.dma("sp", X1[xi], x1s[m * 128:(m + 1) * 128, :], f"x1_{xi}", writes=[f"x1_{xi}"])

            load_group(0)

            def norm_blk(g, i, part, trans):
                xi = (g % 2) * 2 + i
                norm_T2(X1[xi], f"x1_{xi}", 128, XN2[g % 2][:, :, i * 128:(i + 1) * 128], f"xn2T{g % 2}", part=part, trans=trans)

            norm_blk(0, 0, True, True)
            norm_blk(0, 1, True, True)
            LAG = 3
            pending = None
            for g in range(NG):
                if g == 0:
                    load_group(1)
                xn = XN2[g % 2]
                xnk = f"xn2T{g % 2}"
                fbs = None
                for j in range(24):
                    banks = []
                    for tix in (j, 24 + j):
                        bank, bk = ps()
                        for k in range(8):
                            mm(bank[:, 0:256], Wup[:, k, tix * 128:(tix + 1) * 128], xn[:, k, 0:256], k == 0, k == 7,
                               ["wup", xnk], [bk])
                        banks.append((bank, bk))
                    conv_gelu(j, 256, banks[0][0], banks[1][0], banks[0][1], banks[1][1], hT[:, j, 0:256], "defer", j % 2)
                    if j >= 1:
                        gelu_mul(256, hT[:, j - 1, 0:256], (j - 1) % 2, f"hT{j - 1}")
                    if pending is not None and j == 0:
                        tail_B(0)
                        tail_B(1)
                    if pending is not None and j == 1:
                        pf, pg = pending
                        for i in range(2):
                            xi = (pg % 2) * 2 + i
                            row0 = (pg * 2 + i) * 128
                            tail_C(pf[i], i, X1[xi], f"x1_{xi}", y_o[row0:row0 + 128, :])
                        pending = None
                        if g + 1 < NG:
                            load_group(g + 1)
                    if g + 1 < NG:
                        if j == 5:
                            norm_blk(g + 1, 0, True, False)
                        if j == 8:
                            norm_blk(g + 1, 0, False, True)
                        if j == 11:
                            norm_blk(g + 1, 1, True, False)
                        if j == 14:
                            norm_blk(g + 1, 1, False, True)
                    if j - LAG == 0:
                        fbs = [[ps(pin=True), ps(pin=True)] for _ in range(2)]
                    if j - LAG >= 0:
                        for i in range(2):
                            down_mm(fbs[i], j - LAG, 128, i * 128)
                gelu_mul(256, hT[:, 23, 0:256], 23 % 2, "hT23")
                for k in range(24 - LAG, 24):
                    for i in range(2):
                        down_mm(fbs[i], k, 128, i * 128)
                tail_A(fbs[0], 0)
                tail_A(fbs[1], 1)
                pending = (fbs, g)
            tail_B(0)
            tail_B(1)
            pf, pg = pending
            for i in range(2):
                xi = (pg % 2) * 2 + i
                row0 = (pg * 2 + i) * 128
                tail_C(pf[i], i, X1[xi], f"x1_{xi}", y_o[row0:row0 + 128, :])
            CP2 = tmpf
            for q4 in range(12):
                bank, bk = ps()
                for i in range(4):
                    t = q4 * 4 + i
                    tp(bank[0:2, i * 128:(i + 1) * 128], Uh[:, t, :], idf, ["uh", "cst"], [bk], inc=(i == 3))
                cp("act", g1[0:2, 0:256], bank[0:2, 0:256], [bk], ["ca1"])
                cp("act", g2[0:2, 0:256], bank[0:2, 256:512], [bk], ["g2s"])
                S.dma("sp", cp_o[:, q4 * 512:q4 * 512 + 256], g1[0:2, 0:256], "ca1", reads=["ca1"])
                S.dma("sp", cp_o[:, q4 * 512 + 256:q4 * 512 + 512], g2[0:2, 0:256], "g2s", reads=["g2s"])

            if SCUT != 0:
                return
            S.barrier()
            NP = 64
            XS = X1R[:, 0:1024]
            CTX = X1R[:, 1024:2560].rearrange("p (a b) -> p a b", a=48)
            US = X1R[:, 2560:4096].rearrange("p (a b) -> p a b", a=48)
            S.dma("sp", XS[:NP], x1s[NMAIN * 128:NMAIN * 128 + 64, :], "x1_0", writes=["x1_0"])
            for q4 in range(12):
                sc = tmpf[:32, (q4 % 2) * 512:(q4 % 2) * 512 + 512]
                sck = f"sc{q4 % 2}"
                S.dma("sp", sc, scv[:, q4 * 512:(q4 + 1) * 512], sck, writes=[sck])
                bank, bk = ps()
                for i in range(4):
                    tp(bank[:, i * 32:(i + 1) * 32], sc[:, i * 128:(i + 1) * 128], idf[:32, :32], [sck, "cst"], [bk], inc=(i == 3))
                cp("act", CTX[:, q4 * 4:(q4 + 1) * 4, :], bank[:, 0:128].rearrange("p (a b) -> p a b", a=4), [bk], ["ctx"])
            norm_T2(XS, "x1_0", NP, xn2T[:, :, 0:NP], "xn2T")
            for j in range(24):
                slot = j % 2
                banks = []
                for tix in (j, 24 + j):
                    bank, bk = ps()
                    for k in range(8):
                        mm(bank[:, 0:NP], Wup[:, k, tix * 128:(tix + 1) * 128], xn2T[:, k, 0:NP], k == 0, k == 7,
                           ["wup", "xn2T"], [bk])
                    banks.append((bank, bk))
                for (U, (bank_, bk_), tix, cdst, ckey) in ((Ua[slot], banks[0], j, CA[slot], f"ca{slot}"), (Ub[slot], banks[1], 24 + j, CB[slot], f"cb{slot}")):
                    uk = f"U{ckey}"
                    Ue = U[:, 0:96].rearrange("p (b s) -> p b s", b=16)
                    cv_ = cdst[:, 0:64].rearrange("p (b t) -> p b t", b=16)
                    cp("pool", Ue[:, :, 0:2], CTX[:, tix, :].rearrange("p (b c) -> p b c", b=16), ["ctx"], [uk])
                    cp("act", Ue[:, :, 2:6], bank_[:, 0:64].rearrange("p (b t) -> p b t", b=16), [bk_], [uk])
                    cp("pool", US[:, tix, :].rearrange("p (b c) -> p b c", b=16), Ue[:, :, 4:6], [uk], ["us"])
                    ts("dve", cv_, Ue[:, :, 2:6], CW[:, tix, 2:3], CW[:, tix, 3:4], ALU.mult, ALU.add, [uk, "cw"], [ckey])
                    stt(cv_, Ue[:, :, 1:5], CW[:, tix, 1:2], cv_, ALU.mult, ALU.add, [uk, "cw", ckey], [ckey])
                    stt(cv_, Ue[:, :, 0:4], CW[:, tix, 0:1], cv_, ALU.mult, ALU.add, [uk, "cw", ckey], [ckey])
                gelu_mul(NP, hT[:, j, 0:NP], slot, f"hT{j}")
            down_post(XS, "x1_0", NP, 0, ys_o, None)
            for q4 in range(12):
                bank, bk = ps()
                for i in range(4):
                    tp(bank[0:32, i * 128:(i + 1) * 128], US[:, q4 * 4 + i, :], idf, ["us", "cst"], [bk], inc=(i == 3))
                stg = CA[1] if q4 % 2 == 0 else G2s
                sgk = "ca1" if q4 % 2 == 0 else "g2s"
                cp("act", stg[0:32, 0:256], bank[0:32, 0:256], [bk], [sgk])
                S.dma("sp", cs_o[:, q4 * 512:q4 * 512 + 256], stg[0:32, 0:256], sgk, reads=[sgk])
                stg2 = CA[0] if q4 % 2 == 0 else CB[0]
                sgk2 = "ca0" if q4 % 2 == 0 else "cb0"
                cp("act", stg2[0:32, 0:256], bank[0:32, 256:512], [bk], [sgk2])
                S.dma("sp", cs_o[:, q4 * 512 + 256:q4 * 512 + 512], stg2[0:32, 0:256], sgk2, reads=[sgk2])

        ffn_phase()

        S.finish()
    return nc


def _tables(pos):
    pos = np.asarray(pos)
    T = pos.shape[0]
    pf = pos.astype(np.float32)
    out = np.zeros((T, NTAB), np.float32)
    ang = (1.0 / (np.float32(10000.0) ** np.linspace(0.0, 1.0, 64, dtype=np.float32))).astype(np.float32)
    ang = np.repeat(ang, 2)
    th = (pf[:, None] * ang[None, :]).astype(np.float32)
    c = np.cos(th.astype(np.float64))
    s = np.sin(th.astype(np.float64))
    sgn = np.tile(np.array([-1.0, 1.0]), 64)[None, :]
    out[:, 0:128] = c
    out[:, 128:256] = s * sgn
    sc = 128.0 ** -0.5
    out[:, 256:384] = c * sc
    out[:, 384:512] = s * sgn * sc
    half = 8
    inv = (1.0 / (np.float32(500000.0) ** (np.arange(half, dtype=np.float32) / np.float32(half)))).astype(np.float32)
    a = (pf[:, None] * inv[None, :]).astype(np.float32)
    ca = np.cos(a.astype(np.float64))
    sa = np.sin(a.astype(np.float64))
    out[:, 512:520] = ca
    out[:, 520:528] = ca
    out[:, 528:536] = -sa
    out[:, 536:544] = sa
    return out


def _consts(is_b):
    cst = np.zeros((128, 1280), np.float64)
    i = np.arange(128)
    for h in range(4):
        g = GAM[h]
        cst[:, h * 128:(h + 1) * 128] = (g ** (i + 1.0))[None, :]
        cst[:, 512 + h * 128:512 + (h + 1) * 128] = (g ** (-(i + 1.0)))[None, :]
        cst[:, 1024 + h] = g ** (127.0 - i)
    cst[:, 1152:1280] = np.eye(128)
    cbf = np.zeros((128, 1024), np.float32)
    cbf[:, 0:128] = np.eye(128)
    cbf[:, 128:192] = 1.0
    k = np.arange(128)[:, None]
    q = np.arange(128)[None, :]
    cbf[:, 256:384] = (k <= q)
    cbf[:, 384:512] = (k > q)
    cbf[:, 512:640] = (k > q) if is_b else 0.0
    cbf[:, 640:768] = 1.0
    cbf[:, 768:772] = (np.arange(128)[:, None] > np.arange(4)[None, :])
    kk = np.arange(64)[:, None]
    qq = np.arange(64)[None, :]
    cbf[0:64, 832:896] = ((kk // 4) == (qq // 4)) & ((kk % 4) <= (qq % 4))
    cs2 = np.zeros((128, 544), np.float64)
    tok = np.arange(64)
    for h in range(4):
        g = GAM[h]
        cs2[:, h * 64:(h + 1) * 64] = (g ** ((tok % 4) + 1.0))[None, :]
        dm = np.where(((kk // 4) == (qq // 4)) & ((qq % 4) >= (kk % 4)), g ** ((qq % 4) - (kk % 4)).astype(np.float64), 0.0)
        cs2[0:64, 256 + h * 64:256 + (h + 1) * 64] = dm
        cs2[0:64, 512 + h] = g ** (3.0 - (tok % 4))
    cs2[0:64, 516:532] = ((tok[:, None] // 4) == np.arange(16)[None, :])
    return cst.astype(np.float32), cbf.astype(ml_dtypes.bfloat16), cs2.astype(np.float32)


_NC_CACHE = {}


def kernel(x_prompt, x_sample, cache_k, cache_v, state_ret, state_conv,
           w_in, attn_sinks, w_a_proj, w_r_proj, w_o,
           g_pre_mix, g_post_mix, g_pre_ffn, g_post_ffn,
           w_up, conv_w, conv_b, w_down):
    f = lambda a: np.ascontiguousarray(np.asarray(a, dtype=np.float32))
    x_prompt = f(x_prompt); x_sample = f(x_sample)
    cache_k = f(cache_k); cache_v = f(cache_v); state_ret = f(state_ret); state_conv = f(state_conv)
    if "nc" not in _NC_CACHE:
        _NC_CACHE["nc"] = build_program()
    nc = _NC_CACHE["nc"]

    wa = f(w_a_proj)[0].reshape(8, 64, D)
    wa_l = np.zeros((128, 4, D), np.float32)
    for j in range(4):
        wa_l[0:64, j] = wa[j]
        wa_l[64:128, j] = wa[4 + j]
    gvT = np.concatenate([f(g_pre_mix)[0].reshape(8, 128).T, f(g_pre_ffn)[0].reshape(8, 128).T], axis=1)
    gpost = np.stack([f(g_post_mix)[0], f(g_post_ffn)[0]])
    cw = f(conv_w)[0]
    cb = f(conv_b)[0]
    cwT = np.zeros((128, 48, 4), np.float32)
    for t in range(48):
        cwT[:, t, 0:3] = cw[:, t * 128:(t + 1) * 128].T
        cwT[:, t, 3] = cb[t * 128:(t + 1) * 128]
    cwT = cwT.reshape(128, 192)
    sk = f(attn_sinks)[0]
    sinks = np.zeros((128, 4), np.float32)
    sinks[0:64, :] = sk[0:4][None, :]
    sinks[64:128, :] = sk[4:8][None, :]

    in_maps = []
    for c in range(8):
        b, half = c // 2, c % 2
        xe = np.zeros((32 * 128, D), np.float32)
        if half == 0:
            xe[16 * 128:] = x_prompt[b, 0:2048]
            pos = np.concatenate([np.arange(-2048, 2048), 16384 + np.tile(np.arange(4), 16), np.zeros(64, np.int64)])
        else:
            xe[:] = x_prompt[b]
            pos = np.concatenate([np.arange(0, 4096), 16384 + np.tile(np.arange(4), 16), np.zeros(64, np.int64)])
        cst, cbf, cs2 = _consts(half == 1)
        in_maps.append({
            "xe": xe,
            "xsm": x_sample[16 * c:16 * (c + 1)].reshape(64, D),
            "ck": cache_k[0, 16 * c:16 * (c + 1)].reshape(16, 128, 128),
            "cv": cache_v[0, 16 * c:16 * (c + 1)].reshape(16, 128, 128),
            "sr": state_ret[0, 16 * c:16 * (c + 1)],
            "scv": state_conv[0, 16 * c:16 * (c + 1)].reshape(32, F2),
            "w_in": f(w_in)[0], "w_a": wa_l, "w_r": f(w_r_proj)[0], "w_o": f(w_o)[0],
            "w_up": f(w_up)[0], "w_dn": f(w_down)[0],
            "gvT": np.ascontiguousarray(gvT), "gpost": gpost, "cwT": cwT, "sinks": sinks,
            "tabs": _tables(pos), "cst": cst, "cbf": cbf, "cs2": cs2,
        })
    res = run_bass_kernel_spmd(nc, in_maps, core_ids=list(range(8)))
    R = res.results
    y = np.zeros((4, 4096, D), np.float32)
    ys = np.zeros((128, 4, D), np.float32)
    kp = np.zeros((1, 4, 128, 2, 64), np.float32)
    vp = np.zeros((1, 4, 128, 2, 64), np.float32)
    rp = np.zeros((1, 4, 4, 128, 256), np.float32)
    cpo = np.zeros((1, 4, 2, F2), np.float32)
    kso = np.zeros((1, 128, 128, 2, 64), np.float32)
    vso = np.zeros((1, 128, 128, 2, 64), np.float32)
    rso = np.zeros((1, 128, 4, 128, 256), np.float32)
    cso = np.zeros((1, 128, 2, F2), np.float32)
    for c in range(8):
        b, half = c // 2, c % 2
        r = R[c]
        y[b, half * 2048:(half + 1) * 2048] = r["y_o"]
        ys[16 * c:16 * (c + 1)] = r["ys_o"].reshape(16, 4, D)
        if half == 1:
            kp[0, b] = r["kp_o"].reshape(128, 2, 64)
            vp[0, b] = r["vp_o"].reshape(128, 2, 64)
            rp[0, b] = r["rp_o"]
            cpo[0, b] = r["cp_o"]
        kso[0, 16 * c:16 * (c + 1)] = r["ks_o"].reshape(16, 128, 2, 64)
        vso[0, 16 * c:16 * (c + 1)] = r["vs_o"].reshape(16, 128, 2, 64)
        rso[0, 16 * c:16 * (c + 1)] = r["rs_o"]
        cso[0, 16 * c:16 * (c + 1)] = r["cs_o"].reshape(16, 2, F2)
    return (y, ys, kp, vp, rp, cpo, kso, vso, rso, cso)
```
